# Optimizing a Trainium2 kernel written in Bass

```python
import math
import jax, jax.numpy as jnp
from jax import lax
import numpy as np

D_MODEL = 4096
BATCH = 8
SEQ = 2048
DEPTH = 2
DEC_BATCH = 4
DEC_SEQ = 4096
PAST_LEN = 128

MIX_WIDTH = D_MODEL
GROUP_WIDTH = MIX_WIDTH // 4
HEAD_DIM = 128
DA_HEADS = GROUP_WIDTH // HEAD_DIM
DA_PATTERNS = ((128, 1), (512, 4), (2048, 16))
DA_QBLK = 64
ROPE_THETA = 500000.0
ROPE_DIMS = HEAD_DIM // 4
GQA_HEADS = GROUP_WIDTH // HEAD_DIM
GQA_KV_HEADS = GQA_HEADS // 4
GQA_QBLK = 128
AXIAL_THETA = 10000.0
GRID_W = 64
HY_CH = GROUP_WIDTH
HY_ORDER = 2
HY_SHORT = 3
HY_EMB = 33
HY_BANDS = (HY_EMB - 1) // 2
HY_FILTER_W = 64
HY_MIN_DECAY = 3.07
HY_MAX_DECAY = 15.35
ML_HEADS = 4
ML_HEAD_DIM = GROUP_WIDTH // ML_HEADS
ML_CHUNK = 64
FF_DIM = -(-8 * D_MODEL // (3 * 256)) * 256
EPS = 1e-6
IN_SPLIT_SIZES = (GROUP_WIDTH, GROUP_WIDTH, GROUP_WIDTH,
                  GQA_HEADS * HEAD_DIM, GQA_KV_HEADS * HEAD_DIM, GQA_KV_HEADS * HEAD_DIM,
                  (HY_ORDER + 1) * HY_CH,
                  GROUP_WIDTH, GROUP_WIDTH, GROUP_WIDTH, GROUP_WIDTH, 4 * ML_HEADS)
N_IN = sum(IN_SPLIT_SIZES)

kernel_name = "hybrid_bidir_dilated_gqa_hyena_mlstm_encoder"

F32 = jnp.float32


def rmsnorm(x, g):
    xf = x.astype(F32)
    y = xf * lax.rsqrt(jnp.mean(xf * xf, axis=-1, keepdims=True) + EPS)
    return (y * g.astype(F32)).astype(x.dtype)


def rope_tables(pos, dims, theta):
    inv = jnp.float32(theta) ** (-jnp.arange(0, dims, 2, dtype=F32) / dims)
    ang = pos[:, None] * inv[None, :]
    return jnp.cos(ang), jnp.sin(ang)


def apply_rope(x, cos, sin):
    xf = x.astype(F32)
    x1, x2 = jnp.split(xf, 2, axis=-1)
    c = cos[:, None, :]
    s = sin[:, None, :]
    return jnp.concatenate([x1 * c - x2 * s, x2 * c + x1 * s], axis=-1).astype(x.dtype)


def partial_rope(x, cos, sin):
    return jnp.concatenate([apply_rope(x[..., :ROPE_DIMS], cos, sin), x[..., ROPE_DIMS:]], axis=-1)


def axial_rope(x, cos_r, sin_r, cos_c, sin_c):
    half = HEAD_DIM // 2
    return jnp.concatenate([apply_rope(x[..., :half], cos_r, sin_r),
                            apply_rope(x[..., half:], cos_c, sin_c)], axis=-1)


def dilated_window_attention(q, k, v):
    B, L, H, Dh = q.shape
    nblk = L // DA_QBLK
    scale = Dh ** -0.5
    q_blocks = jnp.moveaxis(q.reshape(B, nblk, DA_QBLK, H, Dh), 1, 0)

    def one_block(args):
        qb, bi = args
        tq = bi * DA_QBLK + jnp.arange(DA_QBLK)
        ms, dens, nums = [], [], []
        for window, dil in DA_PATTERNS:
            half = window // dil // 2
            offs = jnp.arange(-half, half + 1) * dil
            idx = tq[:, None] + offs[None, :]
            valid = (idx >= 0) & (idx < L)
            idx = jnp.clip(idx, 0, L - 1)
            kg = k[:, idx]
            vg = v[:, idx]
            s = jnp.einsum('bqhd,bqkhd->bhqk', qb, kg, preferred_element_type=F32) * scale
            s = jnp.where(valid[None, None], s, -jnp.inf)
            m = jnp.max(s, axis=-1)
            p = jnp.exp(s - m[..., None])
            ms.append(m)
            dens.append(jnp.sum(p, axis=-1))
            nums.append(jnp.einsum('bhqk,bqkhd->bhqd', p.astype(v.dtype), vg,
                                   preferred_element_type=F32))
        m_all = jnp.stack(ms)
        w = jnp.exp(m_all - jnp.max(m_all, axis=0))
        den = jnp.sum(w * jnp.stack(dens), axis=0)
        num = jnp.sum(w[..., None] * jnp.stack(nums), axis=0)
        return (num / den[..., None]).astype(q.dtype)

    out = lax.map(one_block, (q_blocks, jnp.arange(nblk)))
    return out.transpose(1, 0, 3, 2, 4).reshape(B, L, H * Dh)


def gqa_attention(q, k, v):
    B, L, Hq, Dh = q.shape
    Hkv = k.shape[2]
    G = Hq // Hkv
    nblk = L // GQA_QBLK
    scale = Dh ** -0.5
    q_blocks = jnp.moveaxis(q.reshape(B, nblk, GQA_QBLK, Hkv, G, Dh), 1, 0)

    def one_block(qb):
        s = jnp.einsum('bqhgd,bkhd->bhgqk', qb, k, preferred_element_type=F32) * scale
        p = jax.nn.softmax(s, axis=-1)
        return jnp.einsum('bhgqk,bkhd->bqhgd', p.astype(v.dtype), v).astype(q.dtype)

    out = lax.map(one_block, q_blocks)
    return jnp.moveaxis(out, 0, 1).reshape(B, L, Hq * Dh)


def short_conv(u, w, b):
    pad = HY_SHORT // 2
    L = u.shape[1]
    up = jnp.pad(u, ((0, 0), (pad, pad), (0, 0)))
    y = b
    for j in range(HY_SHORT):
        y = y + up[:, j:j + L] * w[j]
    return y


def hyena_filters(L, w1, b1, fr1, w2, b2, fr2, w3, decay):
    n = jnp.arange(L, dtype=F32)
    t = n / (L - 1)
    f = jnp.linspace(1e-4, HY_BANDS - 1, HY_BANDS, dtype=F32)
    ang = (2.0 * math.pi / L) * n[:, None] * f[None, :]
    z = jnp.concatenate([t[:, None], jnp.cos(ang), -jnp.sin(ang)], axis=-1)
    h = jnp.sin(fr1.astype(F32) * (z @ w1.astype(F32) + b1.astype(F32)))
    h = jnp.sin(fr2.astype(F32) * (h @ w2.astype(F32) + b2.astype(F32)))
    h = h @ w3.astype(F32)
    r = jnp.abs(n - L // 2) / (L // 2)
    h = h * jnp.exp(-r[:, None] * decay.astype(F32)[None, :])
    return h.reshape(L, HY_ORDER, HY_CH)


def centred_fftconv(u, h):
    L = u.shape[1]
    U = jnp.fft.rfft(u, n=2 * L, axis=1)
    Hf = jnp.fft.rfft(h, n=2 * L, axis=0)
    y = jnp.fft.irfft(U * Hf[None], n=2 * L, axis=1)
    return y[:, L // 2:L // 2 + L]


def hyena_mixer(u, conv_w, conv_b, w1, b1, fr1, w2, b2, fr2, w3, decay, skip):
    L = u.shape[1]
    uc = short_conv(u, conv_w, conv_b).astype(F32)
    v, x1, x2 = jnp.split(uc, 3, axis=-1)
    h = hyena_filters(L, w1, b1, fr1, w2, b2, fr2, w3, decay)
    sk = skip.astype(F32).reshape(HY_ORDER, HY_CH)
    z = v
    for o, gate in enumerate((x1, x2)):
        z = gate * (centred_fftconv(z, h[:, o]) + sk[o] * z)
    return z.astype(u.dtype)


def mlstm_chunkwise(q, k, v, log_i, log_f):
    B, H, L, d = q.shape
    nc = L // ML_CHUNK

    def chunks(a):
        return jnp.moveaxis(a.reshape((B, H, nc, ML_CHUNK) + a.shape[3:]), 2, 0)

    causal = jnp.tril(jnp.ones((ML_CHUNK, ML_CHUNK), dtype=bool))

    def step(carry, xs):
        C, n, m = carry
        qc, kc, vc, li, lf = xs
        b = jnp.cumsum(lf, axis=-1)
        Dm = b[..., :, None] - b[..., None, :] + li[..., None, :]
        Dm = jnp.where(causal, Dm, -jnp.inf)
        inter = b + m[..., None]
        m_t = jnp.maximum(inter, jnp.max(Dm, axis=-1))
        Wm = jnp.exp(Dm - m_t[..., None])
        a_inter = jnp.exp(inter - m_t)
        S = jnp.einsum('bhtd,bhsd->bhts', qc, kc) * Wm
        num = a_inter[..., None] * jnp.einsum('bhtd,bhde->bhte', qc, C) + jnp.einsum('bhts,bhse->bhte', S, vc)
        den = a_inter * jnp.einsum('bhtd,bhd->bht', qc, n) + jnp.sum(S, axis=-1)
        h = num / jnp.maximum(jnp.abs(den), jnp.exp(-m_t))[..., None]
        bL = b[..., -1]
        g = bL[..., None] - b + li
        m_new = jnp.maximum(bL + m, jnp.max(g, axis=-1))
        a_old = jnp.exp(bL + m - m_new)
        wk = jnp.exp(g - m_new[..., None])
        C_new = a_old[..., None, None] * C + jnp.einsum('bhsd,bhse->bhde', kc * wk[..., None], vc)
        n_new = a_old[..., None] * n + jnp.einsum('bhs,bhsd->bhd', wk, kc)
        return (C_new, n_new, m_new), h

    init = (jnp.zeros((B, H, d, d), F32), jnp.zeros((B, H, d), F32), jnp.zeros((B, H), F32))
    _, hs = lax.scan(step, init, (chunks(q), chunks(k), chunks(v), chunks(log_i), chunks(log_f)))
    return jnp.moveaxis(hs, 0, 2).reshape(B, H, L, d)


def mlstm_mixer(q, k, v, o, gates, gate_b, norm_g):
    B, L, _ = q.shape

    def heads(a):
        return a.reshape(B, L, ML_HEADS, ML_HEAD_DIM).transpose(0, 2, 1, 3).astype(F32)

    qh = heads(q)
    kh = heads(k) * (ML_HEAD_DIM ** -0.5)
    vh = heads(v)
    g = (gates.astype(F32).reshape(B, L, 4, ML_HEADS) + gate_b.astype(F32)).transpose(2, 0, 3, 1)
    h_fwd = mlstm_chunkwise(qh, kh, vh, g[0], jax.nn.log_sigmoid(g[1]))
    fl = lambda a: jnp.flip(a, axis=2)
    h_bwd = fl(mlstm_chunkwise(fl(qh), fl(kh), fl(vh), fl(g[2]), fl(jax.nn.log_sigmoid(g[3]))))
    hs = h_fwd + h_bwd
    hs = hs * lax.rsqrt(jnp.mean(hs * hs, axis=-1, keepdims=True) + EPS)
    hs = hs.transpose(0, 2, 1, 3).reshape(B, L, GROUP_WIDTH) * norm_g.astype(F32)
    return (hs * jax.nn.sigmoid(o.astype(F32))).astype(q.dtype)


def trunk(x, norm1_g, w_in, qk_norm_g, hy_conv_w, hy_conv_b, hy_w1, hy_b1, hy_freq1,
          hy_w2, hy_b2, hy_freq2, hy_w3, hy_decay, hy_skip, ml_gate_b, ml_norm_g,
          w_out, norm2_g, w_gate, w_up, w_down, final_g):
    B, L, _ = x.shape
    rows = L // GRID_W
    cos_t, sin_t = rope_tables(jnp.arange(L, dtype=F32), ROPE_DIMS, ROPE_THETA)
    row_pos = jnp.repeat(jnp.arange(rows, dtype=F32), GRID_W)
    col_pos = jnp.tile(jnp.arange(GRID_W, dtype=F32), rows)
    cos_r, sin_r = rope_tables(row_pos, HEAD_DIM // 2, AXIAL_THETA)
    cos_c, sin_c = rope_tables(col_pos, HEAD_DIM // 2, AXIAL_THETA)
    bounds = np.cumsum(IN_SPLIT_SIZES)[:-1].tolist()
    for l in range(DEPTH):
        h = rmsnorm(x, norm1_g[l])
        proj = h @ w_in[l]
        (a_q, a_k, a_v, b_q, b_k, b_v, c_u, d_q, d_k, d_v, d_o, d_g) = jnp.split(proj, bounds, axis=-1)
        a_q = partial_rope(a_q.reshape(B, L, DA_HEADS, HEAD_DIM), cos_t, sin_t)
        a_k = partial_rope(a_k.reshape(B, L, DA_HEADS, HEAD_DIM), cos_t, sin_t)
        y_a = dilated_window_attention(a_q, a_k, a_v.reshape(B, L, DA_HEADS, HEAD_DIM))
        b_q = axial_rope(rmsnorm(b_q.reshape(B, L, GQA_HEADS, HEAD_DIM), qk_norm_g[l, 0]), cos_r, sin_r, cos_c, sin_c)
        b_k = axial_rope(rmsnorm(b_k.reshape(B, L, GQA_KV_HEADS, HEAD_DIM), qk_norm_g[l, 1]), cos_r, sin_r, cos_c, sin_c)
        y_b = gqa_attention(b_q, b_k, b_v.reshape(B, L, GQA_KV_HEADS, HEAD_DIM))
        y_c = hyena_mixer(c_u, hy_conv_w[l], hy_conv_b[l], hy_w1[l], hy_b1[l], hy_freq1[l],
                          hy_w2[l], hy_b2[l], hy_freq2[l], hy_w3[l], hy_decay[l], hy_skip[l])
        y_d = mlstm_mixer(d_q, d_k, d_v, d_o, d_g, ml_gate_b[l], ml_norm_g[l])
        x = x + jnp.concatenate([y_a, y_b, y_c, y_d], axis=-1) @ w_out[l]
        h = rmsnorm(x, norm2_g[l])
        x = x + (jax.nn.silu(h @ w_gate[l]) * (h @ w_up[l])) @ w_down[l]
    return rmsnorm(x, final_g)


def setup_inputs(seed: int = 0) -> dict:
    key = jax.random.key(seed)
    ks = jax.random.split(key, 32)

    def nrm(k, shape, std):
        return std * jax.random.normal(k, shape, F32)

    x_prompt = jax.random.normal(ks[0], (BATCH, SEQ, D_MODEL), F32)
    x_sample = jax.random.normal(ks[1], (DEC_BATCH, DEC_SEQ, D_MODEL), F32)
    norm1_g = 1.0 + nrm(ks[2], (DEPTH, D_MODEL), 0.02)
    w_in = nrm(ks[3], (DEPTH, D_MODEL, N_IN), D_MODEL ** -0.5)
    qk_norm_g = 1.0 + nrm(ks[4], (DEPTH, 2, HEAD_DIM), 0.02)
    hy_conv_w = nrm(ks[5], (DEPTH, HY_SHORT, (HY_ORDER + 1) * HY_CH), HY_SHORT ** -0.5)
    hy_conv_b = nrm(ks[6], (DEPTH, (HY_ORDER + 1) * HY_CH), 0.02)
    hy_w1 = nrm(ks[7], (DEPTH, HY_EMB, HY_FILTER_W), HY_EMB ** -0.5)
    hy_b1 = nrm(ks[8], (DEPTH, HY_FILTER_W), 0.02)
    hy_freq1 = 1.0 + nrm(ks[9], (DEPTH, HY_FILTER_W), 0.1)
    hy_w2 = nrm(ks[10], (DEPTH, HY_FILTER_W, HY_FILTER_W), HY_FILTER_W ** -0.5)
    hy_b2 = nrm(ks[11], (DEPTH, HY_FILTER_W), 0.02)
    hy_freq2 = 1.0 + nrm(ks[12], (DEPTH, HY_FILTER_W), 0.1)
    hy_w3 = nrm(ks[13], (DEPTH, HY_FILTER_W, HY_ORDER * HY_CH), 0.01)
    hy_decay = jax.random.uniform(ks[14], (DEPTH, HY_ORDER * HY_CH), F32, HY_MIN_DECAY, HY_MAX_DECAY)
    hy_skip = nrm(ks[15], (DEPTH, HY_ORDER * HY_CH), 0.1)
    i_bias = nrm(ks[16], (DEPTH, 2, ML_HEADS), 0.1)
    f_bias = jnp.linspace(3.0, 6.0, ML_HEADS, dtype=F32) + nrm(ks[17], (DEPTH, 2, ML_HEADS), 0.1)
    ml_gate_b = jnp.stack([i_bias[:, 0], f_bias[:, 0], i_bias[:, 1], f_bias[:, 1]], axis=1)
    ml_norm_g = 1.0 + nrm(ks[18], (DEPTH, GROUP_WIDTH), 0.02)
    w_out = nrm(ks[19], (DEPTH, MIX_WIDTH, D_MODEL), MIX_WIDTH ** -0.5)
    norm2_g = 1.0 + nrm(ks[20], (DEPTH, D_MODEL), 0.02)
    w_gate = nrm(ks[21], (DEPTH, D_MODEL, FF_DIM), D_MODEL ** -0.5)
    w_up = nrm(ks[22], (DEPTH, D_MODEL, FF_DIM), D_MODEL ** -0.5)
    w_down = nrm(ks[23], (DEPTH, FF_DIM, D_MODEL), FF_DIM ** -0.5)
    final_g = 1.0 + nrm(ks[24], (D_MODEL,), 0.02)
    return {"x_prompt": x_prompt, "x_sample": x_sample, "norm1_g": norm1_g, "w_in": w_in,
            "qk_norm_g": qk_norm_g, "hy_conv_w": hy_conv_w, "hy_conv_b": hy_conv_b,
            "hy_w1": hy_w1, "hy_b1": hy_b1, "hy_freq1": hy_freq1, "hy_w2": hy_w2,
            "hy_b2": hy_b2, "hy_freq2": hy_freq2, "hy_w3": hy_w3, "hy_decay": hy_decay,
            "hy_skip": hy_skip, "ml_gate_b": ml_gate_b, "ml_norm_g": ml_norm_g,
            "w_out": w_out, "norm2_g": norm2_g, "w_gate": w_gate, "w_up": w_up,
            "w_down": w_down, "final_g": final_g}


def reference(x_prompt, x_sample, norm1_g, w_in, qk_norm_g, hy_conv_w, hy_conv_b, hy_w1,
              hy_b1, hy_freq1, hy_w2, hy_b2, hy_freq2, hy_w3, hy_decay, hy_skip, ml_gate_b,
              ml_norm_g, w_out, norm2_g, w_gate, w_up, w_down, final_g):
    weights = (norm1_g, w_in, qk_norm_g, hy_conv_w, hy_conv_b, hy_w1, hy_b1, hy_freq1,
               hy_w2, hy_b2, hy_freq2, hy_w3, hy_decay, hy_skip, ml_gate_b, ml_norm_g,
               w_out, norm2_g, w_gate, w_up, w_down, final_g)
    y_prompt = trunk(x_prompt, *weights)
    y_sample = trunk(x_sample, *weights)
    return (y_prompt, y_sample)
```

```python
import contextlib
import math
import numpy as np
import ml_dtypes
import concourse.bass as bass
import concourse.mybir as mybir
from concourse.bass_utils import run_bass_kernel_spmd

F32 = mybir.dt.float32
BF16 = mybir.dt.bfloat16
AF = mybir.ActivationFunctionType
ALU = mybir.AluOpType
AX = mybir.AxisListType
EPS = 1e-6


class Buf:
    __slots__ = ("name", "w", "r")

    def __init__(self, name=""):
        self.name = name
        self.w = None
        self.r = {}


class Sched:
    COMPUTE = ("pe", "act", "dve", "pool")
    NDMA = {"sp": 10, "pool": 6, "act": 4}

    def __init__(self, nc):
        self.nc = nc
        self.sem = {}
        self.val = {}
        for e in self.COMPUTE:
            self.sem[e] = nc.alloc_semaphore(name="c_" + e)
            self.val[e] = 0
        self.dcnt = {}
        for q, n in self.NDMA.items():
            for i in range(n):
                k = (q, i)
                self.sem[k] = nc.alloc_semaphore(name="d_%s%d" % (q, i))
                self.val[k] = 0
            self.dcnt[q] = 0
        self.streams = {e: [] for e in ("pe", "act", "dve", "pool", "sp")}
        self.known = {e: {} for e in self.streams}
        self.nops = 0

    def _waits(self, eng, reads, writes, is_dma):
        deps = {}
        for b in reads:
            if b.w is not None:
                k, v = b.w
                if deps.get(k, 0) < v:
                    deps[k] = v
        for b in writes:
            if b.w is not None:
                k, v = b.w
                if deps.get(k, 0) < v:
                    deps[k] = v
            for k, v in b.r.items():
                if deps.get(k, 0) < v:
                    deps[k] = v
        out = []
        kn = self.known[eng]
        for k, v in deps.items():
            if k == eng and not is_dma and eng == "pe":
                continue
            if kn.get(k, 0) >= v:
                continue
            kn[k] = v
            out.append((self.sem[k], v))
        return out

    @staticmethod
    def _mark(key, v, reads, writes):
        for b in reads:
            if b.r.get(key, 0) < v:
                b.r[key] = v
        for b in writes:
            b.w = (key, v)
            b.r = {}

    def op(self, eng, fn, reads=(), writes=()):
        waits = self._waits(eng, reads, writes, False)
        self.val[eng] += 1
        v = self.val[eng]
        self._mark(eng, v, reads, writes)
        self.streams[eng].append((waits, fn, self.sem[eng], 1))
        self.nops += 1

    def dma(self, q, fn, reads=(), writes=()):
        i = self.dcnt[q] % self.NDMA[q]
        self.dcnt[q] += 1
        key = (q, i)
        waits = self._waits(q, reads, writes, True)
        pv = self.val[key]
        if pv > 0 and self.known[q].get(key, 0) < pv:
            self.known[q][key] = pv
            waits.append((self.sem[key], pv))
        self.val[key] += 16
        v = self.val[key]
        self._mark(key, v, reads, writes)
        self.streams[q].append((waits, fn, self.sem[key], 16))
        self.nops += 1

    def run_phase(self):
        nc = self.nc
        for q in self.NDMA:
            fin = []
            for i in range(self.NDMA[q]):
                key = (q, i)
                if self.val[key] > self.known[q].get(key, 0):
                    fin.append((self.sem[key], self.val[key]))
            if fin:
                self.streams[q].append((fin, None, None, 0))
        streams = self.streams

        def replay(e, lst):
            for waits, fn, sem, inc in lst:
                for s, v in waits:
                    e.wait_ge(s, v)
                if fn is not None:
                    ins = fn(e)
                    ins.then_inc(sem, inc)

        with nc.Block() as block:
            if streams["pe"]:
                @block.tensor
                def _(e):
                    replay(e, streams["pe"])
            if streams["act"]:
                @block.scalar
                def _(e):
                    replay(e, streams["act"])
            if streams["dve"]:
                @block.vector
                def _(e):
                    replay(e, streams["dve"])
            if streams["pool"]:
                @block.gpsimd
                def _(e):
                    replay(e, streams["pool"])
            if streams["sp"]:
                @block.sync
                def _(e):
                    replay(e, streams["sp"])
        self.streams = {e: [] for e in ("pe", "act", "dve", "pool", "sp")}
        allv = dict(self.val)
        self.known = {e: dict(allv) for e in self.streams}


class Ctx:
    def __init__(self, nc):
        self.nc = nc
        self.S = Sched(nc)
        self.ps = nc.alloc_psum_tensor("psum_all", [128, 8, 512], F32)
        self.psb = [Buf("ps%d" % i) for i in range(8)]
        self._evac = 0
        self._uid = 0
        self.ident = nc.alloc_sbuf_tensor("sb_ident", [128, 128], F32)
        self.identb = Buf("ident")
        self.identh = nc.alloc_sbuf_tensor("sb_identh", [128, 128], BF16)
        self.identhb = Buf("identh")

    def load_consts(self, ident_ap):
        S = self.S
        S.dma("sp", lambda e: e.dma_start(out=self.ident[:], in_=ident_ap), writes=(self.identb,))
        S.op("dve", lambda e: e.tensor_copy(out=self.identh[:], in_=self.ident[:]),
             reads=(self.identb,), writes=(self.identhb,))

    def sbuf_tensor(self, name, shape, dtype):
        self._uid += 1
        return self.nc.sbuf_tensor("%s_u%d" % (name, self._uid), shape, dtype)

    def evac_eng(self):
        self._evac += 1
        return "act" if (self._evac % 2) else "dve"


def copy_op(S, eng, out, in_, reads, writes, scale=None):
    if eng == "act":
        if scale is None:
            S.op("act", lambda e: e.activation(out=out, in_=in_, func=AF.Copy), reads, writes)
        else:
            S.op("act", lambda e: e.activation(out=out, in_=in_, func=AF.Copy, scale=float(scale)), reads, writes)
    else:
        if scale is None:
            S.op(eng, lambda e: e.tensor_copy(out=out, in_=in_), reads, writes)
        else:
            S.op(eng, lambda e: e.tensor_scalar(out=out, in0=in_, scalar1=float(scale), scalar2=None,
                                                op0=ALU.mult), reads, writes)


PK = 16


def gemm_phase(C, *, TOK, K, blocks, a_loader, nbufA=1, nslot=3):
    nc, S = C.nc, C.S
    KC = K // 128
    npiece = (KC + PK - 1) // PK
    NTB = TOK // 512
    with contextlib.ExitStack() as es:
        At = es.enter_context(C.sbuf_tensor("gA", [128, nbufA, KC, 512], BF16))
        Wt = es.enter_context(C.sbuf_tensor("gW", [128, nslot, PK, 512], BF16))
        abufs = [[Buf("A%d_%d" % (i, p)) for p in range(npiece)] for i in range(nbufA)]
        wbufs = [Buf("W%d" % i) for i in range(nslot)]
        wcnt = 0
        bcnt = 0
        for tb in range(NTB):
            ai = tb % nbufA
            a_loader(C, tb, At[:, ai], abufs[ai])
            for blk in blocks:
                width = sum(c[2] for c in blk["cols"])
                orient = blk["orient"]
                nsub = 4 if orient == "T" else (width + 127) // 128
                pset = (bcnt % 2) * 4
                bcnt += 1
                for p in range(npiece):
                    k0 = p * PK
                    k1 = min(KC, k0 + PK)
                    slot = wcnt % nslot
                    wcnt += 1
                    off = 0
                    for (wap, c0, wd) in blk["cols"]:
                        src = wap[k0 * 128:k1 * 128, c0:c0 + wd].rearrange("(c p) n -> p c n", p=128)
                        dst = Wt[:, slot, 0:k1 - k0, off:off + wd]
                        S.dma("sp", (lambda e, dst=dst, src=src: e.dma_start(out=dst, in_=src)),
                              reads=(), writes=(wbufs[slot],))
                        off += wd

                    def mm(e, k0=k0, k1=k1, slot=slot, ai=ai, orient=orient, nsub=nsub, pset=pset, width=width):
                        ins = None
                        for kc in range(k0, k1):
                            for j in range(nsub):
                                if orient == "T":
                                    ins = e.matmul(C.ps[:, pset + j, 0:width],
                                                   lhsT=At[:, ai, kc, j * 128:(j + 1) * 128],
                                                   rhs=Wt[:, slot, kc - k0, 0:width],
                                                   start=(kc == 0), stop=(kc == KC - 1))
                                else:
                                    cw = min(128, width - j * 128)
                                    ins = e.matmul(C.ps[0:cw, pset + j, 0:512],
                                                   lhsT=Wt[:, slot, kc - k0, j * 128:j * 128 + cw],
                                                   rhs=At[:, ai, kc, 0:512],
                                                   start=(kc == 0), stop=(kc == KC - 1))
                        return ins
                    S.op("pe", mm, reads=(wbufs[slot], abufs[ai][p]),
                         writes=tuple(C.psb[pset + j] for j in range(nsub)))
                for j in range(nsub):
                    blk["epi"](C, tb, j, pset + j, width)
        S.run_phase()


def rmsnorm_loader(C, es, x_ap, g_ap, D, name):
    nc, S = C.nc, C.S
    KC = D // 128
    xt = es.enter_context(C.sbuf_tensor(name + "_x", [128, 2, D], F32))
    xn = es.enter_context(C.sbuf_tensor(name + "_xn", [128, 1, D], F32))
    gB = es.enter_context(C.sbuf_tensor(name + "_g", [128, D], F32))
    st = es.enter_context(C.sbuf_tensor(name + "_s", [128, 2, 4], F32))
    xb = [Buf(), Buf()]
    xnb = [Buf()]
    sb = [Buf(), Buf()]
    gb = Buf()
    S.dma("sp", lambda e: e.dma_start(out=gB[:], in_=g_ap.partition_broadcast(128)), writes=(gb,))
    cnt = [0]

    def loader(C, tb, At, abufs):
        for t in range(4):
            i = cnt[0] % 2
            cnt[0] += 1
            r0 = tb * 512 + t * 128
            S.dma("sp", (lambda e, i=i, r0=r0: e.dma_start(out=xt[:, i, :], in_=x_ap[r0:r0 + 128, :])),
                  writes=(xb[i],))
            S.op("act", (lambda e, i=i: e.activation(out=xn[:, 0, :], in_=xt[:, i, :], func=AF.Square,
                                                      accum_out=st[:, i, 0:1])),
                 reads=(xb[i],), writes=(xnb[0], sb[i]))
            S.op("dve", (lambda e, i=i: e.tensor_scalar(out=st[:, i, 1:2], in0=st[:, i, 0:1], scalar1=1.0 / D,
                                                         scalar2=EPS, op0=ALU.mult, op1=ALU.add)),
                 reads=(sb[i],), writes=(sb[i],))
            S.op("act", (lambda e, i=i: e.activation(out=st[:, i, 2:3], in_=st[:, i, 1:2], func=AF.Sqrt)),
                 reads=(sb[i],), writes=(sb[i],))
            S.op("dve", (lambda e, i=i: e.reciprocal(out=st[:, i, 3:4], in_=st[:, i, 2:3])),
                 reads=(sb[i],), writes=(sb[i],))
            S.op("dve", (lambda e, i=i: e.scalar_tensor_tensor(out=xn[:, 0, :], in0=xt[:, i, :],
                                                                scalar=st[:, i, 3:4], in1=gB[:],
                                                                op0=ALU.mult, op1=ALU.mult)),
                 reads=(xb[i], sb[i], gb), writes=(xnb[0],))
            for c4 in range(KC // 4):
                bank = c4 % 8

                def tr(e, i=i, c4=c4, bank=bank):
                    ins = None
                    for q in range(4):
                        c = c4 * 4 + q
                        ins = e.transpose(out=C.ps[:, bank, q * 128:(q + 1) * 128],
                                          in_=xn[:, 0, c * 128:(c + 1) * 128], identity=C.ident[:])
                    return ins
                S.op("pe", tr, reads=(xnb[0], C.identb), writes=(C.psb[bank],))
                p = (c4 * 4) // PK
                eng = C.evac_eng()
                dst = At[:, c4 * 4:c4 * 4 + 4, t * 128:(t + 1) * 128]
                src = C.ps[:, bank, :].rearrange("p (c t) -> p c t", c=4)
                copy_op(S, eng, dst, src, reads=(C.psb[bank],), writes=(abufs[p],))
    return loader


def dram_loader(C, aT_ap, K):
    S = C.S
    KC = K // 128
    npiece = (KC + PK - 1) // PK

    def loader(C, tb, At, abufs):
        for p in range(npiece):
            k0 = p * PK
            k1 = min(KC, k0 + PK)
            src = aT_ap[k0 * 128:k1 * 128, tb * 512:(tb + 1) * 512].rearrange("(c p) t -> p c t", p=128)
            S.dma("sp", (lambda e, src=src, k0=k0, k1=k1: e.dma_start(out=At[:, k0:k1, :], in_=src)),
                  writes=(abufs[p],))
    return loader


D_MODEL = 4096
TOK = 4096
N_IN = 11792
FF = 11008
GW = 1024
O_AQ, O_AK, O_AV, O_BQ, O_BK, O_BV, O_CU, O_DQ, O_DK, O_DV, O_DO, O_DG = (
    0, 1024, 2048, 3072, 4096, 4352, 4608, 7680, 8704, 9728, 10752, 11776)


def cast_weights(C, pairs):
    S = C.S
    b = Buf()
    for src, dst in pairs:
        K, N = src.shape
        sv = src.rearrange("(c p) n -> p c n", p=128)
        dv = dst.rearrange("(c p) n -> p c n", p=128)
        for c in range(0, K // 128, 4):
            c1 = min(K // 128, c + 4)
            for n0 in range(0, N, 2048):
                n1 = min(N, n0 + 2048)
                S.dma("pool", (lambda e, sv=sv, dv=dv, c=c, c1=c1, n0=n0, n1=n1:
                               e.dma_start(out=dv[:, c:c1, n0:n1], in_=sv[:, c:c1, n0:n1])), writes=(b,))
    S.run_phase()


def store_epi(C, es, dst_ap, col0_of_block, dtype=F32, name="ep"):
    nc, S = C.nc, C.S
    ot = es.enter_context(C.sbuf_tensor(name + "_o", [128, 4, 512], dtype))
    ob = [Buf() for _ in range(4)]
    cnt = [0]

    def mk(col0):
        def epi(C, tb, j, bank, width):
            i = cnt[0] % 4
            cnt[0] += 1
            copy_op(S, C.evac_eng(), ot[:, i, 0:width], C.ps[:, bank, 0:width],
                    reads=(C.psb[bank],), writes=(ob[i],))
            r0 = tb * 512 + j * 128
            S.dma("pool", (lambda e, i=i, r0=r0: e.dma_start(out=dst_ap[r0:r0 + 128, col0:col0 + width],
                                                             in_=ot[:, i, 0:width])), reads=(ob[i],))
        return epi
    return mk


def resid_epi(C, es, res_ap, dst_ap, name="re"):
    nc, S = C.nc, C.S
    rt = es.enter_context(C.sbuf_tensor(name + "_r", [128, 4, 512], F32))
    rb = [Buf() for _ in range(4)]
    cnt = [0]

    def mk(col0):
        def epi(C, tb, j, bank, width):
            i = cnt[0] % 4
            cnt[0] += 1
            r0 = tb * 512 + j * 128
            S.dma("sp", (lambda e, i=i, r0=r0: e.dma_start(out=rt[:, i, 0:width],
                                                           in_=res_ap[r0:r0 + 128, col0:col0 + width])),
                  writes=(rb[i],))
            S.op("dve", (lambda e, i=i, bank=bank: e.tensor_tensor(out=rt[:, i, 0:width], in0=C.ps[:, bank, 0:width],
                                                                     in1=rt[:, i, 0:width], op=ALU.add)),
                 reads=(C.psb[bank], rb[i]), writes=(rb[i],))
            S.dma("pool", (lambda e, i=i, r0=r0: e.dma_start(out=dst_ap[r0:r0 + 128, col0:col0 + width],
                                                             in_=rt[:, i, 0:width])), reads=(rb[i],))
        return epi
    return mk


def swiglu_epi(C, es, aT_ap, name="sg"):
    nc, S = C.nc, C.S
    gt = es.enter_context(C.sbuf_tensor(name + "_g", [128, 4, 512], F32))
    at = es.enter_context(C.sbuf_tensor(name + "_a", [128, 4, 512], BF16))
    gb = [Buf() for _ in range(4)]
    ab = [Buf() for _ in range(4)]
    cnt = [0]

    def mk(c0):
        def epi(C, tb, j, bank, width):
            if j < 2:
                i = (cnt[0] * 2 + j) % 4
                S.op("act", (lambda e, i=i, bank=bank: e.activation(out=gt[:, i, :], in_=C.ps[:, bank, :],
                                                                      func=AF.Silu)),
                     reads=(C.psb[bank],), writes=(gb[i],))
            else:
                i = (cnt[0] * 2 + j - 2) % 4
                S.op("dve", (lambda e, i=i, bank=bank: e.tensor_tensor(out=at[:, i, :], in0=C.ps[:, bank, :],
                                                                         in1=gt[:, i, :], op=ALU.mult)),
                     reads=(C.psb[bank], gb[i]), writes=(ab[i],))
                r0 = c0 + (j - 2) * 128
                S.dma("pool", (lambda e, i=i, r0=r0, tb=tb: e.dma_start(
                    out=aT_ap[r0:r0 + 128, tb * 512:(tb + 1) * 512], in_=at[:, i, :])), reads=(ab[i],))
                if j == 3:
                    cnt[0] += 1
        return epi
    return mk


def final_norm(C, x_ap, g_ap, y_ap, D):
    nc, S = C.nc, C.S
    with contextlib.ExitStack() as es:
        xt = es.enter_context(C.sbuf_tensor("fn_x", [128, 2, D], F32))
        junk = es.enter_context(C.sbuf_tensor("fn_j", [128, D], BF16))
        gB = es.enter_context(C.sbuf_tensor("fn_g", [128, D], F32))
        st = es.enter_context(C.sbuf_tensor("fn_s", [128, 2, 4], F32))
        xb = [Buf(), Buf()]
        sb = [Buf(), Buf()]
        jb, gb = Buf(), Buf()
        S.dma("sp", lambda e: e.dma_start(out=gB[:], in_=g_ap.partition_broadcast(128)), writes=(gb,))
        for t in range(TOK // 128):
            i = t % 2
            r0 = t * 128
            S.dma("sp", (lambda e, i=i, r0=r0: e.dma_start(out=xt[:, i, :], in_=x_ap[r0:r0 + 128, :])),
                  writes=(xb[i],))
            S.op("act", (lambda e, i=i: e.activation(out=junk[:], in_=xt[:, i, :], func=AF.Square,
                                                      accum_out=st[:, i, 0:1])),
                 reads=(xb[i],), writes=(jb, sb[i]))
            S.op("dve", (lambda e, i=i: e.tensor_scalar(out=st[:, i, 1:2], in0=st[:, i, 0:1], scalar1=1.0 / D,
                                                         scalar2=EPS, op0=ALU.mult, op1=ALU.add)),
                 reads=(sb[i],), writes=(sb[i],))
            S.op("act", (lambda e, i=i: e.activation(out=st[:, i, 2:3], in_=st[:, i, 1:2], func=AF.Sqrt)),
                 reads=(sb[i],), writes=(sb[i],))
            S.op("dve", (lambda e, i=i: e.reciprocal(out=st[:, i, 3:4], in_=st[:, i, 2:3])),
                 reads=(sb[i],), writes=(sb[i],))
            S.op("dve", (lambda e, i=i: e.scalar_tensor_tensor(out=xt[:, i, :], in0=xt[:, i, :],
                                                                scalar=st[:, i, 3:4], in1=gB[:],
                                                                op0=ALU.mult, op1=ALU.mult)),
                 reads=(xb[i], sb[i], gb), writes=(xb[i],))
            S.dma("pool", (lambda e, i=i, r0=r0: e.dma_start(out=y_ap[r0:r0 + 128, :], in_=xt[:, i, :])),
                  reads=(xb[i],))
        S.run_phase()


def rope_pairs(S, eng, dst, src, cos, sin, tA, tB, reads, writes, half):
    x1, x2 = src[:, :, 0:half], src[:, :, half:2 * half]
    d1, d2 = dst[:, :, 0:half], dst[:, :, half:2 * half]
    ta, tb_ = Buf(), Buf()
    S.op(eng, lambda e: e.tensor_tensor(out=tA, in0=x1, in1=cos, op=ALU.mult), reads=reads, writes=(ta,))
    S.op(eng, lambda e: e.tensor_tensor(out=tB, in0=x2, in1=sin, op=ALU.mult), reads=reads, writes=(tb_,))
    S.op(eng, lambda e: e.tensor_tensor(out=d1, in0=tA, in1=tB, op=ALU.subtract), reads=(ta, tb_), writes=writes)
    S.op(eng, lambda e: e.tensor_tensor(out=tA, in0=x2, in1=cos, op=ALU.mult), reads=reads, writes=(ta,))
    S.op(eng, lambda e: e.tensor_tensor(out=tB, in0=x1, in1=sin, op=ALU.mult), reads=reads, writes=(tb_,))
    S.op(eng, lambda e: e.tensor_tensor(out=d2, in0=tA, in1=tB, op=ALU.add), reads=(ta, tb_), writes=writes)


def mixer_b(C, proj, yT, ropeB_ap, segb_ap, qkg_ap, dbg=None):
    nc, S = C.nc, C.S
    NT = TOK // 128
    with contextlib.ExitStack() as es:
        T = lambda n, s, d: es.enter_context(C.sbuf_tensor(n, s, d))
        qT = T("b_qT", [128, 8, TOK], BF16)
        kT = T("b_kT", [128, 2, TOK], BF16)
        V = T("b_V", [128, NT, 256], BF16)
        ones = T("b_ones", [128, 128], BF16)
        qk = T("b_qk", [128, 2, 1280], F32)
        sq = T("b_sq", [128, 1280], F32)
        ro = T("b_ro", [128, 2, 1280], F32)
        rt = T("b_rt", [128, 2, 128], F32)
        gB = T("b_gB", [128, 1280], F32)
        stt = T("b_st", [128, 2, 3, 10], F32)
        vt = T("b_vt", [128, 2, 256], F32)
        tmp = T("b_tmp", [128, 4, 320], F32)
        segb = T("b_segb", [128, 4], F32)
        pT = T("b_pT", [128, 3, 512], BF16)
        rec = T("b_rec", [128, 2, 512], F32)
        yo = T("b_yo", [128, 2, 512], BF16)
        qTb, kTb, Vb = Buf(), Buf(), Buf()
        onesb, gBb, segbb = Buf(), Buf(), Buf()
        qkb, rob, rtb, stb, vtb = [Buf(), Buf()], [Buf(), Buf()], [Buf(), Buf()], [Buf(), Buf()], [Buf(), Buf()]
        sqb = Buf()
        pTb = [Buf() for _ in range(3)]
        recb, yob = [Buf(), Buf()], [Buf(), Buf()]
        S.op("pool", lambda e: e.memset(ones[:], 1.0), writes=(onesb,))
        for hh in range(10):
            src = qkg_ap[0 if hh < 8 else 1, :].partition_broadcast(128)
            S.dma("sp", (lambda e, hh=hh, src=src: e.dma_start(out=gB[:, hh * 128:(hh + 1) * 128], in_=src)),
                  writes=(gBb,))
        S.dma("sp", lambda e: e.dma_start(out=segb[:], in_=segb_ap), writes=(segbb,))
        for t in range(NT):
            i = t % 2
            r0 = t * 128
            S.dma("sp", (lambda e, i=i, r0=r0: e.dma_start(out=qk[:, i, 0:1024], in_=proj[r0:r0 + 128, O_BQ:O_BQ + 1024])),
                  writes=(qkb[i],))
            S.dma("sp", (lambda e, i=i, r0=r0: e.dma_start(out=qk[:, i, 1024:1280], in_=proj[r0:r0 + 128, O_BK:O_BK + 256])),
                  writes=(qkb[i],))
            S.dma("sp", (lambda e, i=i, r0=r0: e.dma_start(out=vt[:, i, :], in_=proj[r0:r0 + 128, O_BV:O_BV + 256])),
                  writes=(vtb[i],))
            S.dma("sp", (lambda e, i=i, r0=r0: e.dma_start(out=rt[:, i, :], in_=ropeB_ap[r0:r0 + 128, :])),
                  writes=(rtb[i],))
            S.op("dve", (lambda e, i=i: e.tensor_tensor(out=sq[:], in0=qk[:, i, :], in1=qk[:, i, :], op=ALU.mult)),
                 reads=(qkb[i],), writes=(sqb,))

            S.op("dve", (lambda e, i=i: e.tensor_reduce(out=stt[:, i, 0, :], in_=sq[:].rearrange("p (h d) -> p h d", d=128),
                                                         axis=AX.X, op=ALU.add)), reads=(sqb,), writes=(stb[i],))
            S.op("dve", (lambda e, i=i: e.tensor_scalar(out=stt[:, i, 1, :], in0=stt[:, i, 0, :], scalar1=1.0 / 128,
                                                         scalar2=EPS, op0=ALU.mult, op1=ALU.add)),
                 reads=(stb[i],), writes=(stb[i],))
            S.op("act", (lambda e, i=i: e.activation(out=stt[:, i, 2, :], in_=stt[:, i, 1, :], func=AF.Sqrt)),
                 reads=(stb[i],), writes=(stb[i],))
            S.op("dve", (lambda e, i=i: e.reciprocal(out=stt[:, i, 0, :], in_=stt[:, i, 2, :])),
                 reads=(stb[i],), writes=(stb[i],))
            S.op("dve", (lambda e, i=i: e.tensor_scalar(out=stt[:, i, 0, 0:8], in0=stt[:, i, 0, 0:8], scalar1=128.0 ** -0.5,
                                                         scalar2=None, op0=ALU.mult)), reads=(stb[i],), writes=(stb[i],))
            S.op("dve", (lambda e, i=i: e.tensor_tensor(
                out=qk[:, i, :].rearrange("p (h d) -> p h d", d=128), in0=qk[:, i, :].rearrange("p (h d) -> p h d", d=128),
                in1=stt[:, i, 0, :].unsqueeze(2).broadcast_to([128, 10, 128]), op=ALU.mult)),
                reads=(stb[i], qkb[i]), writes=(qkb[i],))
            S.op("dve", (lambda e, i=i: e.tensor_tensor(out=qk[:, i, :], in0=qk[:, i, :], in1=gB[:], op=ALU.mult)),
                 reads=(qkb[i], gBb), writes=(qkb[i],))
            q3 = qk[:, i, :].rearrange("p (h d) -> p h d", d=128)
            r3 = ro[:, i, :].rearrange("p (h d) -> p h d", d=128)
            bc = lambda a: a.unsqueeze(1).broadcast_to([128, 10, 32])
            tv = lambda k: tmp[:, k, :].rearrange("p (h d) -> p h d", d=32)
            rope_pairs(S, "dve", r3[:, :, 0:64], q3[:, :, 0:64], bc(rt[:, i, 0:32]), bc(rt[:, i, 32:64]),
                       tv(0), tv(1), reads=(qkb[i], rtb[i]), writes=(rob[i],), half=32)
            rope_pairs(S, "dve", r3[:, :, 64:128], q3[:, :, 64:128], bc(rt[:, i, 64:96]), bc(rt[:, i, 96:128]),
                       tv(2), tv(3), reads=(qkb[i], rtb[i]), writes=(rob[i],), half=32)
            for g in range(3):
                nh = 4 if g < 2 else 2
                bank = 6 + (g % 2)

                def tr(e, i=i, g=g, nh=nh, bank=bank):
                    ins = None
                    for q in range(nh):
                        hh = g * 4 + q
                        ins = e.transpose(out=C.ps[:, bank, q * 128:(q + 1) * 128],
                                          in_=ro[:, i, hh * 128:(hh + 1) * 128], identity=C.ident[:])
                    return ins
                S.op("pe", tr, reads=(rob[i], C.identb), writes=(C.psb[bank],))
                src = C.ps[:, bank, 0:nh * 128].rearrange("p (h t) -> p h t", t=128)
                if g < 2:
                    copy_op(S, C.evac_eng(), qT[:, g * 4:g * 4 + 4, r0:r0 + 128], src, reads=(C.psb[bank],), writes=(qTb,))
                else:
                    copy_op(S, C.evac_eng(), kT[:, 0:2, r0:r0 + 128], src, reads=(C.psb[bank],), writes=(kTb,))
            S.op("pool", (lambda e, i=i, t=t: e.tensor_copy(out=V[:, t, :], in_=vt[:, i, :])), reads=(vtb[i],), writes=(Vb,))
        if dbg is not None:
            S.dma("sp", lambda e: e.dma_start(out=dbg["qT"], in_=qT[:].rearrange("p h t -> p (h t)")), reads=(qTb,))
            S.dma("sp", lambda e: e.dma_start(out=dbg["kT"], in_=kT[:].rearrange("p h t -> p (h t)")), reads=(kTb,))
            S.dma("sp", lambda e: e.dma_start(out=dbg["V"], in_=V[:].rearrange("p h t -> p (h t)")), reads=(Vb,))
        ucnt = 0
        acnt = 0
        for h in range(8):
            kv = h // 4
            for qb in range(TOK // 512):
                nb, db = 2 + (acnt % 2), 4 + (acnt % 2)
                ai = acnt % 2
                acnt += 1
                qs = 1 if qb >= (TOK // 1024) else 0

                def smm(kt, u):
                    sbank = u % 2
                    S.op("pe", (lambda e, kt=kt, sbank=sbank, kv=kv, h=h, qb=qb: e.matmul(
                        C.ps[:, sbank, :], lhsT=kT[:, kv, kt * 128:(kt + 1) * 128], rhs=qT[:, h, qb * 512:(qb + 1) * 512],
                        start=True, stop=True)), reads=(kTb, qTb), writes=(C.psb[sbank],))
                smm(0, ucnt)
                for kt in range(NT):
                    u = ucnt + kt
                    if kt + 1 < NT:
                        smm(kt + 1, u + 1)
                    sbank = u % 2
                    sl = u % 3
                    idx = (2 if kt >= NT // 2 else 0) + qs
                    S.op("act", (lambda e, sbank=sbank, sl=sl, idx=idx: e.activation(
                        out=pT[:, sl, :], in_=C.ps[:, sbank, :], func=AF.Exp, bias=segb[:, idx:idx + 1], scale=1.0)),
                        reads=(C.psb[sbank], segbb), writes=(pTb[sl],))

                    def pv(e, kt=kt, sl=sl, nb=nb, db=db, kv=kv):
                        e.matmul(C.ps[:, nb, :], lhsT=V[:, kt, kv * 128:(kv + 1) * 128], rhs=pT[:, sl, :],
                                 start=(kt == 0), stop=(kt == NT - 1))
                        return e.matmul(C.ps[:, db, :], lhsT=ones[:], rhs=pT[:, sl, :],
                                        start=(kt == 0), stop=(kt == NT - 1))
                    S.op("pe", pv, reads=(Vb, onesb, pTb[sl]), writes=(C.psb[nb], C.psb[db]))
                ucnt += NT
                S.op("dve", (lambda e, ai=ai, db=db: e.reciprocal(out=rec[:, ai, :], in_=C.ps[:, db, :])),
                     reads=(C.psb[db],), writes=(recb[ai],))
                S.op("dve", (lambda e, ai=ai, nb=nb: e.tensor_tensor(out=yo[:, ai, :], in0=C.ps[:, nb, :], in1=rec[:, ai, :],
                                                                      op=ALU.mult)),
                     reads=(C.psb[nb], recb[ai]), writes=(yob[ai],))
                S.dma("pool", (lambda e, ai=ai, h=h, qb=qb: e.dma_start(
                    out=yT[1024 + h * 128:1024 + (h + 1) * 128, qb * 512:(qb + 1) * 512], in_=yo[:, ai, :])),
                    reads=(yob[ai],))
        S.run_phase()


A_PAT = (1, 4, 16)
A_PAD = 1088


def a_units():
    idx = {}
    n = 0
    for pi, d in enumerate(A_PAT):
        for it in range(TOK // d // 128):
            for blk in range(2):
                idx[(pi, it, blk)] = n
                n += 1
    return idx, n


def mixer_a_prep(C, proj, aqT, akT, avb, ropeA_ap):
    nc, S = C.nc, C.S
    NT = TOK // 128
    with contextlib.ExitStack() as es:
        T = lambda n, s, d: es.enter_context(C.sbuf_tensor(n, s, d))
        qk = T("a_qk", [128, 2, 2048], F32)
        rt = T("a_rt", [128, 2, 32], F32)
        tmp = T("a_tmp", [128, 4, 256], F32)
        vt = T("a_vt", [128, 2, 1024], F32)
        vb = T("a_vb", [128, 2, 1024], BF16)
        stg = T("a_stg", [128, 2, 16, 512], BF16)
        qkb, rtb, vtb, vbb = [Buf(), Buf()], [Buf(), Buf()], [Buf(), Buf()], [Buf(), Buf()]
        stgb = [Buf(), Buf()]
        tb4 = [Buf() for _ in range(4)]
        for t in range(NT):
            i = t % 2
            r0 = t * 128
            S.dma("sp", (lambda e, i=i, r0=r0: e.dma_start(out=qk[:, i, :], in_=proj[r0:r0 + 128, O_AQ:O_AQ + 2048])),
                  writes=(qkb[i],))
            S.dma("sp", (lambda e, i=i, r0=r0: e.dma_start(out=rt[:, i, :], in_=ropeA_ap[r0:r0 + 128, :])),
                  writes=(rtb[i],))
            S.dma("sp", (lambda e, i=i, r0=r0: e.dma_start(out=vt[:, i, :], in_=proj[r0:r0 + 128, O_AV:O_AV + 1024])),
                  writes=(vtb[i],))
            S.op("pool", (lambda e, i=i: e.tensor_copy(out=vb[:, i, :], in_=vt[:, i, :])), reads=(vtb[i],), writes=(vbb[i],))
            S.dma("pool", (lambda e, i=i, r0=r0: e.dma_start(out=avb[r0:r0 + 128, :], in_=vb[:, i, :])), reads=(vbb[i],))
            q3 = qk[:, i, :].rearrange("p (h d) -> p h d", d=128)
            x1, x2 = q3[:, :, 0:16], q3[:, :, 16:32]
            cs = rt[:, i, 0:16].unsqueeze(1).broadcast_to([128, 16, 16])
            sn = rt[:, i, 16:32].unsqueeze(1).broadcast_to([128, 16, 16])
            tv = lambda k: tmp[:, k, :].rearrange("p (h d) -> p h d", d=16)
            rd = (qkb[i], rtb[i])
            S.op("dve", (lambda e, x1=x1, cs=cs: e.tensor_tensor(out=tv(0), in0=x1, in1=cs, op=ALU.mult)), reads=rd, writes=(tb4[0],))
            S.op("dve", (lambda e, x2=x2, sn=sn: e.tensor_tensor(out=tv(1), in0=x2, in1=sn, op=ALU.mult)), reads=rd, writes=(tb4[1],))
            S.op("dve", (lambda e, x2=x2, cs=cs: e.tensor_tensor(out=tv(2), in0=x2, in1=cs, op=ALU.mult)), reads=rd, writes=(tb4[2],))
            S.op("dve", (lambda e, x1=x1, sn=sn: e.tensor_tensor(out=tv(3), in0=x1, in1=sn, op=ALU.mult)), reads=rd, writes=(tb4[3],))
            S.op("dve", (lambda e, x1=x1: e.tensor_tensor(out=x1, in0=tv(0), in1=tv(1), op=ALU.subtract)),
                 reads=(tb4[0], tb4[1]), writes=(qkb[i],))
            S.op("dve", (lambda e, x2=x2: e.tensor_tensor(out=x2, in0=tv(2), in1=tv(3), op=ALU.add)),
                 reads=(tb4[2], tb4[3]), writes=(qkb[i],))
            si = (t // 4) % 2
            tt = t % 4
            for g in range(4):
                bank = 4 + g

                def tr(e, i=i, g=g, bank=bank):
                    ins = None
                    for q in range(4):
                        hh = g * 4 + q
                        ins = e.transpose(out=C.ps[:, bank, q * 128:(q + 1) * 128],
                                          in_=qk[:, i, hh * 128:(hh + 1) * 128], identity=C.ident[:])
                    return ins
                S.op("pe", tr, reads=(qkb[i], C.identb), writes=(C.psb[bank],))
                src = C.ps[:, bank, :].rearrange("p (h t) -> p h t", t=128)
                copy_op(S, C.evac_eng(), stg[:, si, g * 4:g * 4 + 4, tt * 128:(tt + 1) * 128], src,
                        reads=(C.psb[bank],), writes=(stgb[si],), scale=(128.0 ** -0.5 if g < 2 else None))
            if tt == 3:
                c0 = (t // 4) * 512
                S.dma("pool", (lambda e, si=si, c0=c0: e.dma_start(
                    out=aqT[:, :, c0:c0 + 512].rearrange("h p t -> p h t"), in_=stg[:, si, 0:8, :])), reads=(stgb[si],))
                S.dma("pool", (lambda e, si=si, c0=c0: e.dma_start(
                    out=akT[:, :, c0:c0 + 512].rearrange("h p t -> p h t"), in_=stg[:, si, 8:16, :])), reads=(stgb[si],))
        S.run_phase()


def mixer_a(C, aqT, akT, avb, yT, abias_ap, amask_ap):
    nc, S = C.nc, C.S
    uidx, NU = a_units()
    with contextlib.ExitStack() as es:
        T = lambda n, s, d: es.enter_context(C.sbuf_tensor(n, s, d))
        kTh = T("a_kT", [128, 2, TOK + 2 * A_PAD], BF16)
        qTh = T("a_qT", [128, 2, TOK], BF16)
        V1 = T("a_V1", [128, 2, 33, 128], BF16)
        V4 = T("a_V4", [128, 2, 4, 9, 128], BF16)
        V16 = T("a_V16", [128, 2, 16, 3, 128], BF16)
        num = T("a_num", [128, TOK], F32)
        den = T("a_den", [128, TOK], F32)
        yo = T("a_yo", [128, TOK], BF16)
        ones = T("a_ones", [128, 128], BF16)
        mskf = T("a_mskf", [128, 256], F32)
        msk = T("a_msk", [128, 256], BF16)
        bias = T("a_bias", [128, NU], F32)
        pT = T("a_pT", [128, 3, 256], BF16)
        pM = T("a_pM", [128, 3, 256], BF16)
        kb, qb_, vb = [Buf(), Buf()], [Buf(), Buf()], [Buf(), Buf()]
        numb, denb, yob, onesb, mskb, biasb, mskfb = Buf(), Buf(), Buf(), Buf(), Buf(), Buf(), Buf()
        pTb, pMb = [Buf() for _ in range(3)], [Buf() for _ in range(3)]
        S.op("pool", lambda e: e.memset(ones[:], 1.0), writes=(onesb,))
        for i in range(2):
            S.op("pool", (lambda e, i=i: e.memset(kTh[:, i, :], 0.0)), writes=(kb[i],))
            S.op("pool", (lambda e, i=i: e.memset(V1[:, i], 0.0)), writes=(vb[i],))
            S.op("pool", (lambda e, i=i: e.memset(V4[:, i], 0.0)), writes=(vb[i],))
            S.op("pool", (lambda e, i=i: e.memset(V16[:, i], 0.0)), writes=(vb[i],))
        S.dma("sp", lambda e: e.dma_start(out=mskf[:], in_=amask_ap), writes=(mskfb,))
        S.op("dve", lambda e: e.tensor_copy(out=msk[:], in_=mskf[:]), reads=(mskfb,), writes=(mskb,))
        S.dma("sp", lambda e: e.dma_start(out=bias[:], in_=abias_ap), writes=(biasb,))
        Vt = (V1, V4, V16)
        u = 0
        for h in range(8):
            hi = h % 2
            S.dma("sp", (lambda e, h=h, hi=hi: e.dma_start(out=kTh[:, hi, A_PAD:A_PAD + TOK], in_=akT[h])), writes=(kb[hi],))
            S.dma("sp", (lambda e, h=h, hi=hi: e.dma_start(out=qTh[:, hi, :], in_=aqT[h])), writes=(qb_[hi],))
            for pi, d in enumerate(A_PAT):
                nm = TOK // d // 128 + 1
                for r in range(d):
                    for m in range(nm):
                        p0 = 64 if m == 0 else 0
                        p1 = 64 if m == nm - 1 else 128
                        t0 = r + d * (128 * m - 64 + p0)
                        src = avb[t0:t0 + d * (p1 - p0 - 1) + 1:d, h * 128:(h + 1) * 128]
                        if pi == 0:
                            dst = V1[p0:p1, hi, m, :]
                        elif pi == 1:
                            dst = V4[p0:p1, hi, r, m, :]
                        else:
                            dst = V16[p0:p1, hi, r, m, :]
                        S.dma("sp", (lambda e, dst=dst, src=src: e.dma_start(out=dst, in_=src)), writes=(vb[hi],))
            for pi, d in enumerate(A_PAT):
                for r in range(d):
                    for it in range(TOK // d // 128):
                        sbank = u % 2
                        nbank = 2 + (u % 2)
                        dbank = 4 + (u % 2)
                        sl = u % 3
                        u += 1
                        qc0 = r + d * 128 * it
                        qsl = slice(qc0, qc0 + d * 127 + 1, d)

                        def smm(e, hi=hi, d=d, r=r, it=it, sbank=sbank, qsl=qsl):
                            ins = None
                            for blk in range(2):
                                k0 = A_PAD + r + d * (128 * it + (64 if blk else -64))
                                ins = e.matmul(C.ps[:, sbank, blk * 128:(blk + 1) * 128],
                                               lhsT=kTh[:, hi, k0:k0 + d * 127 + 1:d], rhs=qTh[:, hi, qsl],
                                               start=True, stop=True)
                            return ins
                        S.op("pe", smm, reads=(kb[hi], qb_[hi]), writes=(C.psb[sbank],))
                        for blk in range(2):
                            col = uidx[(pi, it, blk)]
                            S.op("act", (lambda e, sbank=sbank, sl=sl, blk=blk, col=col: e.activation(
                                out=pT[:, sl, blk * 128:(blk + 1) * 128], in_=C.ps[:, sbank, blk * 128:(blk + 1) * 128],
                                func=AF.Exp, bias=bias[:, col:col + 1], scale=1.0)),
                                reads=(C.psb[sbank], biasb), writes=(pTb[sl],))
                        S.op("pool", (lambda e, sl=sl: e.tensor_tensor(out=pM[:, sl, :], in0=pT[:, sl, :], in1=msk[:], op=ALU.mult)),
                             reads=(pTb[sl], mskb), writes=(pMb[sl],))

                        def pv(e, hi=hi, pi=pi, r=r, it=it, sl=sl, nbank=nbank, dbank=dbank):
                            for blk in range(2):
                                if pi == 0:
                                    vv = V1[:, hi, it + blk, :]
                                elif pi == 1:
                                    vv = V4[:, hi, r, it + blk, :]
                                else:
                                    vv = V16[:, hi, r, it + blk, :]
                                e.matmul(C.ps[:, nbank, 0:128], lhsT=vv, rhs=pM[:, sl, blk * 128:(blk + 1) * 128],
                                         start=(blk == 0), stop=(blk == 1))
                            ins = None
                            for blk in range(2):
                                ins = e.matmul(C.ps[:, dbank, 0:128], lhsT=ones[:], rhs=pM[:, sl, blk * 128:(blk + 1) * 128],
                                               start=(blk == 0), stop=(blk == 1))
                            return ins
                        S.op("pe", pv, reads=(vb[hi], onesb, pMb[sl]), writes=(C.psb[nbank], C.psb[dbank]))
                        if pi == 0:
                            S.op("dve", (lambda e, nbank=nbank, qsl=qsl: e.tensor_copy(out=num[:, qsl], in_=C.ps[:, nbank, 0:128])),
                                 reads=(C.psb[nbank],), writes=(numb,))
                            S.op("act", (lambda e, dbank=dbank, qsl=qsl: e.activation(out=den[:, qsl], in_=C.ps[:, dbank, 0:128],
                                                                                       func=AF.Copy)),
                                 reads=(C.psb[dbank],), writes=(denb,))
                        else:
                            S.op("dve", (lambda e, nbank=nbank, qsl=qsl: e.tensor_tensor(
                                out=num[:, qsl], in0=C.ps[:, nbank, 0:128], in1=num[:, qsl], op=ALU.add)),
                                reads=(C.psb[nbank], numb), writes=(numb,))
                            S.op("dve", (lambda e, dbank=dbank, qsl=qsl: e.tensor_tensor(
                                out=den[:, qsl], in0=C.ps[:, dbank, 0:128], in1=den[:, qsl], op=ALU.add)),
                                reads=(C.psb[dbank], denb), writes=(denb,))
            S.op("dve", lambda e: e.reciprocal(out=den[:], in_=den[:]), reads=(denb,), writes=(denb,))
            S.op("dve", lambda e: e.tensor_tensor(out=yo[:], in0=num[:], in1=den[:], op=ALU.mult),
                 reads=(numb, denb), writes=(yob,))
            S.dma("pool", (lambda e, h=h: e.dma_start(out=yT[h * 128:(h + 1) * 128, :], in_=yo[:])), reads=(yob,))
        S.run_phase()


def a_consts(is_prompt):
    L = 2048 if is_prompt else 4096
    pos = np.tile(np.arange(L, dtype=np.float32), TOK // L)
    inv = (np.float32(500000.0) ** (-np.arange(0, 32, 2, dtype=np.float32) / np.float32(32))).astype(np.float32)
    ang = pos[:, None] * inv[None, :]
    ropeA = np.concatenate([np.cos(ang), np.sin(ang)], -1).astype(np.float32)
    uidx, NU = a_units()
    abias = np.zeros((128, NU), np.float32)
    a = np.arange(128)
    for (pi, it, blk), col in uidx.items():
        d = A_PAT[pi]
        Lseg = L // d
        kidx = 128 * it + (64 if blk else -64) + a
        seg = (128 * it) // Lseg
        valid = (kidx >= seg * Lseg) & (kidx < (seg + 1) * Lseg)
        abias[:, col] = np.where(valid, 0.0, -30000.0)
    aa, bb = np.meshgrid(np.arange(128), np.arange(128), indexing="ij")
    amask = np.concatenate([(aa >= bb), (aa <= bb)], 1).astype(np.float32)
    return {"ropeA": ropeA, "abias": abias, "amask": amask}


def mixer_d_prep(C, proj, dqT, dkT, dkb, dvb, dgt, gateb_ap, carry_ap, tri_ap):
    nc, S = C.nc, C.S
    NT = TOK // 128
    with contextlib.ExitStack() as es:
        T = lambda n, s, d: es.enter_context(C.sbuf_tensor(n, s, d))
        qk = T("d_qk", [128, 2, 2048], F32)
        vt = T("d_vt", [128, 2, 1024], F32)
        kvb = T("d_kvb", [128, 2, 2048], BF16)
        stg = T("d_stg", [128, 2, 16, 512], BF16)
        g = T("d_g", [128, 2, 16], F32)
        gb_ = T("d_gb", [128, 16], F32)
        tri = T("d_tri", [128, 384], F32)
        car = T("d_car", [128, NT, 8], F32)
        tmp = T("d_tmp", [128, 2, 16], F32)
        gt = T("d_gt", [128, NT, 24], F32)
        qkb, vtb, kvbb, stgb, gbuf, tmpb = [Buf(), Buf()], [Buf(), Buf()], [Buf(), Buf()], [Buf(), Buf()], [Buf(), Buf()], [Buf(), Buf()]
        gbb, trib, carb, gtb = Buf(), Buf(), Buf(), Buf()
        S.dma("sp", lambda e: e.dma_start(out=gb_[:], in_=gateb_ap.rearrange("a b -> (a b)").partition_broadcast(128)), writes=(gbb,))
        S.dma("sp", lambda e: e.dma_start(out=tri[:], in_=tri_ap), writes=(trib,))
        S.dma("sp", lambda e: e.dma_start(out=car[:], in_=carry_ap), writes=(carb,))
        for t in range(NT):
            i = t % 2
            r0 = t * 128
            S.dma("sp", (lambda e, i=i, r0=r0: e.dma_start(out=qk[:, i, :], in_=proj[r0:r0 + 128, O_DQ:O_DQ + 2048])), writes=(qkb[i],))
            S.dma("sp", (lambda e, i=i, r0=r0: e.dma_start(out=vt[:, i, :], in_=proj[r0:r0 + 128, O_DV:O_DV + 1024])), writes=(vtb[i],))
            S.dma("sp", (lambda e, i=i, r0=r0: e.dma_start(out=g[:, i, :], in_=proj[r0:r0 + 128, O_DG:O_DG + 16])), writes=(gbuf[i],))
            S.op("pool", (lambda e, i=i: e.tensor_scalar(out=kvb[:, i, 0:1024], in0=qk[:, i, 1024:2048], scalar1=1.0 / 16, scalar2=None,
                                                          op0=ALU.mult)), reads=(qkb[i],), writes=(kvbb[i],))
            S.op("pool", (lambda e, i=i: e.tensor_copy(out=kvb[:, i, 1024:2048], in_=vt[:, i, :])), reads=(vtb[i],), writes=(kvbb[i],))
            S.dma("pool", (lambda e, i=i, r0=r0: e.dma_start(out=dkb[r0:r0 + 128, :], in_=kvb[:, i, 0:1024])), reads=(kvbb[i],))
            S.dma("pool", (lambda e, i=i, r0=r0: e.dma_start(out=dvb[r0:r0 + 128, :], in_=kvb[:, i, 1024:2048])), reads=(kvbb[i],))
            si = (t // 4) % 2
            tt = t % 4
            for gq in range(4):
                bank = 4 + gq

                def tr(e, i=i, gq=gq, bank=bank):
                    ins = None
                    for q in range(4):
                        hh = gq * 4 + q
                        ins = e.transpose(out=C.ps[:, bank, q * 128:(q + 1) * 128],
                                          in_=qk[:, i, hh * 128:(hh + 1) * 128], identity=C.ident[:])
                    return ins
                S.op("pe", tr, reads=(qkb[i], C.identb), writes=(C.psb[bank],))
                src = C.ps[:, bank, :].rearrange("p (h t) -> p h t", t=128)
                copy_op(S, C.evac_eng(), stg[:, si, gq * 4:gq * 4 + 4, tt * 128:(tt + 1) * 128], src,
                        reads=(C.psb[bank],), writes=(stgb[si],), scale=(1.0 / 16 if gq >= 2 else None))
            if tt == 3:
                c0 = (t // 4) * 512
                S.dma("pool", (lambda e, si=si, c0=c0: e.dma_start(
                    out=dqT[:, :, c0:c0 + 512].rearrange("h p t -> p h t"), in_=stg[:, si, 0:8, :])), reads=(stgb[si],))
                S.dma("pool", (lambda e, si=si, c0=c0: e.dma_start(
                    out=dkT[:, :, c0:c0 + 512].rearrange("h p t -> p h t"), in_=stg[:, si, 8:16, :])), reads=(stgb[si],))
            S.op("dve", (lambda e, i=i: e.tensor_tensor(out=g[:, i, :], in0=g[:, i, :], in1=gb_[:], op=ALU.add)),
                 reads=(gbuf[i], gbb), writes=(gbuf[i],))
            g4 = lambda i: g[:, i, :].rearrange("p (a h) -> p a h", h=4)
            t4 = lambda i: tmp[:, i, :].rearrange("p (a h) -> p a h", h=4)
            S.op("act", (lambda e, i=i: e.activation(out=t4(i)[:, 0:2, :], in_=g4(i)[:, 1:4:2, :], func=AF.Exp, scale=-1.0)),
                 reads=(gbuf[i],), writes=(tmpb[i],))
            S.op("act", (lambda e, i=i: e.activation(out=t4(i)[:, 0:2, :], in_=t4(i)[:, 0:2, :], func=AF.Ln, bias=1.0, scale=1.0)),
                 reads=(tmpb[i],), writes=(tmpb[i],))
            S.op("dve", (lambda e, i=i: e.tensor_scalar(out=tmp[:, i, 0:8], in0=tmp[:, i, 0:8], scalar1=-1.0, scalar2=None, op0=ALU.mult)),
                 reads=(tmpb[i],), writes=(tmpb[i],))

            def cums(e, i=i):
                e.matmul(C.ps[:, 3, 0:4], lhsT=tri[:, 0:128], rhs=tmp[:, i, 0:4], start=True, stop=True)
                e.matmul(C.ps[:, 3, 4:8], lhsT=tri[:, 128:256], rhs=tmp[:, i, 4:8], start=True, stop=True)
                return e.matmul(C.ps[:, 3, 8:16], lhsT=tri[:, 256:384], rhs=tmp[:, i, 0:8], start=True, stop=True)
            S.op("pe", cums, reads=(tmpb[i], trib), writes=(C.psb[3],))
            S.op("act", (lambda e, t=t: e.activation(out=gt[:, t, 0:8], in_=C.ps[:, 3, 0:8], func=AF.Exp)),
                 reads=(C.psb[3],), writes=(gtb,))
            S.op("act", (lambda e, t=t: e.activation(out=gt[:, t, 16:24], in_=C.ps[:, 3, 8:16], func=AF.Exp)),
                 reads=(C.psb[3],), writes=(gtb,))
            S.op("dve", (lambda e, t=t: e.tensor_tensor(out=gt[:, t, 16:24], in0=gt[:, t, 16:24], in1=car[:, t, :], op=ALU.mult)),
                 reads=(gtb, carb), writes=(gtb,))
            S.op("dve", (lambda e, i=i, t=t: e.tensor_tensor(
                out=gt[:, t, 8:16].rearrange("p (a h) -> p a h", h=4), in0=g4(i)[:, 0:4:2, :],
                in1=C.ps[:, 3, 0:8].rearrange("p (a h) -> p a h", h=4), op=ALU.subtract)),
                reads=(gbuf[i], C.psb[3]), writes=(gtb,))
            S.op("act", (lambda e, t=t: e.activation(out=gt[:, t, 8:16], in_=gt[:, t, 8:16], func=AF.Exp)),
                 reads=(gtb,), writes=(gtb,))
        S.dma("pool", lambda e: e.dma_start(out=dgt, in_=gt[:]), reads=(gtb,))
        S.run_phase()


def mixer_d(C, proj, dqT, dkT, dkb, dvb, dgt, yT, mlg_ap, tri_ap):
    nc, S = C.nc, C.S
    NT = TOK // 128
    with contextlib.ExitStack() as es:
        T = lambda n, s, d: es.enter_context(C.sbuf_tensor(n, s, d))
        qTh = T("m_qT", [128, 2, TOK], BF16)
        kTh = T("m_kT", [128, 2, TOK], BF16)
        kt = T("m_kt", [128, NT, 256], BF16)
        va = T("m_va", [128, NT, 258], BF16)
        gt = T("m_gt", [128, NT, 24], F32)
        tri = T("m_tri", [128, 384], F32)
        hacc = T("m_h", [128, NT, 256], F32)
        St = T("m_S", [128, 2, 2, 257], F32)
        Sb = T("m_Sb", [128, 2, 2, 258], BF16)
        At = T("m_A", [128, 2, 2, 128], BF16)
        Ku = T("m_Ku", [128, 2, 2, 256], BF16)
        sc = T("m_sc", [128, 2, 2, 4], F32)
        ot = T("m_o", [128, 2, 256], F32)
        gB = T("m_gB", [128, 256], F32)
        fs = T("m_fs", [128, 2, 4], F32)
        yy = T("m_y", [128, 2, 256], F32)
        stg = T("m_stg", [128, 2, 2, 512], BF16)
        B = Buf
        qb_, kb_, ktb, vab, gtb, trib, haccb, gBb = B(), B(), B(), B(), B(), B(), B(), B()
        Stb = [[B(), B()], [B(), B()]]
        Sbb = [[B(), B()], [B(), B()]]
        Atb = [[B(), B()], [B(), B()]]
        Kub = [[B(), B()], [B(), B()]]
        scb = [[B(), B()], [B(), B()]]
        otb, fsb, yyb, stgb = [B(), B()], [B(), B()], [B(), B()], [B(), B()]
        S.dma("sp", lambda e: e.dma_start(out=gt[:], in_=dgt), writes=(gtb,))
        S.dma("sp", lambda e: e.dma_start(out=tri[:], in_=tri_ap), writes=(trib,))
        S.op("pool", lambda e: e.memset(va[:], 1.0), writes=(vab,))
        for hd in range(4):
            S.dma("sp", (lambda e, hd=hd: e.dma_start(out=qTh[:], in_=dqT[2 * hd:2 * hd + 2].rearrange("c p t -> p c t"))), writes=(qb_,))
            S.dma("sp", (lambda e, hd=hd: e.dma_start(out=kTh[:], in_=dkT[2 * hd:2 * hd + 2].rearrange("c p t -> p c t"))), writes=(kb_,))
            S.dma("sp", (lambda e, hd=hd: e.dma_start(out=kt[:], in_=dkb[:, hd * 256:(hd + 1) * 256].rearrange("(c p) d -> p c d", p=128))),
                  writes=(ktb,))
            S.dma("sp", (lambda e, hd=hd: e.dma_start(out=va[:, :, 0:256], in_=dvb[:, hd * 256:(hd + 1) * 256].rearrange("(c p) d -> p c d", p=128))),
                  writes=(vab,))
            S.dma("sp", (lambda e, hd=hd: e.dma_start(out=gB[:], in_=mlg_ap[hd * 256:(hd + 1) * 256].partition_broadcast(128))), writes=(gBb,))
            for dr in range(2):
                for dc in range(2):
                    S.op("pool", (lambda e, dr=dr, dc=dc: e.memset(St[:, dr, dc, :], 0.0)), writes=(Stb[dr][dc],))
                    S.op("pool", (lambda e, dr=dr, dc=dc: e.memset(Sb[:, dr, dc, :], 0.0)), writes=(Sbb[dr][dc],))
            for j in range(NT):
                sl = j % 2
                for dr in range(2):
                    c = j if dr == 0 else NT - 1 - j
                    cs = slice(c * 128, (c + 1) * 128)
                    col = dr * 4 + hd
                    sbank = dr
                    xbank = 2 + dr
                    rb = 4 + 2 * dr

                    def smm(e, cs=cs, sbank=sbank):
                        e.matmul(C.ps[:, sbank, 0:128], lhsT=kTh[:, 0, cs], rhs=qTh[:, 0, cs], start=True, stop=False)
                        return e.matmul(C.ps[:, sbank, 0:128], lhsT=kTh[:, 1, cs], rhs=qTh[:, 1, cs], start=False, stop=True)
                    S.op("pe", smm, reads=(kb_, qb_), writes=(C.psb[sbank],))
                    S.op("dve", (lambda e, sl=sl, dr=dr, c=c, col=col, sbank=sbank: e.scalar_tensor_tensor(
                        out=At[:, sl, dr, :], in0=C.ps[:, sbank, 0:128], scalar=gt[:, c, 8 + col:9 + col],
                        in1=tri[:, dr * 128:(dr + 1) * 128], op0=ALU.mult, op1=ALU.mult)),
                        reads=(C.psb[sbank], gtb, trib), writes=(Atb[sl][dr],))
                    S.op("pool", (lambda e, sl=sl, dr=dr, c=c, col=col: e.tensor_scalar(
                        out=Ku[:, sl, dr, :], in0=kt[:, c, :], scalar1=gt[:, c, 8 + col:9 + col], scalar2=None, op0=ALU.mult)),
                        reads=(ktb, gtb), writes=(Kub[sl][dr],))

                    def xmm(e, cs=cs, c=c, sl=sl, dr=dr, xbank=xbank):
                        e.matmul(C.ps[:, xbank, 0:257], lhsT=qTh[:, 0, cs], rhs=Sb[:, dr, 0, 0:257], start=True, stop=False)
                        e.matmul(C.ps[:, xbank, 0:257], lhsT=qTh[:, 1, cs], rhs=Sb[:, dr, 1, 0:257], start=False, stop=False)
                        return e.matmul(C.ps[:, xbank, 0:257], lhsT=At[:, sl, dr, :], rhs=va[:, c, 0:257], start=False, stop=True)
                    S.op("pe", xmm, reads=(qb_, Sbb[dr][0], Sbb[dr][1], Atb[sl][dr], vab), writes=(C.psb[xbank],))

                    def rmm(e, c=c, sl=sl, dr=dr, rb=rb):
                        e.matmul(C.ps[:, rb, 0:257], lhsT=Ku[:, sl, dr, 0:128], rhs=va[:, c, 0:257], start=True, stop=True)
                        return e.matmul(C.ps[:, rb + 1, 0:257], lhsT=Ku[:, sl, dr, 128:256], rhs=va[:, c, 0:257], start=True, stop=True)
                    S.op("pe", rmm, reads=(Kub[sl][dr], vab), writes=(C.psb[rb], C.psb[rb + 1]))
                    sv = sc[:, sl, dr, :]
                    S.op("dve", (lambda e, sv=sv, c=c, col=col, xbank=xbank: e.tensor_scalar(
                        out=sv[:, 3:4], in0=C.ps[:, xbank, 256:257], scalar1=gt[:, c, col:col + 1], scalar2=None,
                        op0=ALU.mult)), reads=(C.psb[xbank], gtb), writes=(scb[sl][dr],))
                    S.op("act", (lambda e, sv=sv: e.activation(out=sv[:, 0:1], in_=sv[:, 3:4], func=AF.Abs)),
                         reads=(scb[sl][dr],), writes=(scb[sl][dr],))
                    S.op("dve", (lambda e, sv=sv: e.tensor_scalar(out=sv[:, 0:1], in0=sv[:, 0:1], scalar1=1.0, scalar2=None, op0=ALU.max)),
                         reads=(scb[sl][dr],), writes=(scb[sl][dr],))
                    S.op("dve", (lambda e, sv=sv: e.reciprocal(out=sv[:, 1:2], in_=sv[:, 0:1])), reads=(scb[sl][dr],), writes=(scb[sl][dr],))
                    S.op("dve", (lambda e, sv=sv, c=c, col=col: e.tensor_tensor(out=sv[:, 2:3], in0=sv[:, 1:2], in1=gt[:, c, col:col + 1],
                                                                                op=ALU.mult)), reads=(scb[sl][dr], gtb), writes=(scb[sl][dr],))
                    first = (dr == 0 and c < NT // 2) or (dr == 1 and c >= NT // 2)
                    if first:
                        S.op("act", (lambda e, sv=sv, c=c, xbank=xbank: e.activation(out=hacc[:, c, :], in_=C.ps[:, xbank, 0:256],
                                                                                      func=AF.Copy, scale=sv[:, 2:3])),
                             reads=(C.psb[xbank], scb[sl][dr]), writes=(haccb,))
                    else:
                        S.op("dve", (lambda e, sv=sv, c=c, xbank=xbank: e.scalar_tensor_tensor(
                            out=hacc[:, c, :], in0=C.ps[:, xbank, 0:256], scalar=sv[:, 2:3], in1=hacc[:, c, :],
                            op0=ALU.mult, op1=ALU.add)), reads=(C.psb[xbank], scb[sl][dr], haccb), writes=(haccb,))
                    for dc in range(2):
                        S.op("dve", (lambda e, dr=dr, dc=dc, c=c, col=col: e.tensor_scalar(
                            out=St[:, dr, dc, :], in0=St[:, dr, dc, :], scalar1=gt[:, c, 16 + col:17 + col], scalar2=None, op0=ALU.mult)),
                            reads=(gtb,), writes=(Stb[dr][dc],))
                        S.op("dve", (lambda e, dr=dr, dc=dc, c=c, col=col, rb=rb: e.scalar_tensor_tensor(
                            out=St[:, dr, dc, :], in0=C.ps[:, rb + dc, 0:257], scalar=gt[:, c, 16 + col:17 + col], in1=St[:, dr, dc, :],
                            op0=ALU.mult, op1=ALU.add)), reads=(C.psb[rb + dc], gtb), writes=(Stb[dr][dc],))
                        S.op("act", (lambda e, dr=dr, dc=dc: e.activation(out=Sb[:, dr, dc, 0:257], in_=St[:, dr, dc, :], func=AF.Copy)),
                             reads=(Stb[dr][dc],), writes=(Sbb[dr][dc],))
            for c in range(NT):
                i = c % 2
                r0 = c * 128
                S.dma("sp", (lambda e, i=i, r0=r0, hd=hd: e.dma_start(out=ot[:, i, :], in_=proj[r0:r0 + 128, O_DO + hd * 256:O_DO + (hd + 1) * 256])),
                      writes=(otb[i],))
                S.op("act", (lambda e, i=i, c=c: e.activation(out=yy[:, i, :], in_=hacc[:, c, :], func=AF.Square, accum_out=fs[:, i, 0:1])),
                     reads=(haccb,), writes=(yyb[i], fsb[i]))
                S.op("dve", (lambda e, i=i: e.tensor_scalar(out=fs[:, i, 1:2], in0=fs[:, i, 0:1], scalar1=1.0 / 256, scalar2=EPS,
                                                             op0=ALU.mult, op1=ALU.add)), reads=(fsb[i],), writes=(fsb[i],))
                S.op("act", (lambda e, i=i: e.activation(out=fs[:, i, 2:3], in_=fs[:, i, 1:2], func=AF.Sqrt)), reads=(fsb[i],), writes=(fsb[i],))
                S.op("dve", (lambda e, i=i: e.reciprocal(out=fs[:, i, 3:4], in_=fs[:, i, 2:3])), reads=(fsb[i],), writes=(fsb[i],))
                S.op("dve", (lambda e, i=i, c=c: e.scalar_tensor_tensor(out=yy[:, i, :], in0=hacc[:, c, :], scalar=fs[:, i, 3:4], in1=gB[:],
                                                                          op0=ALU.mult, op1=ALU.mult)),
                     reads=(haccb, fsb[i], gBb), writes=(yyb[i],))
                S.op("act", (lambda e, i=i: e.activation(out=ot[:, i, :], in_=ot[:, i, :], func=AF.Sigmoid)), reads=(otb[i],), writes=(otb[i],))
                S.op("dve", (lambda e, i=i: e.tensor_tensor(out=yy[:, i, :], in0=yy[:, i, :], in1=ot[:, i, :], op=ALU.mult)),
                     reads=(yyb[i], otb[i]), writes=(yyb[i],))
                bank = c % 2

                def tr(e, i=i, bank=bank):
                    e.transpose(out=C.ps[:, bank, 0:128], in_=yy[:, i, 0:128], identity=C.ident[:])
                    return e.transpose(out=C.ps[:, bank, 128:256], in_=yy[:, i, 128:256], identity=C.ident[:])
                S.op("pe", tr, reads=(yyb[i], C.identb), writes=(C.psb[bank],))
                si = (c // 4) % 2
                tt = c % 4
                copy_op(S, C.evac_eng(), stg[:, si, :, tt * 128:(tt + 1) * 128], C.ps[:, bank, 0:256].rearrange("p (a t) -> p a t", t=128),
                        reads=(C.psb[bank],), writes=(stgb[si],))
                if tt == 3:
                    c0 = (c // 4) * 512
                    row0 = 3072 + hd * 256
                    S.dma("pool", (lambda e, si=si, c0=c0, row0=row0: e.dma_start(
                        out=yT[row0:row0 + 256, c0:c0 + 512].rearrange("(a p) t -> p a t", p=128), in_=stg[:, si, :, :])), reads=(stgb[si],))
        S.run_phase()


def d_consts(is_prompt):
    NT = TOK // 128
    a, b = np.meshgrid(np.arange(128), np.arange(128), indexing="ij")
    tri = np.concatenate([(a <= b), (a >= b), np.ones((128, 128), bool)], 1).astype(np.float32)
    carry = np.ones((128, NT, 8), np.float32)
    if is_prompt:
        carry[:, NT // 2 - 1, 0:4] = 0.0
        carry[:, NT // 2, 4:8] = 0.0
    return {"tri": tri, "carry": carry}


NF = 3072
NDFT = 6144


def hy_conv_phase(C, proj, ucT, vtok, cw_ap, cb_ap, mprev_ap, mnext_ap):
    nc, S = C.nc, C.S
    NT = TOK // 128
    with contextlib.ExitStack() as es:
        T = lambda n, s, d: es.enter_context(C.sbuf_tensor(n, s, d))
        u = T("c_u", [128, 2, 3, 1024], F32)
        wB = T("c_w", [128, 3, 1024], F32)
        bB = T("c_b", [128, 1024], F32)
        acc = T("c_acc", [128, 2, 3, 1024], F32)
        vb = T("c_vb", [128, 2, 1024], BF16)
        mp = T("c_mp", [128, 2, NT], F32)
        stg = T("c_stg", [128, 2, 8, 512], F32)
        B = Buf
        ub = [[B(), B(), B()], [B(), B(), B()]]
        wb, bb, mpb = B(), B(), B()
        accb = [[B(), B(), B()], [B(), B(), B()]]
        vbb, stgb = [B(), B()], [B(), B()]
        S.dma("sp", lambda e: e.dma_start(out=mp[:, 0, :], in_=mprev_ap), writes=(mpb,))
        S.dma("sp", lambda e: e.dma_start(out=mp[:, 1, :], in_=mnext_ap), writes=(mpb,))
        for i in range(2):
            for k in range(3):
                S.op("pool", (lambda e, i=i, k=k: e.memset(u[:, i, k, :], 0.0)), writes=(ub[i][k],))
        for grp in range(3):
            g0 = grp * 1024
            for k in range(3):
                S.dma("sp", (lambda e, k=k, g0=g0: e.dma_start(out=wB[:, k, :], in_=cw_ap[k, g0:g0 + 1024].partition_broadcast(128))),
                      writes=(wb,))
            S.dma("sp", (lambda e, g0=g0: e.dma_start(out=bB[:], in_=cb_ap[g0:g0 + 1024].partition_broadcast(128))), writes=(bb,))
            for t in range(NT):
                i = t % 2
                r0 = t * 128
                c0 = O_CU + g0
                if t == 0:
                    S.dma("sp", (lambda e, i=i, c0=c0: e.dma_start(out=u[1:128, i, 0, :], in_=proj[0:127, c0:c0 + 1024])), writes=(ub[i][0],))
                else:
                    S.dma("sp", (lambda e, i=i, c0=c0, r0=r0: e.dma_start(out=u[:, i, 0, :], in_=proj[r0 - 1:r0 + 127, c0:c0 + 1024])),
                          writes=(ub[i][0],))
                S.dma("sp", (lambda e, i=i, c0=c0, r0=r0: e.dma_start(out=u[:, i, 1, :], in_=proj[r0:r0 + 128, c0:c0 + 1024])), writes=(ub[i][1],))
                if t == NT - 1:
                    S.dma("sp", (lambda e, i=i, c0=c0, r0=r0: e.dma_start(out=u[0:127, i, 2, :], in_=proj[r0 + 1:r0 + 128, c0:c0 + 1024])),
                          writes=(ub[i][2],))
                else:
                    S.dma("sp", (lambda e, i=i, c0=c0, r0=r0: e.dma_start(out=u[:, i, 2, :], in_=proj[r0 + 1:r0 + 129, c0:c0 + 1024])),
                          writes=(ub[i][2],))
                S.op("dve", (lambda e, i=i, t=t: e.scalar_tensor_tensor(out=acc[:, i, 0, :], in0=u[:, i, 0, :], scalar=mp[:, 0, t:t + 1],
                                                                          in1=wB[:, 0, :], op0=ALU.mult, op1=ALU.mult)),
                     reads=(ub[i][0], mpb, wb), writes=(accb[i][0],))
                S.op("pool", (lambda e, i=i: e.tensor_tensor(out=acc[:, i, 1, :], in0=u[:, i, 1, :], in1=wB[:, 1, :], op=ALU.mult)),
                     reads=(ub[i][1], wb), writes=(accb[i][1],))
                S.op("dve", (lambda e, i=i, t=t: e.scalar_tensor_tensor(out=acc[:, i, 2, :], in0=u[:, i, 2, :], scalar=mp[:, 1, t:t + 1],
                                                                          in1=wB[:, 2, :], op0=ALU.mult, op1=ALU.mult)),
                     reads=(ub[i][2], mpb, wb), writes=(accb[i][2],))
                S.op("pool", (lambda e, i=i: e.tensor_tensor(out=acc[:, i, 1, :], in0=acc[:, i, 1, :], in1=bB[:], op=ALU.add)),
                     reads=(accb[i][1], bb), writes=(accb[i][1],))
                S.op("dve", (lambda e, i=i: e.tensor_tensor(out=acc[:, i, 0, :], in0=acc[:, i, 0, :], in1=acc[:, i, 2, :], op=ALU.add)),
                     reads=(accb[i][0], accb[i][2]), writes=(accb[i][0],))
                S.op("dve", (lambda e, i=i: e.tensor_tensor(out=acc[:, i, 0, :], in0=acc[:, i, 0, :], in1=acc[:, i, 1, :], op=ALU.add)),
                     reads=(accb[i][0], accb[i][1]), writes=(accb[i][0],))
                if grp == 0:
                    S.op("act", (lambda e, i=i: e.activation(out=vb[:, i, :], in_=acc[:, i, 0, :], func=AF.Copy)),
                         reads=(accb[i][0],), writes=(vbb[i],))
                    S.dma("pool", (lambda e, i=i, r0=r0: e.dma_start(out=vtok[r0:r0 + 128, :], in_=vb[:, i, :])), reads=(vbb[i],))
                si = (t // 4) % 2
                tt = t % 4
                for gq in range(2):
                    bank = 4 + (t % 2) * 2 + gq

                    def tr(e, i=i, gq=gq, bank=bank):
                        ins = None
                        for q in range(4):
                            cc = gq * 4 + q
                            ins = e.transpose(out=C.ps[:, bank, q * 128:(q + 1) * 128],
                                              in_=acc[:, i, 0, cc * 128:(cc + 1) * 128], identity=C.ident[:])
                        return ins
                    S.op("pe", tr, reads=(accb[i][0], C.identb), writes=(C.psb[bank],))
                    copy_op(S, C.evac_eng(), stg[:, si, gq * 4:gq * 4 + 4, tt * 128:(tt + 1) * 128],
                            C.ps[:, bank, :].rearrange("p (a t) -> p a t", t=128), reads=(C.psb[bank],), writes=(stgb[si],))
                if tt == 3:
                    t0 = (t // 4) * 512
                    S.dma("pool", (lambda e, si=si, t0=t0, g0=g0: e.dma_start(
                        out=ucT[g0:g0 + 1024, t0:t0 + 512].rearrange("(a p) t -> p a t", p=128), in_=stg[:, si, :, :])), reads=(stgb[si],))
        S.run_phase()


def hy_filter_phase(C, hb, zT_ap, r_ap, w1, b1, f1, w2, b2, f2, w3, decay):
    nc, S = C.nc, C.S
    with contextlib.ExitStack() as es:
        T = lambda n, s, d: es.enter_context(C.sbuf_tensor(n, s, d))
        zt = T("f_z", [33, 2, 512], F32)
        W1 = T("f_w1", [33, 64], F32)
        W2 = T("f_w2", [64, 64], F32)
        W3 = T("f_w3", [64, 2048], F32)
        pr = T("f_pr", [64, 8], F32)
        dB = T("f_dB", [128, 2048], F32)
        rr = T("f_r", [128, TOK // 128], F32)
        nr = T("f_nr", [128, TOK // 128], F32)
        xa = T("f_xa", [64, 6, 512], F32)
        h2 = T("f_h2", [64, 2, 512], F32)
        E = T("f_E", [128, 2, 512], F32)
        ho = T("f_ho", [128, 2, 512], BF16)
        B = Buf
        ztb, w1b, w2b, w3b, prb, dBb, rrb = [B(), B()], B(), B(), B(), B(), B(), B()
        xab, h2b, Eb, hob = B(), [B(), B()], [B(), B()], [B(), B()]
        col = lambda ap: ap.rearrange("(p o) -> p o", o=1)
        S.dma("sp", lambda e: e.dma_start(out=W1[:], in_=w1), writes=(w1b,))
        S.dma("sp", lambda e: e.dma_start(out=W2[:], in_=w2), writes=(w2b,))
        S.dma("sp", lambda e: e.dma_start(out=W3[:], in_=w3), writes=(w3b,))
        for k, ap in enumerate((b1, f1, b2, f2)):
            S.dma("sp", (lambda e, k=k, ap=ap: e.dma_start(out=pr[:, k:k + 1], in_=col(ap))), writes=(prb,))
        S.dma("sp", lambda e: e.dma_start(out=dB[:], in_=decay.partition_broadcast(128)), writes=(dBb,))
        S.dma("sp", lambda e: e.dma_start(out=rr[:], in_=r_ap), writes=(rrb,))
        S.op("dve", lambda e: e.tensor_tensor(out=pr[:, 4:5], in0=pr[:, 0:1], in1=pr[:, 1:2], op=ALU.mult), reads=(prb,), writes=(prb,))
        S.op("dve", lambda e: e.tensor_tensor(out=pr[:, 5:6], in0=pr[:, 2:3], in1=pr[:, 3:4], op=ALU.mult), reads=(prb,), writes=(prb,))
        S.op("dve", lambda e: e.tensor_scalar(out=nr[:], in0=rr[:], scalar1=-1.0, scalar2=None, op0=ALU.mult), reads=(rrb,), writes=(rrb,))

        def sin_layer(bank, fcol, fbcol, out_ap, out_buf):
            S.op("act", (lambda e: e.activation(out=xa[:, 0, :], in_=C.ps[0:64, bank, :], func=AF.Identity,
                                                scale=pr[:, fcol:fcol + 1], bias=pr[:, fbcol:fbcol + 1])),
                 reads=(C.psb[bank], prb), writes=(xab,))
            S.op("act", lambda e: e.activation(out=xa[:, 1, :], in_=xa[:, 0, :], func=AF.Sin, scale=0.5), reads=(xab,), writes=(xab,))
            S.op("act", lambda e: e.activation(out=xa[:, 2, :], in_=xa[:, 0, :], func=AF.Sin, scale=0.25), reads=(xab,), writes=(xab,))
            S.op("dve", lambda e: e.tensor_tensor(out=xa[:, 3, :], in0=xa[:, 2, :], in1=xa[:, 2, :], op=ALU.mult), reads=(xab,), writes=(xab,))
            S.op("dve", lambda e: e.tensor_scalar(out=xa[:, 4, :], in0=xa[:, 3, :], scalar1=-2.0, scalar2=1.0, op0=ALU.mult, op1=ALU.add),
                 reads=(xab,), writes=(xab,))
            S.op("dve", lambda e: e.scalar_tensor_tensor(out=out_ap, in0=xa[:, 1, :], scalar=2.0, in1=xa[:, 4, :], op0=ALU.mult, op1=ALU.mult),
                 reads=(xab,), writes=(xab, out_buf))
        cnt = 0
        for nb in range(TOK // 512):
            i = nb % 2
            S.dma("sp", (lambda e, i=i, nb=nb: e.dma_start(out=zt[:, i, :], in_=zT_ap[:, nb * 512:(nb + 1) * 512])), writes=(ztb[i],))
            S.op("pe", (lambda e, i=i: e.matmul(C.ps[0:64, 0, :], lhsT=W1[:], rhs=zt[:, i, :], start=True, stop=True)),
                 reads=(w1b, ztb[i]), writes=(C.psb[0],))
            sin_layer(0, 1, 4, xa[:, 5, :], xab)
            S.op("pe", lambda e: e.matmul(C.ps[0:64, 1, :], lhsT=W2[:], rhs=xa[:, 5, :], start=True, stop=True),
                 reads=(w2b, xab), writes=(C.psb[1],))
            sin_layer(1, 3, 5, h2[:, i, :], h2b[i])
            for nt in range(4):
                tile_n = nb * 4 + nt
                for cb in range(4):
                    k = cnt % 2
                    bank = 2 + cnt % 4
                    cnt += 1
                    S.op("pe", (lambda e, i=i, nt=nt, cb=cb, bank=bank: e.matmul(
                        C.ps[:, bank, :], lhsT=h2[:, i, nt * 128:(nt + 1) * 128], rhs=W3[:, cb * 512:(cb + 1) * 512], start=True, stop=True)),
                        reads=(h2b[i], w3b), writes=(C.psb[bank],))
                    S.op("act", (lambda e, k=k, cb=cb, tile_n=tile_n: e.activation(
                        out=E[:, k, :], in_=dB[:, cb * 512:(cb + 1) * 512], func=AF.Exp, scale=nr[:, tile_n:tile_n + 1])),
                        reads=(dBb, rrb), writes=(Eb[k],))
                    S.op("dve", (lambda e, k=k, bank=bank: e.tensor_tensor(out=ho[:, k, :], in0=C.ps[:, bank, :], in1=E[:, k, :], op=ALU.mult)),
                         reads=(C.psb[bank], Eb[k]), writes=(hob[k],))
                    S.dma("pool", (lambda e, k=k, tile_n=tile_n, cb=cb: e.dma_start(
                        out=hb[tile_n * 128:(tile_n + 1) * 128, cb * 512:(cb + 1) * 512], in_=ho[:, k, :])), reads=(hob[k],))
        S.run_phase()


def storeF_epi(C, es, dst_ap, name="sf"):
    nc, S = C.nc, C.S
    ot = es.enter_context(C.sbuf_tensor(name + "_o", [128, 4, 512], F32))
    ob = [Buf() for _ in range(4)]
    cnt = [0]

    def mk(row0):
        def epi(C, tb, j, bank, width):
            i = cnt[0] % 4
            cnt[0] += 1
            copy_op(S, C.evac_eng(), ot[:, i, :], C.ps[:, bank, :], reads=(C.psb[bank],), writes=(ob[i],))
            r = row0 + j * 128
            S.dma("pool", (lambda e, i=i, r=r, tb=tb: e.dma_start(out=dst_ap[r:r + 128, tb * 512:(tb + 1) * 512], in_=ot[:, i, :])),
                  reads=(ob[i],))
        return epi
    return mk


def cmul_epi(C, es, Hc, Hs, hc0, YY, name="cm"):
    nc, S = C.nc, C.S
    T = lambda n, s, d: es.enter_context(C.sbuf_tensor(name + n, s, d))
    hh = T("_h", [128, 2, 2, 512], F32)
    xx = T("_x", [128, 2, 2, 512], F32)
    tt = T("_t", [128, 2, 4, 512], F32)
    yy = T("_y", [128, 2, 2, 512], BF16)
    hb_, xb, tb_, yb = [Buf(), Buf()], [Buf(), Buf()], [[Buf() for _ in range(4)] for _ in range(2)], [[Buf(), Buf()], [Buf(), Buf()]]
    cnt = [0]

    def mk(f0):
        def epi(C, tb, j, bank, width):
            if j < 2:
                return
            jj = j - 2
            i = cnt[0] % 2
            cnt[0] += 1
            fr = f0 + jj * 128
            cs = slice(hc0 + tb * 512, hc0 + (tb + 1) * 512)
            S.dma("sp", (lambda e, i=i, fr=fr, cs=cs: e.dma_start(out=hh[:, i, 0, :], in_=Hc[fr:fr + 128, cs])), writes=(hb_[i],))
            S.dma("sp", (lambda e, i=i, fr=fr, cs=cs: e.dma_start(out=hh[:, i, 1, :], in_=Hs[fr:fr + 128, cs])), writes=(hb_[i],))
            S.op("act", (lambda e, i=i, b=bank - 2: e.activation(out=xx[:, i, 0, :], in_=C.ps[:, b, :], func=AF.Copy)),
                 reads=(C.psb[bank - 2],), writes=(xb[i],))
            S.op("act", (lambda e, i=i, b=bank: e.activation(out=xx[:, i, 1, :], in_=C.ps[:, b, :], func=AF.Copy)),
                 reads=(C.psb[bank],), writes=(xb[i],))
            rd = (xb[i], hb_[i])
            S.op("dve", (lambda e, i=i: e.tensor_tensor(out=tt[:, i, 0, :], in0=xx[:, i, 0, :], in1=hh[:, i, 0, :], op=ALU.mult)), reads=rd, writes=(tb_[i][0],))
            S.op("pool", (lambda e, i=i: e.tensor_tensor(out=tt[:, i, 1, :], in0=xx[:, i, 1, :], in1=hh[:, i, 1, :], op=ALU.mult)), reads=rd, writes=(tb_[i][1],))
            S.op("dve", (lambda e, i=i: e.tensor_tensor(out=tt[:, i, 2, :], in0=xx[:, i, 0, :], in1=hh[:, i, 1, :], op=ALU.mult)), reads=rd, writes=(tb_[i][2],))
            S.op("pool", (lambda e, i=i: e.tensor_tensor(out=tt[:, i, 3, :], in0=xx[:, i, 1, :], in1=hh[:, i, 0, :], op=ALU.mult)), reads=rd, writes=(tb_[i][3],))
            S.op("dve", (lambda e, i=i: e.tensor_tensor(out=yy[:, i, 0, :], in0=tt[:, i, 0, :], in1=tt[:, i, 1, :], op=ALU.subtract)),
                 reads=(tb_[i][0], tb_[i][1]), writes=(yb[i][0],))
            S.op("pool", (lambda e, i=i: e.tensor_tensor(out=yy[:, i, 1, :], in0=tt[:, i, 2, :], in1=tt[:, i, 3, :], op=ALU.add)),
                 reads=(tb_[i][2], tb_[i][3]), writes=(yb[i][1],))
            S.dma("pool", (lambda e, i=i, fr=fr, tb=tb: e.dma_start(out=YY[fr:fr + 128, tb * 512:(tb + 1) * 512], in_=yy[:, i, 0, :])),
                  reads=(yb[i][0],))
            S.dma("pool", (lambda e, i=i, fr=fr, tb=tb: e.dma_start(out=YY[NF + fr:NF + fr + 128, tb * 512:(tb + 1) * 512], in_=yy[:, i, 1, :])),
                  reads=(yb[i][1],))
        return epi
    return mk


def gate_epi(C, es, order, ucT, z1T, z1tok, yT, skip_ap, name="ge"):
    nc, S = C.nc, C.S
    T = lambda n, s, d: es.enter_context(C.sbuf_tensor(name + n, s, d))
    ys = T("_ys", [128, 2, 512], F32)
    zp = T("_zp", [128, 2, 512], F32)
    gt = T("_gt", [128, 2, 512], F32)
    zo = T("_zo", [128, 2, 512], BF16)
    zt = T("_zt", [128, 2, 512], BF16)
    sk = T("_sk", [128, 8], F32)
    ysb, zpb, gtb, zob, ztb = [Buf(), Buf()], [Buf(), Buf()], [Buf(), Buf()], [Buf(), Buf()], [Buf(), Buf()]
    skb = Buf()
    S.dma("sp", lambda e: e.dma_start(out=sk[:], in_=skip_ap[order * 1024:(order + 1) * 1024].rearrange("(a p) -> p a", p=128),
                                      allow_slow_non_contiguous=True), writes=(skb,))
    zprev = ucT if order == 0 else z1T
    cnt = [0]

    def mk(t0):
        def epi(C, tb, j, bank, width):
            i = cnt[0] % 2
            cnt[0] += 1
            cc = tb * 4 + j
            c0 = cc * 128
            g0 = 1024 * (order + 1) + c0
            S.dma("sp", (lambda e, i=i, c0=c0: e.dma_start(out=zp[:, i, :], in_=zprev[c0:c0 + 128, t0:t0 + 512])), writes=(zpb[i],))
            S.dma("sp", (lambda e, i=i, g0=g0: e.dma_start(out=gt[:, i, :], in_=ucT[g0:g0 + 128, t0:t0 + 512])), writes=(gtb[i],))
            S.op("act", (lambda e, i=i, bank=bank: e.activation(out=ys[:, i, :], in_=C.ps[:, bank, :], func=AF.Copy, scale=2.0 / NDFT)),
                 reads=(C.psb[bank],), writes=(ysb[i],))
            S.op("dve", (lambda e, i=i, cc=cc: e.scalar_tensor_tensor(out=ys[:, i, :], in0=zp[:, i, :], scalar=sk[:, cc:cc + 1], in1=ys[:, i, :],
                                                                       op0=ALU.mult, op1=ALU.add)),
                 reads=(zpb[i], skb, ysb[i]), writes=(ysb[i],))
            if order == 0:
                S.op("dve", (lambda e, i=i: e.tensor_tensor(out=zp[:, i, :], in0=ys[:, i, :], in1=gt[:, i, :], op=ALU.mult)),
                     reads=(ysb[i], gtb[i]), writes=(zpb[i],))
                S.dma("pool", (lambda e, i=i, c0=c0: e.dma_start(out=z1T[c0:c0 + 128, t0:t0 + 512], in_=zp[:, i, :])), reads=(zpb[i],))

                def tr(e, i=i, bank=bank):
                    ins = None
                    for q in range(4):
                        ins = e.transpose(out=C.ps[:, bank, q * 128:(q + 1) * 128], in_=zp[:, i, q * 128:(q + 1) * 128], identity=C.ident[:])
                    return ins
                S.op("pe", tr, reads=(zpb[i], C.identb), writes=(C.psb[bank],))
                S.op("act", (lambda e, i=i, bank=bank: e.activation(out=zt[:, i, :], in_=C.ps[:, bank, :], func=AF.Copy)),
                     reads=(C.psb[bank],), writes=(ztb[i],))
                S.dma("pool", (lambda e, i=i, c0=c0: e.dma_start(
                    out=z1tok[t0:t0 + 512, c0:c0 + 128].rearrange("(q p) c -> p q c", p=128),
                    in_=zt[:, i, :].rearrange("p (q c) -> p q c", c=128))), reads=(ztb[i],))
            else:
                S.op("dve", (lambda e, i=i: e.tensor_tensor(out=zo[:, i, :], in0=ys[:, i, :], in1=gt[:, i, :], op=ALU.mult)),
                     reads=(ysb[i], gtb[i]), writes=(zob[i],))
                S.dma("pool", (lambda e, i=i, c0=c0: e.dma_start(out=yT[2048 + c0:2048 + c0 + 128, t0:t0 + 512], in_=zo[:, i, :])),
                      reads=(zob[i],))
        return epi
    return mk


def mixer_c(C, proj, yT, sc, hp, tabs):
    hy_conv_phase(C, proj, sc["ucT"], sc["vtok"], hp["conv_w"], hp["conv_b"], tabs["mprev"], tabs["mnext"])
    hy_filter_phase(C, sc["hb"], tabs["zT"], tabs["hr"], hp["w1"], hp["b1"], hp["f1"], hp["w2"], hp["b2"], hp["f2"], hp["w3"], hp["decay"])
    with contextlib.ExitStack() as es:
        mkc = storeF_epi(C, es, sc["Hc"], "hfc")
        mks = storeF_epi(C, es, sc["Hs"], "hfs")
        blocks = [dict(cols=[(tabs["Eh"], f0, 512)], orient="F", epi=mkc(f0)) for f0 in range(0, NF, 512)]
        blocks += [dict(cols=[(tabs["Eh"], NF + f0, 512)], orient="F", epi=mks(f0)) for f0 in range(0, NF, 512)]
        gemm_phase(C, TOK=2048, K=TOK, blocks=blocks, a_loader=dram_loader(C, sc["hb"], TOK), nbufA=2)
    for order in range(2):
        src = sc["vtok"] if order == 0 else sc["z1tok"]
        with contextlib.ExitStack() as es:
            mk = cmul_epi(C, es, sc["Hc"], sc["Hs"], order * 1024, sc["YY"], "cm%d" % order)
            blocks = [dict(cols=[(tabs["Eu"], f0, 256), (tabs["Eu"], NF + f0, 256)], orient="F", epi=mk(f0)) for f0 in range(0, NF, 256)]
            gemm_phase(C, TOK=1024, K=TOK, blocks=blocks, a_loader=dram_loader(C, src, TOK), nbufA=2)
        with contextlib.ExitStack() as es:
            mk = gate_epi(C, es, order, sc["ucT"], sc["z1T"], sc["z1tok"], yT, hp["skip"], "ge%d" % order)
            blocks = [dict(cols=[(tabs["Ei"], t0, 512)], orient="T", epi=mk(t0)) for t0 in range(0, TOK, 512)]
            gemm_phase(C, TOK=1024, K=2 * NF, blocks=blocks, a_loader=dram_loader(C, sc["YY"], 2 * NF), nbufA=1)


def c_consts(is_prompt):
    L = 2048 if is_prompt else 4096
    NT = TOK // 128
    nseg = TOK // L
    pos_u = np.concatenate([np.arange(L) + s * 3072 for s in range(nseg)]).astype(np.int64)
    k2 = (2 * np.arange(NF, dtype=np.int64) + 1)

    def table(pos):
        m = (pos[:, None] * k2[None, :]) % (2 * NDFT)
        ang = m.astype(np.float64) * (np.pi / NDFT)
        return np.cos(ang), np.sin(ang)
    cu, su = table(pos_u)
    Eu = np.concatenate([cu, su], 1).astype(ml_dtypes.bfloat16)
    Ei = np.ascontiguousarray(np.concatenate([cu, su], 1).T).astype(ml_dtypes.bfloat16)
    pos_h = np.arange(TOK, dtype=np.int64) - L // 2
    ch, sh = table(pos_h)
    Eh = np.concatenate([ch, sh], 1)
    Eh[L:, :] = 0.0
    Eh = Eh.astype(ml_dtypes.bfloat16)
    n = np.arange(L, dtype=np.float32)
    t = n / np.float32(L - 1)
    f = np.linspace(1e-4, 15, 16, dtype=np.float32)
    ang = (np.float32(2.0 * math.pi / L)) * n[:, None] * f[None, :]
    z = np.concatenate([t[:, None], np.cos(ang), -np.sin(ang)], -1).astype(np.float32)
    zT = np.zeros((33, TOK), np.float32)
    zT[:, :L] = z.T
    r = np.zeros(TOK, np.float32)
    r[:L] = np.abs(n - L // 2) / np.float32(L // 2)
    hr = np.ascontiguousarray(r.reshape(NT, 128).T)
    mprev = np.ones((128, NT), np.float32)
    mnext = np.ones((128, NT), np.float32)
    for s in range(nseg):
        mprev[0, s * L // 128] = 0.0
        mnext[127, (s + 1) * L // 128 - 1] = 0.0
    return {"Eu": Eu, "Ei": Ei, "Eh": Eh, "zT": zT, "hr": hr, "mprev": mprev, "mnext": mnext}


WNAMES = ("w_in", "w_out", "w_gate", "w_up", "w_down")
WSHAPES = {"w_in": (D_MODEL, N_IN), "w_out": (D_MODEL, D_MODEL), "w_gate": (D_MODEL, FF),
           "w_up": (D_MODEL, FF), "w_down": (FF, D_MODEL)}
SMALL = {"norm1_g": (2, D_MODEL), "qk_norm_g": (2, 2, 128), "norm2_g": (2, D_MODEL), "final_g": (D_MODEL,),
         "hy_conv_w": (2, 3, 3072), "hy_conv_b": (2, 3072), "hy_w1": (2, 33, 64), "hy_b1": (2, 64), "hy_freq1": (2, 64),
         "hy_w2": (2, 64, 64), "hy_b2": (2, 64), "hy_freq2": (2, 64), "hy_w3": (2, 64, 2048), "hy_decay": (2, 2048),
         "hy_skip": (2, 2048), "ml_gate_b": (2, 4, 4), "ml_norm_g": (2, 1024)}
A_NU = a_units()[1]
CONST_SPECS = {"ident": ((128, 128), "f"), "ropeB": ((TOK, 128), "f"), "segb": ((128, 4), "f"),
               "ropeA": ((TOK, 32), "f"), "abias": ((128, A_NU), "f"), "amask": ((128, 256), "f"),
               "tri": ((128, 384), "f"), "carry": ((128, TOK // 128, 8), "f"),
               "Eu": ((TOK, 2 * NF), "b"), "Ei": ((2 * NF, TOK), "b"), "Eh": ((TOK, 2 * NF), "b"),
               "zT": ((33, TOK), "f"), "hr": ((128, TOK // 128), "f"), "mprev": ((128, TOK // 128), "f"),
               "mnext": ((128, TOK // 128), "f")}
DEPTH = 2


def zero_rows(C, yT, r0, r1):
    nc, S = C.nc, C.S
    with C.sbuf_tensor("zr", [128, TOK], BF16) as z:
        zb = Buf()
        S.op("pool", lambda e: e.memset(z[:], 0.0), writes=(zb,))
        for r in range(r0, r1, 128):
            S.dma("sp", (lambda e, r=r: e.dma_start(out=yT[r:r + 128, :], in_=z[:])), reads=(zb,))
        S.run_phase()


def build_program():
    nc = bass.Bass("TRN2", target_bir_lowering=False)
    dt = lambda n, s, d, k: nc.dram_tensor(n, list(s), d, kind=k).ap()
    x_in = dt("x", (TOK, D_MODEL), F32, "ExternalInput")
    y_out = dt("y", (TOK, D_MODEL), F32, "ExternalOutput")
    W = {n: dt(n, (DEPTH,) + WSHAPES[n], F32, "ExternalInput") for n in WNAMES}
    P = {n: dt(n, s, F32, "ExternalInput") for n, s in SMALL.items()}
    CT = {n: dt(n, sh, F32 if k == "f" else BF16, "ExternalInput") for n, (sh, k) in CONST_SPECS.items()}
    ident, ropeB, segb = CT["ident"], CT["ropeB"], CT["segb"]
    Wb = {n: [dt("%s_bf%d" % (n, l), WSHAPES[n], BF16, "Internal") for l in range(DEPTH)] for n in WNAMES}
    proj = dt("proj", (TOK, N_IN), F32, "Internal")
    yT = dt("yT", (D_MODEL, TOK), BF16, "Internal")
    aT = dt("aT", (FF, TOK), BF16, "Internal")
    XA = dt("XA", (TOK, D_MODEL), F32, "Internal")
    XB = dt("XB", (TOK, D_MODEL), F32, "Internal")
    aqT = dt("aqT", (8, 128, TOK), BF16, "Internal")
    akT = dt("akT", (8, 128, TOK), BF16, "Internal")
    avb = dt("avb", (TOK, 1024), BF16, "Internal")
    dqT = dt("dqT", (8, 128, TOK), BF16, "Internal")
    dkT = dt("dkT", (8, 128, TOK), BF16, "Internal")
    dkb = dt("dkb", (TOK, 1024), BF16, "Internal")
    dvb = dt("dvb", (TOK, 1024), BF16, "Internal")
    dgt = dt("dgt", (128, TOK // 128, 24), F32, "Internal")
    sc = {"ucT": dt("ucT", (3072, TOK), F32, "Internal"), "vtok": dt("vtok", (TOK, 1024), BF16, "Internal"),
          "hb": dt("hb", (TOK, 2048), BF16, "Internal"), "Hc": dt("Hc", (NF, 2048), F32, "Internal"),
          "Hs": dt("Hs", (NF, 2048), F32, "Internal"), "YY": dt("YY", (2 * NF, 1024), BF16, "Internal"),
          "z1T": dt("z1T", (1024, TOK), F32, "Internal"), "z1tok": dt("z1tok", (TOK, 1024), BF16, "Internal")}

    C = Ctx(nc)
    C.load_consts(ident)
    cast_weights(C, [(W[n][l], Wb[n][l]) for l in range(DEPTH) for n in WNAMES])
    for l in range(DEPTH):
        xl = x_in if l == 0 else XB
        with contextlib.ExitStack() as es:
            ld = rmsnorm_loader(C, es, xl, P["norm1_g"][l], D_MODEL, "n1_%d" % l)
            mk = store_epi(C, es, proj, None, F32, "pe%d" % l)
            blocks = []
            for c0 in range(0, N_IN, 512):
                wd = min(512, N_IN - c0)
                blocks.append(dict(cols=[(Wb["w_in"][l], c0, wd)], orient="T", epi=mk(c0)))
            gemm_phase(C, TOK=TOK, K=D_MODEL, blocks=blocks, a_loader=ld, nbufA=2)
        mixer_a_prep(C, proj, aqT, akT, avb, CT["ropeA"])
        mixer_a(C, aqT, akT, avb, yT, CT["abias"], CT["amask"])
        mixer_b(C, proj, yT, ropeB, segb, P["qk_norm_g"][l])
        hp = {"conv_w": P["hy_conv_w"][l], "conv_b": P["hy_conv_b"][l], "w1": P["hy_w1"][l], "b1": P["hy_b1"][l],
              "f1": P["hy_freq1"][l], "w2": P["hy_w2"][l], "b2": P["hy_b2"][l], "f2": P["hy_freq2"][l],
              "w3": P["hy_w3"][l], "decay": P["hy_decay"][l], "skip": P["hy_skip"][l]}
        mixer_c(C, proj, yT, sc, hp, CT)
        mixer_d_prep(C, proj, dqT, dkT, dkb, dvb, dgt, P["ml_gate_b"][l], CT["carry"], CT["tri"])
        mixer_d(C, proj, dqT, dkT, dkb, dvb, dgt, yT, P["ml_norm_g"][l], CT["tri"])
        with contextlib.ExitStack() as es:
            mk = resid_epi(C, es, xl, XA, "ro%d" % l)
            blocks = [dict(cols=[(Wb["w_out"][l], c0, 512)], orient="T", epi=mk(c0)) for c0 in range(0, D_MODEL, 512)]
            gemm_phase(C, TOK=TOK, K=D_MODEL, blocks=blocks, a_loader=dram_loader(C, yT, D_MODEL), nbufA=2)
        with contextlib.ExitStack() as es:
            ld = rmsnorm_loader(C, es, XA, P["norm2_g"][l], D_MODEL, "n2_%d" % l)
            mk = swiglu_epi(C, es, aT, "sg%d" % l)
            blocks = [dict(cols=[(Wb["w_gate"][l], c0, 256), (Wb["w_up"][l], c0, 256)], orient="F", epi=mk(c0))
                      for c0 in range(0, FF, 256)]
            gemm_phase(C, TOK=TOK, K=D_MODEL, blocks=blocks, a_loader=ld, nbufA=2)
        with contextlib.ExitStack() as es:
            mk = resid_epi(C, es, XA, XB, "rd%d" % l)
            blocks = [dict(cols=[(Wb["w_down"][l], c0, 512)], orient="T", epi=mk(c0)) for c0 in range(0, D_MODEL, 512)]
            gemm_phase(C, TOK=TOK, K=FF, blocks=blocks, a_loader=dram_loader(C, aT, FF), nbufA=1)
    final_norm(C, XB, P["final_g"], y_out, D_MODEL)
    return nc


def _rope_tab_b(L):
    pos = np.arange(L)
    row = (pos // 64).astype(np.float32)
    col = (pos % 64).astype(np.float32)
    inv = (np.float32(10000.0) ** (-np.arange(0, 64, 2, dtype=np.float32) / np.float32(64))).astype(np.float32)
    ar = row[:, None] * inv[None]
    ac = col[:, None] * inv[None]
    return np.concatenate([np.cos(ar), np.sin(ar), np.cos(ac), np.sin(ac)], -1).astype(np.float32)


def _core_consts(is_prompt):
    L = 2048 if is_prompt else 4096
    segb = np.zeros((128, 4), np.float32)
    if is_prompt:
        segb[:, 1] = -30000.0
        segb[:, 2] = -30000.0
    out = {"ident": np.eye(128, dtype=np.float32),
           "ropeB": np.concatenate([_rope_tab_b(L)] * (TOK // L), 0),
           "segb": segb}
    out.update(a_consts(is_prompt))
    out.update(d_consts(is_prompt))
    out.update(c_consts(is_prompt))
    return out


def kernel(**inputs):
    xp = np.asarray(inputs["x_prompt"], np.float32)
    xs = np.asarray(inputs["x_sample"], np.float32)
    nc = build_program()
    shared = {n: np.ascontiguousarray(np.asarray(inputs[n], np.float32)) for n in list(WNAMES) + list(SMALL)}
    in_maps = []
    cp, cs_ = _core_consts(True), _core_consts(False)
    for c in range(8):
        m = dict(shared)
        if c < 4:
            m["x"] = np.ascontiguousarray(xp[2 * c:2 * c + 2].reshape(TOK, D_MODEL))
        else:
            m["x"] = np.ascontiguousarray(xs[c - 4].reshape(TOK, D_MODEL))
        m.update(cp if c < 4 else cs_)
        in_maps.append(m)
    res = run_bass_kernel_spmd(nc, in_maps, core_ids=list(range(8)))
    ys = [np.asarray(r["y"], np.float32) for r in res.results]
    y_prompt = np.stack([ys[c].reshape(2, 2048, D_MODEL) for c in range(4)], 0).reshape(8, 2048, D_MODEL)
    y_sample = np.stack([ys[c] for c in range(4, 8)], 0)
    return (y_prompt, y_sample)
```

```python
import contextlib
import math
import numpy as np
import ml_dtypes
import concourse.bass as bass
import concourse.mybir as mybir
from concourse.bass_utils import run_bass_kernel_spmd

F32 = mybir.dt.float32
BF16 = mybir.dt.bfloat16
AF = mybir.ActivationFunctionType
ALU = mybir.AluOpType
AX = mybir.AxisListType
EPS = 1e-6


class Buf:
    __slots__ = ("name", "w", "r")

    def __init__(self, name=""):
        self.name = name
        self.w = None
        self.r = {}


class Sched:
    COMPUTE = ("pe", "act", "dve", "pool")
    NDMA = {"sp": 10, "pool": 6, "poolc": 4}
    QSTREAM = {"sp": "sp", "pool": "pool", "poolc": "pool"}

    def __init__(self, nc):
        self.nc = nc
        self.sem = {}
        self.val = {}
        for e in self.COMPUTE:
            self.sem[e] = nc.alloc_semaphore(name="c_" + e)
            self.val[e] = 0
        self.dcnt = {}
        for q, n in self.NDMA.items():
            for i in range(n):
                k = (q, i)
                self.sem[k] = nc.alloc_semaphore(name="d_%s%d" % (q, i))
                self.val[k] = 0
            self.dcnt[q] = 0
        self.streams = {e: [] for e in ("pe", "act", "dve", "pool", "sp")}
        self.known = {e: {} for e in self.streams}
        self.nops = 0

    def _waits(self, eng, reads, writes, is_dma):
        deps = {}
        for b in reads:
            if b.w is not None:
                k, v = b.w
                if deps.get(k, 0) < v:
                    deps[k] = v
        for b in writes:
            if b.w is not None:
                k, v = b.w
                if deps.get(k, 0) < v:
                    deps[k] = v
            for k, v in b.r.items():
                if deps.get(k, 0) < v:
                    deps[k] = v
        out = []
        kn = self.known[eng]
        for k, v in deps.items():
            if k == eng and not is_dma and eng == "pe":
                continue
            if kn.get(k, 0) >= v:
                continue
            kn[k] = v
            out.append((self.sem[k], v))
        return out

    @staticmethod
    def _mark(key, v, reads, writes):
        for b in reads:
            if b.r.get(key, 0) < v:
                b.r[key] = v
        for b in writes:
            b.w = (key, v)
            b.r = {}

    def op(self, eng, fn, reads=(), writes=()):
        waits = self._waits(eng, reads, writes, False)
        self.val[eng] += 1
        v = self.val[eng]
        self._mark(eng, v, reads, writes)
        self.streams[eng].append((waits, fn, self.sem[eng], 1))
        self.nops += 1

    def dma(self, q, fn, reads=(), writes=()):
        st = self.QSTREAM[q]
        i = self.dcnt[q] % self.NDMA[q]
        self.dcnt[q] += 1
        key = (q, i)
        waits = self._waits(st, reads, writes, True)
        pv = self.val[key]
        if pv > 0 and self.known[st].get(key, 0) < pv:
            self.known[st][key] = pv
            waits.append((self.sem[key], pv))
        self.val[key] += 16
        v = self.val[key]
        self._mark(key, v, reads, writes)
        self.streams[st].append((waits, fn, self.sem[key], 16))
        self.nops += 1

    def run_phase(self):
        nc = self.nc
        for q in self.NDMA:
            st = self.QSTREAM[q]
            fin = []
            for i in range(self.NDMA[q]):
                key = (q, i)
                if self.val[key] > self.known[st].get(key, 0):
                    fin.append((self.sem[key], self.val[key]))
            if fin:
                self.streams[st].append((fin, None, None, 0))
        streams = self.streams

        def replay(e, lst):
            for waits, fn, sem, inc in lst:
                for s, v in waits:
                    e.wait_ge(s, v)
                if fn is not None:
                    ins = fn(e)
                    ins.then_inc(sem, inc)

        with nc.Block() as block:
            if streams["pe"]:
                @block.tensor
                def _(e):
                    replay(e, streams["pe"])
            if streams["act"]:
                @block.scalar
                def _(e):
                    replay(e, streams["act"])
            if streams["dve"]:
                @block.vector
                def _(e):
                    replay(e, streams["dve"])
            if streams["pool"]:
                @block.gpsimd
                def _(e):
                    replay(e, streams["pool"])
            if streams["sp"]:
                @block.sync
                def _(e):
                    replay(e, streams["sp"])
        self.streams = {e: [] for e in ("pe", "act", "dve", "pool", "sp")}
        allv = dict(self.val)
        self.known = {e: dict(allv) for e in self.streams}


class Ctx:
    def __init__(self, nc):
        self.nc = nc
        self.S = Sched(nc)
        self.ps = nc.alloc_psum_tensor("psum_all", [128, 8, 512], F32)
        self.psb = [Buf("ps%d" % i) for i in range(8)]
        self._evac = 0
        self._uid = 0
        self.bg_jobs = []
        self.ident = nc.alloc_sbuf_tensor("sb_ident", [128, 128], F32)
        self.identb = Buf("ident")
        self.identh = nc.alloc_sbuf_tensor("sb_identh", [128, 128], BF16)
        self.identhb = Buf("identh")

    def load_consts(self, ident_ap):
        S = self.S
        S.dma("sp", lambda e: e.dma_start(out=self.ident[:], in_=ident_ap), writes=(self.identb,))
        S.op("dve", lambda e: e.tensor_copy(out=self.identh[:], in_=self.ident[:]),
             reads=(self.identb,), writes=(self.identhb,))

    def sbuf_tensor(self, name, shape, dtype):
        self._uid += 1
        return self.nc.sbuf_tensor("%s_u%d" % (name, self._uid), shape, dtype)

    def evac_eng(self):
        self._evac += 1
        return "act" if (self._evac % 2) else "dve"


def copy_op(S, eng, out, in_, reads, writes, scale=None):
    if eng == "act":
        if scale is None:
            S.op("act", lambda e: e.activation(out=out, in_=in_, func=AF.Copy), reads, writes)
        else:
            S.op("act", lambda e: e.activation(out=out, in_=in_, func=AF.Copy, scale=float(scale)), reads, writes)
    else:
        if scale is None:
            S.op(eng, lambda e: e.tensor_copy(out=out, in_=in_), reads, writes)
        else:
            S.op(eng, lambda e: e.tensor_scalar(out=out, in0=in_, scalar1=float(scale), scalar2=None,
                                                op0=ALU.mult), reads, writes)


PK = 16


def gemm_phase(C, *, TOK, K, blocks, a_loader, nbufA=1, nslot=3):
    nc, S = C.nc, C.S
    KC = K // 128
    npiece = (KC + PK - 1) // PK
    NTB = TOK // 512
    with contextlib.ExitStack() as es:
        At = es.enter_context(C.sbuf_tensor("gA", [128, nbufA, KC, 512], BF16))
        Wt = es.enter_context(C.sbuf_tensor("gW", [128, nslot, PK, 512], BF16))
        abufs = [[Buf("A%d_%d" % (i, p)) for p in range(npiece)] for i in range(nbufA)]
        wbufs = [Buf("W%d" % i) for i in range(nslot)]
        wcnt = 0
        bcnt = 0
        for tb in range(NTB):
            ai = tb % nbufA
            a_loader(C, tb, At[:, ai], abufs[ai])
            for blk in blocks:
                width = sum(c[2] for c in blk["cols"])
                orient = blk["orient"]
                nsub = 4 if orient == "T" else (width + 127) // 128
                pset = (bcnt % 2) * 4
                bcnt += 1
                if C.bg_jobs:
                    C.bg_jobs.pop(0)()
                for p in range(npiece):
                    k0 = p * PK
                    k1 = min(KC, k0 + PK)
                    slot = wcnt % nslot
                    wcnt += 1
                    off = 0
                    for (wap, c0, wd) in blk["cols"]:
                        src = wap[k0 * 128:k1 * 128, c0:c0 + wd].rearrange("(c p) n -> p c n", p=128)
                        dst = Wt[:, slot, 0:k1 - k0, off:off + wd]
                        S.dma("sp", (lambda e, dst=dst, src=src: e.dma_start(out=dst, in_=src)),
                              reads=(), writes=(wbufs[slot],))
                        off += wd

                    def mm(e, k0=k0, k1=k1, slot=slot, ai=ai, orient=orient, nsub=nsub, pset=pset, width=width):
                        ins = None
                        for kc in range(k0, k1):
                            for j in range(nsub):
                                if orient == "T":
                                    ins = e.matmul(C.ps[:, pset + j, 0:width],
                                                   lhsT=At[:, ai, kc, j * 128:(j + 1) * 128],
                                                   rhs=Wt[:, slot, kc - k0, 0:width],
                                                   start=(kc == 0), stop=(kc == KC - 1))
                                else:
                                    cw = min(128, width - j * 128)
                                    ins = e.matmul(C.ps[0:cw, pset + j, 0:512],
                                                   lhsT=Wt[:, slot, kc - k0, j * 128:j * 128 + cw],
                                                   rhs=At[:, ai, kc, 0:512],
                                                   start=(kc == 0), stop=(kc == KC - 1))
                        return ins
                    S.op("pe", mm, reads=(wbufs[slot], abufs[ai][p]),
                         writes=tuple(C.psb[pset + j] for j in range(nsub)))
                for j in range(nsub):
                    blk["epi"](C, tb, j, pset + j, width)
        S.run_phase()


def rmsnorm_loader(C, es, x_ap, g_ap, D, name):
    nc, S = C.nc, C.S
    KC = D // 128
    xt = es.enter_context(C.sbuf_tensor(name + "_x", [128, 2, D], F32))
    xn = es.enter_context(C.sbuf_tensor(name + "_xn", [128, 1, D], F32))
    gB = es.enter_context(C.sbuf_tensor(name + "_g", [128, D], F32))
    st = es.enter_context(C.sbuf_tensor(name + "_s", [128, 2, 4], F32))
    xb = [Buf(), Buf()]
    xnb = [Buf()]
    sb = [Buf(), Buf()]
    gb = Buf()
    S.dma("sp", lambda e: e.dma_start(out=gB[:], in_=g_ap.partition_broadcast(128)), writes=(gb,))
    cnt = [0]

    def loader(C, tb, At, abufs):
        for t in range(4):
            i = cnt[0] % 2
            cnt[0] += 1
            r0 = tb * 512 + t * 128
            S.dma("sp", (lambda e, i=i, r0=r0: e.dma_start(out=xt[:, i, :], in_=x_ap[r0:r0 + 128, :])),
                  writes=(xb[i],))
            S.op("act", (lambda e, i=i: e.activation(out=xn[:, 0, :], in_=xt[:, i, :], func=AF.Square,
                                                      accum_out=st[:, i, 0:1])),
                 reads=(xb[i],), writes=(xnb[0], sb[i]))
            S.op("dve", (lambda e, i=i: e.tensor_scalar(out=st[:, i, 1:2], in0=st[:, i, 0:1], scalar1=1.0 / D,
                                                         scalar2=EPS, op0=ALU.mult, op1=ALU.add)),
                 reads=(sb[i],), writes=(sb[i],))
            S.op("act", (lambda e, i=i: e.activation(out=st[:, i, 2:3], in_=st[:, i, 1:2], func=AF.Sqrt)),
                 reads=(sb[i],), writes=(sb[i],))
            S.op("dve", (lambda e, i=i: e.reciprocal(out=st[:, i, 3:4], in_=st[:, i, 2:3])),
                 reads=(sb[i],), writes=(sb[i],))
            S.op("dve", (lambda e, i=i: e.scalar_tensor_tensor(out=xn[:, 0, :], in0=xt[:, i, :],
                                                                scalar=st[:, i, 3:4], in1=gB[:],
                                                                op0=ALU.mult, op1=ALU.mult)),
                 reads=(xb[i], sb[i], gb), writes=(xnb[0],))
            for c4 in range(KC // 4):
                bank = c4 % 8

                def tr(e, i=i, c4=c4, bank=bank):
                    ins = None
                    for q in range(4):
                        c = c4 * 4 + q
                        ins = e.transpose(out=C.ps[:, bank, q * 128:(q + 1) * 128],
                                          in_=xn[:, 0, c * 128:(c + 1) * 128], identity=C.ident[:])
                    return ins
                S.op("pe", tr, reads=(xnb[0], C.identb), writes=(C.psb[bank],))
                p = (c4 * 4) // PK
                eng = C.evac_eng()
                dst = At[:, c4 * 4:c4 * 4 + 4, t * 128:(t + 1) * 128]
                src = C.ps[:, bank, :].rearrange("p (c t) -> p c t", c=4)
                copy_op(S, eng, dst, src, reads=(C.psb[bank],), writes=(abufs[p],))
    return loader


def dram_loader(C, aT_ap, K):
    S = C.S
    KC = K // 128
    npiece = (KC + PK - 1) // PK

    def loader(C, tb, At, abufs):
        for p in range(npiece):
            k0 = p * PK
            k1 = min(KC, k0 + PK)
            src = aT_ap[k0 * 128:k1 * 128, tb * 512:(tb + 1) * 512].rearrange("(c p) t -> p c t", p=128)
            S.dma("sp", (lambda e, src=src, k0=k0, k1=k1: e.dma_start(out=At[:, k0:k1, :], in_=src)),
                  writes=(abufs[p],))
    return loader


D_MODEL = 4096
TOK = 4096
N_IN = 11792
FF = 11008
GW = 1024
O_AQ, O_AK, O_AV, O_BQ, O_BK, O_BV, O_CU, O_DQ, O_DK, O_DV, O_DO, O_DG = (
    0, 1024, 2048, 3072, 4096, 4352, 4608, 7680, 8704, 9728, 10752, 11776)


def cast_jobs(C, pairs, q="pool"):
    S = C.S
    jobs = []
    for src, dst in pairs:
        K, N = src.shape
        sv = src.rearrange("(c p) n -> p c n", p=128)
        dv = dst.rearrange("(c p) n -> p c n", p=128)
        for c in range(0, K // 128, 4):
            c1 = min(K // 128, c + 4)
            for n0 in range(0, N, 2048):
                n1 = min(N, n0 + 2048)

                def job(sv=sv, dv=dv, c=c, c1=c1, n0=n0, n1=n1):
                    S.dma(q, (lambda e: e.dma_start(out=dv[:, c:c1, n0:n1], in_=sv[:, c:c1, n0:n1])), writes=(Buf(),))
                jobs.append(job)
    return jobs


def cast_weights(C, pairs):
    for j in cast_jobs(C, pairs):
        j()
    C.S.run_phase()


def store_epi(C, es, dst_ap, col0_of_block, dtype=F32, name="ep"):
    nc, S = C.nc, C.S
    ot = es.enter_context(C.sbuf_tensor(name + "_o", [128, 4, 512], dtype))
    ob = [Buf() for _ in range(4)]
    cnt = [0]

    def mk(col0):
        def epi(C, tb, j, bank, width):
            i = cnt[0] % 4
            cnt[0] += 1
            copy_op(S, C.evac_eng(), ot[:, i, 0:width], C.ps[:, bank, 0:width],
                    reads=(C.psb[bank],), writes=(ob[i],))
            r0 = tb * 512 + j * 128
            S.dma("pool", (lambda e, i=i, r0=r0: e.dma_start(out=dst_ap[r0:r0 + 128, col0:col0 + width],
                                                             in_=ot[:, i, 0:width])), reads=(ob[i],))
        return epi
    return mk


def resid_epi(C, es, res_ap, dst_ap, name="re"):
    nc, S = C.nc, C.S
    rt = es.enter_context(C.sbuf_tensor(name + "_r", [128, 4, 512], F32))
    rb = [Buf() for _ in range(4)]
    cnt = [0]

    def mk(col0):
        def epi(C, tb, j, bank, width):
            i = cnt[0] % 4
            cnt[0] += 1
            r0 = tb * 512 + j * 128
            S.dma("sp", (lambda e, i=i, r0=r0: e.dma_start(out=rt[:, i, 0:width],
                                                           in_=res_ap[r0:r0 + 128, col0:col0 + width])),
                  writes=(rb[i],))
            S.op("dve", (lambda e, i=i, bank=bank: e.tensor_tensor(out=rt[:, i, 0:width], in0=C.ps[:, bank, 0:width],
                                                                     in1=rt[:, i, 0:width], op=ALU.add)),
                 reads=(C.psb[bank], rb[i]), writes=(rb[i],))
            S.dma("pool", (lambda e, i=i, r0=r0: e.dma_start(out=dst_ap[r0:r0 + 128, col0:col0 + width],
                                                             in_=rt[:, i, 0:width])), reads=(rb[i],))
        return epi
    return mk


def swiglu_epi(C, es, aT_ap, name="sg"):
    nc, S = C.nc, C.S
    gt = es.enter_context(C.sbuf_tensor(name + "_g", [128, 4, 512], F32))
    at = es.enter_context(C.sbuf_tensor(name + "_a", [128, 4, 512], BF16))
    gb = [Buf() for _ in range(4)]
    ab = [Buf() for _ in range(4)]
    cnt = [0]

    def mk(c0):
        def epi(C, tb, j, bank, width):
            if j < 2:
                i = (cnt[0] * 2 + j) % 4
                S.op("act", (lambda e, i=i, bank=bank: e.activation(out=gt[:, i, :], in_=C.ps[:, bank, :],
                                                                      func=AF.Silu)),
                     reads=(C.psb[bank],), writes=(gb[i],))
            else:
                i = (cnt[0] * 2 + j - 2) % 4
                S.op("dve", (lambda e, i=i, bank=bank: e.tensor_tensor(out=at[:, i, :], in0=C.ps[:, bank, :],
                                                                         in1=gt[:, i, :], op=ALU.mult)),
                     reads=(C.psb[bank], gb[i]), writes=(ab[i],))
                r0 = c0 + (j - 2) * 128
                S.dma("pool", (lambda e, i=i, r0=r0, tb=tb: e.dma_start(
                    out=aT_ap[r0:r0 + 128, tb * 512:(tb + 1) * 512], in_=at[:, i, :])), reads=(ab[i],))
                if j == 3:
                    cnt[0] += 1
        return epi
    return mk


def final_norm(C, x_ap, g_ap, y_ap, D):
    nc, S = C.nc, C.S
    with contextlib.ExitStack() as es:
        xt = es.enter_context(C.sbuf_tensor("fn_x", [128, 2, D], F32))
        junk = es.enter_context(C.sbuf_tensor("fn_j", [128, D], BF16))
        gB = es.enter_context(C.sbuf_tensor("fn_g", [128, D], F32))
        st = es.enter_context(C.sbuf_tensor("fn_s", [128, 2, 4], F32))
        xb = [Buf(), Buf()]
        sb = [Buf(), Buf()]
        jb, gb = Buf(), Buf()
        S.dma("sp", lambda e: e.dma_start(out=gB[:], in_=g_ap.partition_broadcast(128)), writes=(gb,))
        for t in range(TOK // 128):
            i = t % 2
            r0 = t * 128
            S.dma("sp", (lambda e, i=i, r0=r0: e.dma_start(out=xt[:, i, :], in_=x_ap[r0:r0 + 128, :])),
                  writes=(xb[i],))
            S.op("act", (lambda e, i=i: e.activation(out=junk[:], in_=xt[:, i, :], func=AF.Square,
                                                      accum_out=st[:, i, 0:1])),
                 reads=(xb[i],), writes=(jb, sb[i]))
            S.op("dve", (lambda e, i=i: e.tensor_scalar(out=st[:, i, 1:2], in0=st[:, i, 0:1], scalar1=1.0 / D,
                                                         scalar2=EPS, op0=ALU.mult, op1=ALU.add)),
                 reads=(sb[i],), writes=(sb[i],))
            S.op("act", (lambda e, i=i: e.activation(out=st[:, i, 2:3], in_=st[:, i, 1:2], func=AF.Sqrt)),
                 reads=(sb[i],), writes=(sb[i],))
            S.op("dve", (lambda e, i=i: e.reciprocal(out=st[:, i, 3:4], in_=st[:, i, 2:3])),
                 reads=(sb[i],), writes=(sb[i],))
            S.op("dve", (lambda e, i=i: e.scalar_tensor_tensor(out=xt[:, i, :], in0=xt[:, i, :],
                                                                scalar=st[:, i, 3:4], in1=gB[:],
                                                                op0=ALU.mult, op1=ALU.mult)),
                 reads=(xb[i], sb[i], gb), writes=(xb[i],))
            S.dma("pool", (lambda e, i=i, r0=r0: e.dma_start(out=y_ap[r0:r0 + 128, :], in_=xt[:, i, :])),
                  reads=(xb[i],))
        S.run_phase()


def rope_pairs(S, eng, dst, src, cos, sin, tA, tB, reads, writes, half):
    x1, x2 = src[:, :, 0:half], src[:, :, half:2 * half]
    d1, d2 = dst[:, :, 0:half], dst[:, :, half:2 * half]
    ta, tb_ = Buf(), Buf()
    S.op(eng, lambda e: e.tensor_tensor(out=tA, in0=x1, in1=cos, op=ALU.mult), reads=reads, writes=(ta,))
    S.op(eng, lambda e: e.tensor_tensor(out=tB, in0=x2, in1=sin, op=ALU.mult), reads=reads, writes=(tb_,))
    S.op(eng, lambda e: e.tensor_tensor(out=d1, in0=tA, in1=tB, op=ALU.subtract), reads=(ta, tb_), writes=writes)
    S.op(eng, lambda e: e.tensor_tensor(out=tA, in0=x2, in1=cos, op=ALU.mult), reads=reads, writes=(ta,))
    S.op(eng, lambda e: e.tensor_tensor(out=tB, in0=x1, in1=sin, op=ALU.mult), reads=reads, writes=(tb_,))
    S.op(eng, lambda e: e.tensor_tensor(out=d2, in0=tA, in1=tB, op=ALU.add), reads=(ta, tb_), writes=writes)


def mixer_b(C, proj, yT, ropeB_ap, segb_ap, qkg_ap, dbg=None):
    nc, S = C.nc, C.S
    NT = TOK // 128
    with contextlib.ExitStack() as es:
        T = lambda n, s, d: es.enter_context(C.sbuf_tensor(n, s, d))
        qT = T("b_qT", [128, 8, TOK], BF16)
        kT = T("b_kT", [128, 2, TOK], BF16)
        V = T("b_V", [128, NT, 256], BF16)
        ones = T("b_ones", [128, 128], BF16)
        qk = T("b_qk", [128, 2, 1280], F32)
        sq = T("b_sq", [128, 1280], F32)
        ro = T("b_ro", [128, 2, 1280], F32)
        rt = T("b_rt", [128, 2, 128], F32)
        gB = T("b_gB", [128, 1280], F32)
        stt = T("b_st", [128, 2, 3, 10], F32)
        vt = T("b_vt", [128, 2, 256], F32)
        tmp = T("b_tmp", [128, 4, 320], F32)
        segb = T("b_segb", [128, 4], F32)
        pT = T("b_pT", [128, 3, 512], BF16)
        rec = T("b_rec", [128, 2, 512], F32)
        yo = T("b_yo", [128, 2, 512], BF16)
        qTb, kTb, Vb = Buf(), Buf(), Buf()
        onesb, gBb, segbb = Buf(), Buf(), Buf()
        qkb, rob, rtb, stb, vtb = [Buf(), Buf()], [Buf(), Buf()], [Buf(), Buf()], [Buf(), Buf()], [Buf(), Buf()]
        sqb = Buf()
        pTb = [Buf() for _ in range(3)]
        recb, yob = [Buf(), Buf()], [Buf(), Buf()]
        S.op("pool", lambda e: e.memset(ones[:], 1.0), writes=(onesb,))
        for hh in range(10):
            src = qkg_ap[0 if hh < 8 else 1, :].partition_broadcast(128)
            S.dma("sp", (lambda e, hh=hh, src=src: e.dma_start(out=gB[:, hh * 128:(hh + 1) * 128], in_=src)),
                  writes=(gBb,))
        S.dma("sp", lambda e: e.dma_start(out=segb[:], in_=segb_ap), writes=(segbb,))
        for t in range(NT):
            i = t % 2
            r0 = t * 128
            S.dma("sp", (lambda e, i=i, r0=r0: e.dma_start(out=qk[:, i, 0:1024], in_=proj[r0:r0 + 128, O_BQ:O_BQ + 1024])),
                  writes=(qkb[i],))
            S.dma("sp", (lambda e, i=i, r0=r0: e.dma_start(out=qk[:, i, 1024:1280], in_=proj[r0:r0 + 128, O_BK:O_BK + 256])),
                  writes=(qkb[i],))
            S.dma("sp", (lambda e, i=i, r0=r0: e.dma_start(out=vt[:, i, :], in_=proj[r0:r0 + 128, O_BV:O_BV + 256])),
                  writes=(vtb[i],))
            S.dma("sp", (lambda e, i=i, r0=r0: e.dma_start(out=rt[:, i, :], in_=ropeB_ap[r0:r0 + 128, :])),
                  writes=(rtb[i],))
            S.op("dve", (lambda e, i=i: e.tensor_tensor(out=sq[:], in0=qk[:, i, :], in1=qk[:, i, :], op=ALU.mult)),
                 reads=(qkb[i],), writes=(sqb,))

            S.op("dve", (lambda e, i=i: e.tensor_reduce(out=stt[:, i, 0, :], in_=sq[:].rearrange("p (h d) -> p h d", d=128),
                                                         axis=AX.X, op=ALU.add)), reads=(sqb,), writes=(stb[i],))
            S.op("dve", (lambda e, i=i: e.tensor_scalar(out=stt[:, i, 1, :], in0=stt[:, i, 0, :], scalar1=1.0 / 128,
                                                         scalar2=EPS, op0=ALU.mult, op1=ALU.add)),
                 reads=(stb[i],), writes=(stb[i],))
            S.op("act", (lambda e, i=i: e.activation(out=stt[:, i, 2, :], in_=stt[:, i, 1, :], func=AF.Sqrt)),
                 reads=(stb[i],), writes=(stb[i],))
            S.op("dve", (lambda e, i=i: e.reciprocal(out=stt[:, i, 0, :], in_=stt[:, i, 2, :])),
                 reads=(stb[i],), writes=(stb[i],))
            S.op("dve", (lambda e, i=i: e.tensor_scalar(out=stt[:, i, 0, 0:8], in0=stt[:, i, 0, 0:8], scalar1=128.0 ** -0.5,
                                                         scalar2=None, op0=ALU.mult)), reads=(stb[i],), writes=(stb[i],))
            S.op("dve", (lambda e, i=i: e.tensor_tensor(
                out=qk[:, i, :].rearrange("p (h d) -> p h d", d=128), in0=qk[:, i, :].rearrange("p (h d) -> p h d", d=128),
                in1=stt[:, i, 0, :].unsqueeze(2).broadcast_to([128, 10, 128]), op=ALU.mult)),
                reads=(stb[i], qkb[i]), writes=(qkb[i],))
            S.op("dve", (lambda e, i=i: e.tensor_tensor(out=qk[:, i, :], in0=qk[:, i, :], in1=gB[:], op=ALU.mult)),
                 reads=(qkb[i], gBb), writes=(qkb[i],))
            q3 = qk[:, i, :].rearrange("p (h d) -> p h d", d=128)
            r3 = ro[:, i, :].rearrange("p (h d) -> p h d", d=128)
            bc = lambda a: a.unsqueeze(1).broadcast_to([128, 10, 32])
            tv = lambda k: tmp[:, k, :].rearrange("p (h d) -> p h d", d=32)
            rope_pairs(S, "dve", r3[:, :, 0:64], q3[:, :, 0:64], bc(rt[:, i, 0:32]), bc(rt[:, i, 32:64]),
                       tv(0), tv(1), reads=(qkb[i], rtb[i]), writes=(rob[i],), half=32)
            rope_pairs(S, "dve", r3[:, :, 64:128], q3[:, :, 64:128], bc(rt[:, i, 64:96]), bc(rt[:, i, 96:128]),
                       tv(2), tv(3), reads=(qkb[i], rtb[i]), writes=(rob[i],), half=32)
            for g in range(3):
                nh = 4 if g < 2 else 2
                bank = 6 + (g % 2)

                def tr(e, i=i, g=g, nh=nh, bank=bank):
                    ins = None
                    for q in range(nh):
                        hh = g * 4 + q
                        ins = e.transpose(out=C.ps[:, bank, q * 128:(q + 1) * 128],
                                          in_=ro[:, i, hh * 128:(hh + 1) * 128], identity=C.ident[:])
                    return ins
                S.op("pe", tr, reads=(rob[i], C.identb), writes=(C.psb[bank],))
                src = C.ps[:, bank, 0:nh * 128].rearrange("p (h t) -> p h t", t=128)
                if g < 2:
                    copy_op(S, C.evac_eng(), qT[:, g * 4:g * 4 + 4, r0:r0 + 128], src, reads=(C.psb[bank],), writes=(qTb,))
                else:
                    copy_op(S, C.evac_eng(), kT[:, 0:2, r0:r0 + 128], src, reads=(C.psb[bank],), writes=(kTb,))
            S.op("pool", (lambda e, i=i, t=t: e.tensor_copy(out=V[:, t, :], in_=vt[:, i, :])), reads=(vtb[i],), writes=(Vb,))
        if dbg is not None:
            S.dma("sp", lambda e: e.dma_start(out=dbg["qT"], in_=qT[:].rearrange("p h t -> p (h t)")), reads=(qTb,))
            S.dma("sp", lambda e: e.dma_start(out=dbg["kT"], in_=kT[:].rearrange("p h t -> p (h t)")), reads=(kTb,))
            S.dma("sp", lambda e: e.dma_start(out=dbg["V"], in_=V[:].rearrange("p h t -> p (h t)")), reads=(Vb,))
        ucnt = 0
        acnt = 0
        for h in range(8):
            kv = h // 4
            for qb in range(TOK // 512):
                nb, db = 2 + (acnt % 2), 4 + (acnt % 2)
                ai = acnt % 2
                acnt += 1
                qs = 1 if qb >= (TOK // 1024) else 0

                def smm(kt, u):
                    sbank = u % 2
                    S.op("pe", (lambda e, kt=kt, sbank=sbank, kv=kv, h=h, qb=qb: e.matmul(
                        C.ps[:, sbank, :], lhsT=kT[:, kv, kt * 128:(kt + 1) * 128], rhs=qT[:, h, qb * 512:(qb + 1) * 512],
                        start=True, stop=True)), reads=(kTb, qTb), writes=(C.psb[sbank],))
                smm(0, ucnt)
                for kt in range(NT):
                    u = ucnt + kt
                    if kt + 1 < NT:
                        smm(kt + 1, u + 1)
                    sbank = u % 2
                    sl = u % 3
                    idx = (2 if kt >= NT // 2 else 0) + qs
                    S.op("act", (lambda e, sbank=sbank, sl=sl, idx=idx: e.activation(
                        out=pT[:, sl, :], in_=C.ps[:, sbank, :], func=AF.Exp, bias=segb[:, idx:idx + 1], scale=1.0)),
                        reads=(C.psb[sbank], segbb), writes=(pTb[sl],))

                    def pv(e, kt=kt, sl=sl, nb=nb, db=db, kv=kv):
                        e.matmul(C.ps[:, nb, :], lhsT=V[:, kt, kv * 128:(kv + 1) * 128], rhs=pT[:, sl, :],
                                 start=(kt == 0), stop=(kt == NT - 1))
                        return e.matmul(C.ps[:, db, :], lhsT=ones[:], rhs=pT[:, sl, :],
                                        start=(kt == 0), stop=(kt == NT - 1))
                    S.op("pe", pv, reads=(Vb, onesb, pTb[sl]), writes=(C.psb[nb], C.psb[db]))
                ucnt += NT
                S.op("dve", (lambda e, ai=ai, db=db: e.reciprocal(out=rec[:, ai, :], in_=C.ps[:, db, :])),
                     reads=(C.psb[db],), writes=(recb[ai],))
                S.op("dve", (lambda e, ai=ai, nb=nb: e.tensor_tensor(out=yo[:, ai, :], in0=C.ps[:, nb, :], in1=rec[:, ai, :],
                                                                      op=ALU.mult)),
                     reads=(C.psb[nb], recb[ai]), writes=(yob[ai],))
                S.dma("pool", (lambda e, ai=ai, h=h, qb=qb: e.dma_start(
                    out=yT[1024 + h * 128:1024 + (h + 1) * 128, qb * 512:(qb + 1) * 512], in_=yo[:, ai, :])),
                    reads=(yob[ai],))
        S.run_phase()


A_PAT = (1, 4, 16)
A_PAD = 1088


def a_units():
    idx = {}
    n = 0
    for pi, d in enumerate(A_PAT):
        for it in range(TOK // d // 128):
            for blk in range(2):
                idx[(pi, it, blk)] = n
                n += 1
    return idx, n


def mixer_a_prep(C, proj, aqT, akT, avb, ropeA_ap):
    nc, S = C.nc, C.S
    NT = TOK // 128
    with contextlib.ExitStack() as es:
        T = lambda n, s, d: es.enter_context(C.sbuf_tensor(n, s, d))
        qk = T("a_qk", [128, 2, 2048], F32)
        rt = T("a_rt", [128, 2, 32], F32)
        tmp = T("a_tmp", [128, 4, 256], F32)
        vt = T("a_vt", [128, 2, 1024], F32)
        vb = T("a_vb", [128, 2, 1024], BF16)
        stg = T("a_stg", [128, 2, 16, 512], BF16)
        qkb, rtb, vtb, vbb = [Buf(), Buf()], [Buf(), Buf()], [Buf(), Buf()], [Buf(), Buf()]
        stgb = [Buf(), Buf()]
        tb4 = [Buf() for _ in range(4)]
        for t in range(NT):
            i = t % 2
            r0 = t * 128
            S.dma("sp", (lambda e, i=i, r0=r0: e.dma_start(out=qk[:, i, :], in_=proj[r0:r0 + 128, O_AQ:O_AQ + 2048])),
                  writes=(qkb[i],))
            S.dma("sp", (lambda e, i=i, r0=r0: e.dma_start(out=rt[:, i, :], in_=ropeA_ap[r0:r0 + 128, :])),
                  writes=(rtb[i],))
            S.dma("sp", (lambda e, i=i, r0=r0: e.dma_start(out=vt[:, i, :], in_=proj[r0:r0 + 128, O_AV:O_AV + 1024])),
                  writes=(vtb[i],))
            S.op("pool", (lambda e, i=i: e.tensor_copy(out=vb[:, i, :], in_=vt[:, i, :])), reads=(vtb[i],), writes=(vbb[i],))
            S.dma("pool", (lambda e, i=i, r0=r0: e.dma_start(out=avb[r0:r0 + 128, :], in_=vb[:, i, :])), reads=(vbb[i],))
            q3 = qk[:, i, :].rearrange("p (h d) -> p h d", d=128)
            x1, x2 = q3[:, :, 0:16], q3[:, :, 16:32]
            cs = rt[:, i, 0:16].unsqueeze(1).broadcast_to([128, 16, 16])
            sn = rt[:, i, 16:32].unsqueeze(1).broadcast_to([128, 16, 16])
            tv = lambda k: tmp[:, k, :].rearrange("p (h d) -> p h d", d=16)
            rd = (qkb[i], rtb[i])
            S.op("dve", (lambda e, x1=x1, cs=cs: e.tensor_tensor(out=tv(0), in0=x1, in1=cs, op=ALU.mult)), reads=rd, writes=(tb4[0],))
            S.op("dve", (lambda e, x2=x2, sn=sn: e.tensor_tensor(out=tv(1), in0=x2, in1=sn, op=ALU.mult)), reads=rd, writes=(tb4[1],))
            S.op("dve", (lambda e, x2=x2, cs=cs: e.tensor_tensor(out=tv(2), in0=x2, in1=cs, op=ALU.mult)), reads=rd, writes=(tb4[2],))
            S.op("dve", (lambda e, x1=x1, sn=sn: e.tensor_tensor(out=tv(3), in0=x1, in1=sn, op=ALU.mult)), reads=rd, writes=(tb4[3],))
            S.op("dve", (lambda e, x1=x1: e.tensor_tensor(out=x1, in0=tv(0), in1=tv(1), op=ALU.subtract)),
                 reads=(tb4[0], tb4[1]), writes=(qkb[i],))
            S.op("dve", (lambda e, x2=x2: e.tensor_tensor(out=x2, in0=tv(2), in1=tv(3), op=ALU.add)),
                 reads=(tb4[2], tb4[3]), writes=(qkb[i],))
            si = (t // 4) % 2
            tt = t % 4
            for g in range(4):
                bank = 4 + g

                def tr(e, i=i, g=g, bank=bank):
                    ins = None
                    for q in range(4):
                        hh = g * 4 + q
                        ins = e.transpose(out=C.ps[:, bank, q * 128:(q + 1) * 128],
                                          in_=qk[:, i, hh * 128:(hh + 1) * 128], identity=C.ident[:])
                    return ins
                S.op("pe", tr, reads=(qkb[i], C.identb), writes=(C.psb[bank],))
                src = C.ps[:, bank, :].rearrange("p (h t) -> p h t", t=128)
                copy_op(S, C.evac_eng(), stg[:, si, g * 4:g * 4 + 4, tt * 128:(tt + 1) * 128], src,
                        reads=(C.psb[bank],), writes=(stgb[si],), scale=(128.0 ** -0.5 if g < 2 else None))
            if tt == 3:
                c0 = (t // 4) * 512
                S.dma("pool", (lambda e, si=si, c0=c0: e.dma_start(
                    out=aqT[:, :, c0:c0 + 512].rearrange("h p t -> p h t"), in_=stg[:, si, 0:8, :])), reads=(stgb[si],))
                S.dma("pool", (lambda e, si=si, c0=c0: e.dma_start(
                    out=akT[:, :, c0:c0 + 512].rearrange("h p t -> p h t"), in_=stg[:, si, 8:16, :])), reads=(stgb[si],))
        S.run_phase()


def mixer_a(C, aqT, akT, avb, yT, abias_ap, amask_ap):
    nc, S = C.nc, C.S
    uidx, NU = a_units()
    with contextlib.ExitStack() as es:
        T = lambda n, s, d: es.enter_context(C.sbuf_tensor(n, s, d))
        kTh = T("a_kT", [128, 2, TOK + 2 * A_PAD], BF16)
        qTh = T("a_qT", [128, 2, TOK], BF16)
        V1 = T("a_V1", [128, 2, 33, 128], BF16)
        V4 = T("a_V4", [128, 2, 4, 9, 128], BF16)
        V16 = T("a_V16", [128, 2, 16, 3, 128], BF16)
        num = T("a_num", [128, TOK], F32)
        den = T("a_den", [128, TOK], F32)
        yo = T("a_yo", [128, TOK], BF16)
        ones = T("a_ones", [128, 128], BF16)
        mskf = T("a_mskf", [128, 256], F32)
        msk = T("a_msk", [128, 256], BF16)
        bias = T("a_bias", [128, NU], F32)
        pT = T("a_pT", [128, 3, 256], BF16)
        pM = T("a_pM", [128, 3, 256], BF16)
        kb, qb_, vb = [Buf(), Buf()], [Buf(), Buf()], [Buf(), Buf()]
        numb, denb, yob, onesb, mskb, biasb, mskfb = Buf(), Buf(), Buf(), Buf(), Buf(), Buf(), Buf()
        pTb, pMb = [[Buf(), Buf()] for _ in range(3)], [Buf() for _ in range(3)]
        S.op("pool", lambda e: e.memset(ones[:], 1.0), writes=(onesb,))
        for i in range(2):
            S.op("pool", (lambda e, i=i: e.memset(kTh[:, i, :], 0.0)), writes=(kb[i],))
            S.op("pool", (lambda e, i=i: e.memset(V1[:, i], 0.0)), writes=(vb[i],))
            S.op("pool", (lambda e, i=i: e.memset(V4[:, i], 0.0)), writes=(vb[i],))
            S.op("pool", (lambda e, i=i: e.memset(V16[:, i], 0.0)), writes=(vb[i],))
        S.dma("sp", lambda e: e.dma_start(out=mskf[:], in_=amask_ap), writes=(mskfb,))
        S.op("dve", lambda e: e.tensor_scalar(out=msk[:], in0=mskf[:], scalar1=30000.0, scalar2=-30000.0, op0=ALU.mult, op1=ALU.add),
             reads=(mskfb,), writes=(mskb,))
        S.dma("sp", lambda e: e.dma_start(out=bias[:], in_=abias_ap), writes=(biasb,))
        Vt = (V1, V4, V16)
        u = 0
        for h in range(8):
            hi = h % 2
            S.dma("sp", (lambda e, h=h, hi=hi: e.dma_start(out=kTh[:, hi, A_PAD:A_PAD + TOK], in_=akT[h])), writes=(kb[hi],))
            S.dma("sp", (lambda e, h=h, hi=hi: e.dma_start(out=qTh[:, hi, :], in_=aqT[h])), writes=(qb_[hi],))
            for pi, d in enumerate(A_PAT):
                nm = TOK // d // 128 + 1
                for r in range(d):
                    for m in range(nm):
                        p0 = 64 if m == 0 else 0
                        p1 = 64 if m == nm - 1 else 128
                        t0 = r + d * (128 * m - 64 + p0)
                        src = avb[t0:t0 + d * (p1 - p0 - 1) + 1:d, h * 128:(h + 1) * 128]
                        if pi == 0:
                            dst = V1[p0:p1, hi, m, :]
                        elif pi == 1:
                            dst = V4[p0:p1, hi, r, m, :]
                        else:
                            dst = V16[p0:p1, hi, r, m, :]
                        S.dma("sp", (lambda e, dst=dst, src=src: e.dma_start(out=dst, in_=src)), writes=(vb[hi],))
            for pi, d in enumerate(A_PAT):
                for r in range(d):
                    for it in range(TOK // d // 128):
                        sbank = u % 3
                        nbank = 3 + (u % 2)
                        dbank = 5 + (u % 2)
                        sl = u % 3
                        u += 1
                        first_u = (pi == 0 and r == 0 and it == 0)
                        last_u = (pi == 2 and r == d - 1 and it == TOK // d // 128 - 1)
                        qc0 = r + d * 128 * it
                        qsl = slice(qc0, qc0 + d * 127 + 1, d)

                        def smm(e, hi=hi, d=d, r=r, it=it, sbank=sbank, qsl=qsl):
                            ins = None
                            for blk in range(2):
                                k0 = A_PAD + r + d * (128 * it + (64 if blk else -64))
                                e.matmul(C.ps[:, sbank, blk * 128:(blk + 1) * 128],
                                         lhsT=kTh[:, hi, k0:k0 + d * 127 + 1:d], rhs=qTh[:, hi, qsl],
                                         start=True, stop=False)
                                ins = e.matmul(C.ps[:, sbank, blk * 128:(blk + 1) * 128],
                                               lhsT=C.identh[:], rhs=msk[:, blk * 128:(blk + 1) * 128],
                                               start=False, stop=True)
                            return ins
                        S.op("pe", smm, reads=(kb[hi], qb_[hi], mskb, C.identhb), writes=(C.psb[sbank],))
                        for blk in range(2):
                            col = uidx[(pi, it, blk)]
                            S.op("act", (lambda e, sbank=sbank, sl=sl, blk=blk, col=col: e.activation(
                                out=pT[:, sl, blk * 128:(blk + 1) * 128], in_=C.ps[:, sbank, blk * 128:(blk + 1) * 128],
                                func=AF.Exp, bias=bias[:, col:col + 1], scale=1.0)),
                                reads=(C.psb[sbank], biasb), writes=(pTb[sl][blk],))
                        def pv(e, hi=hi, pi=pi, r=r, it=it, sl=sl, nbank=nbank, dbank=dbank):
                            for blk in range(2):
                                if pi == 0:
                                    vv = V1[:, hi, it + blk, :]
                                elif pi == 1:
                                    vv = V4[:, hi, r, it + blk, :]
                                else:
                                    vv = V16[:, hi, r, it + blk, :]
                                e.matmul(C.ps[:, nbank, 0:128], lhsT=vv, rhs=pT[:, sl, blk * 128:(blk + 1) * 128],
                                         start=(blk == 0), stop=(blk == 1))
                            ins = None
                            for blk in range(2):
                                ins = e.matmul(C.ps[:, dbank, 0:128], lhsT=ones[:], rhs=pT[:, sl, blk * 128:(blk + 1) * 128],
                                               start=(blk == 0), stop=(blk == 1))
                            return ins
                        S.op("pe", pv, reads=(vb[hi], onesb, pTb[sl][0], pTb[sl][1]), writes=(C.psb[nbank], C.psb[dbank]))
                        wr = (numb, denb) if (first_u or last_u) else ()
                        if pi == 0:
                            S.op("dve", (lambda e, nbank=nbank, qsl=qsl: e.tensor_copy(out=num[:, qsl], in_=C.ps[:, nbank, 0:128])),
                                 reads=(C.psb[nbank],), writes=wr)
                            S.op("dve", (lambda e, dbank=dbank, qsl=qsl: e.tensor_copy(out=den[:, qsl], in_=C.ps[:, dbank, 0:128])),
                                 reads=(C.psb[dbank],), writes=wr)
                        else:
                            S.op("dve", (lambda e, nbank=nbank, qsl=qsl: e.tensor_tensor(
                                out=num[:, qsl], in0=C.ps[:, nbank, 0:128], in1=num[:, qsl], op=ALU.add)),
                                reads=(C.psb[nbank],), writes=wr)
                            S.op("dve", (lambda e, dbank=dbank, qsl=qsl: e.tensor_tensor(
                                out=den[:, qsl], in0=C.ps[:, dbank, 0:128], in1=den[:, qsl], op=ALU.add)),
                                reads=(C.psb[dbank],), writes=wr)
            S.op("dve", lambda e: e.reciprocal(out=den[:], in_=den[:]), reads=(denb,), writes=(denb,))
            S.op("dve", lambda e: e.tensor_tensor(out=yo[:], in0=num[:], in1=den[:], op=ALU.mult),
                 reads=(numb, denb), writes=(yob,))
            S.dma("pool", (lambda e, h=h: e.dma_start(out=yT[h * 128:(h + 1) * 128, :], in_=yo[:])), reads=(yob,))
        S.run_phase()


def a_consts(is_prompt):
    L = 2048 if is_prompt else 4096
    pos = np.tile(np.arange(L, dtype=np.float32), TOK // L)
    inv = (np.float32(500000.0) ** (-np.arange(0, 32, 2, dtype=np.float32) / np.float32(32))).astype(np.float32)
    ang = pos[:, None] * inv[None, :]
    ropeA = np.concatenate([np.cos(ang), np.sin(ang)], -1).astype(np.float32)
    uidx, NU = a_units()
    abias = np.zeros((128, NU), np.float32)
    a = np.arange(128)
    for (pi, it, blk), col in uidx.items():
        d = A_PAT[pi]
        Lseg = L // d
        kidx = 128 * it + (64 if blk else -64) + a
        seg = (128 * it) // Lseg
        valid = (kidx >= seg * Lseg) & (kidx < (seg + 1) * Lseg)
        abias[:, col] = np.where(valid, 0.0, -30000.0)
    aa, bb = np.meshgrid(np.arange(128), np.arange(128), indexing="ij")
    amask = np.concatenate([(aa >= bb), (aa <= bb)], 1).astype(np.float32)
    return {"ropeA": ropeA, "abias": abias, "amask": amask}


def mixer_d_prep(C, proj, dqT, dkT, dkb, dvb, dgt, gateb_ap, carry_ap, tri_ap):
    nc, S = C.nc, C.S
    NT = TOK // 128
    with contextlib.ExitStack() as es:
        T = lambda n, s, d: es.enter_context(C.sbuf_tensor(n, s, d))
        qk = T("d_qk", [128, 2, 2048], F32)
        vt = T("d_vt", [128, 2, 1024], F32)
        kvb = T("d_kvb", [128, 2, 2048], BF16)
        stg = T("d_stg", [128, 2, 16, 512], BF16)
        g = T("d_g", [128, 2, 16], F32)
        gb_ = T("d_gb", [128, 16], F32)
        tri = T("d_tri", [128, 384], F32)
        car = T("d_car", [128, NT, 8], F32)
        tmp = T("d_tmp", [128, 2, 16], F32)
        gt = T("d_gt", [128, NT, 32], F32)
        qkb, vtb, kvbb, stgb, gbuf, tmpb = [Buf(), Buf()], [Buf(), Buf()], [Buf(), Buf()], [Buf(), Buf()], [Buf(), Buf()], [Buf(), Buf()]
        gbb, trib, carb, gtb = Buf(), Buf(), Buf(), Buf()
        S.dma("sp", lambda e: e.dma_start(out=gb_[:], in_=gateb_ap.rearrange("a b -> (a b)").partition_broadcast(128)), writes=(gbb,))
        S.dma("sp", lambda e: e.dma_start(out=tri[:], in_=tri_ap), writes=(trib,))
        S.dma("sp", lambda e: e.dma_start(out=car[:], in_=carry_ap), writes=(carb,))
        for t in range(NT):
            i = t % 2
            r0 = t * 128
            S.dma("sp", (lambda e, i=i, r0=r0: e.dma_start(out=qk[:, i, :], in_=proj[r0:r0 + 128, O_DQ:O_DQ + 2048])), writes=(qkb[i],))
            S.dma("sp", (lambda e, i=i, r0=r0: e.dma_start(out=vt[:, i, :], in_=proj[r0:r0 + 128, O_DV:O_DV + 1024])), writes=(vtb[i],))
            S.dma("sp", (lambda e, i=i, r0=r0: e.dma_start(out=g[:, i, :], in_=proj[r0:r0 + 128, O_DG:O_DG + 16])), writes=(gbuf[i],))
            S.op("pool", (lambda e, i=i: e.tensor_scalar(out=kvb[:, i, 0:1024], in0=qk[:, i, 1024:2048], scalar1=1.0 / 16, scalar2=None,
                                                          op0=ALU.mult)), reads=(qkb[i],), writes=(kvbb[i],))
            S.op("pool", (lambda e, i=i: e.tensor_copy(out=kvb[:, i, 1024:2048], in_=vt[:, i, :])), reads=(vtb[i],), writes=(kvbb[i],))
            S.dma("pool", (lambda e, i=i, r0=r0: e.dma_start(out=dkb[r0:r0 + 128, :], in_=kvb[:, i, 0:1024])), reads=(kvbb[i],))
            S.dma("pool", (lambda e, i=i, r0=r0: e.dma_start(out=dvb[r0:r0 + 128, :], in_=kvb[:, i, 1024:2048])), reads=(kvbb[i],))
            si = (t // 4) % 2
            tt = t % 4
            for gq in range(4):
                bank = 4 + gq

                def tr(e, i=i, gq=gq, bank=bank):
                    ins = None
                    for q in range(4):
                        hh = gq * 4 + q
                        ins = e.transpose(out=C.ps[:, bank, q * 128:(q + 1) * 128],
                                          in_=qk[:, i, hh * 128:(hh + 1) * 128], identity=C.ident[:])
                    return ins
                S.op("pe", tr, reads=(qkb[i], C.identb), writes=(C.psb[bank],))
                src = C.ps[:, bank, :].rearrange("p (h t) -> p h t", t=128)
                copy_op(S, C.evac_eng(), stg[:, si, gq * 4:gq * 4 + 4, tt * 128:(tt + 1) * 128], src,
                        reads=(C.psb[bank],), writes=(stgb[si],), scale=(1.0 / 16 if gq >= 2 else None))
            if tt == 3:
                c0 = (t // 4) * 512
                S.dma("pool", (lambda e, si=si, c0=c0: e.dma_start(
                    out=dqT[:, :, c0:c0 + 512].rearrange("h p t -> p h t"), in_=stg[:, si, 0:8, :])), reads=(stgb[si],))
                S.dma("pool", (lambda e, si=si, c0=c0: e.dma_start(
                    out=dkT[:, :, c0:c0 + 512].rearrange("h p t -> p h t"), in_=stg[:, si, 8:16, :])), reads=(stgb[si],))
            S.op("dve", (lambda e, i=i: e.tensor_tensor(out=g[:, i, :], in0=g[:, i, :], in1=gb_[:], op=ALU.add)),
                 reads=(gbuf[i], gbb), writes=(gbuf[i],))
            g4 = lambda i: g[:, i, :].rearrange("p (a h) -> p a h", h=4)
            t4 = lambda i: tmp[:, i, :].rearrange("p (a h) -> p a h", h=4)
            S.op("act", (lambda e, i=i: e.activation(out=t4(i)[:, 0:2, :], in_=g4(i)[:, 1:4:2, :], func=AF.Exp, scale=-1.0)),
                 reads=(gbuf[i],), writes=(tmpb[i],))
            S.op("act", (lambda e, i=i: e.activation(out=t4(i)[:, 0:2, :], in_=t4(i)[:, 0:2, :], func=AF.Ln, bias=1.0, scale=1.0)),
                 reads=(tmpb[i],), writes=(tmpb[i],))
            S.op("dve", (lambda e, i=i: e.tensor_scalar(out=tmp[:, i, 0:8], in0=tmp[:, i, 0:8], scalar1=-1.0, scalar2=None, op0=ALU.mult)),
                 reads=(tmpb[i],), writes=(tmpb[i],))

            def cums(e, i=i):
                e.matmul(C.ps[:, 3, 0:4], lhsT=tri[:, 0:128], rhs=tmp[:, i, 0:4], start=True, stop=True)
                e.matmul(C.ps[:, 3, 4:8], lhsT=tri[:, 128:256], rhs=tmp[:, i, 4:8], start=True, stop=True)
                return e.matmul(C.ps[:, 3, 8:16], lhsT=tri[:, 256:384], rhs=tmp[:, i, 0:8], start=True, stop=True)
            S.op("pe", cums, reads=(tmpb[i], trib), writes=(C.psb[3],))
            S.op("act", (lambda e, t=t: e.activation(out=gt[:, t, 0:8], in_=C.ps[:, 3, 0:8], func=AF.Exp)),
                 reads=(C.psb[3],), writes=(gtb,))
            S.op("act", (lambda e, t=t: e.activation(out=gt[:, t, 16:24], in_=C.ps[:, 3, 8:16], func=AF.Exp)),
                 reads=(C.psb[3],), writes=(gtb,))
            S.op("act", (lambda e, t=t: e.activation(out=gt[:, t, 24:32], in_=C.ps[:, 3, 0:8], func=AF.Exp, scale=-1.0)),
                 reads=(C.psb[3],), writes=(gtb,))
            S.op("dve", (lambda e, t=t: e.tensor_tensor(out=gt[:, t, 16:24], in0=gt[:, t, 16:24], in1=car[:, t, :], op=ALU.mult)),
                 reads=(gtb, carb), writes=(gtb,))
            S.op("dve", (lambda e, i=i, t=t: e.tensor_tensor(
                out=gt[:, t, 8:16].rearrange("p (a h) -> p a h", h=4), in0=g4(i)[:, 0:4:2, :],
                in1=C.ps[:, 3, 0:8].rearrange("p (a h) -> p a h", h=4), op=ALU.subtract)),
                reads=(gbuf[i], C.psb[3]), writes=(gtb,))
            S.op("act", (lambda e, t=t: e.activation(out=gt[:, t, 8:16], in_=gt[:, t, 8:16], func=AF.Exp)),
                 reads=(gtb,), writes=(gtb,))
        S.dma("pool", lambda e: e.dma_start(out=dgt, in_=gt[:]), reads=(gtb,))
        S.run_phase()


def mixer_d(C, proj, dqT, dkT, dkb, dvb, dgt, yT, mlg_ap, tri_ap):
    nc, S = C.nc, C.S
    NT = TOK // 128
    with contextlib.ExitStack() as es:
        T = lambda n, s, d: es.enter_context(C.sbuf_tensor(n, s, d))
        qTh = T("m_qT", [128, 2, TOK], BF16)
        kTh = T("m_kT", [128, 2, TOK], BF16)
        kt = T("m_kt", [128, NT, 256], BF16)
        va = T("m_va", [128, NT, 258], BF16)
        gt = T("m_gt", [128, NT, 32], F32)
        tri = T("m_tri", [128, 384], F32)
        hacc = T("m_h", [128, NT, 256], F32)
        St = T("m_S", [128, 2, 2, 257], F32)
        St2 = T("m_S2", [128, 2, 2, 257], F32)
        Sb = T("m_Sb", [128, 2, 2, 258], BF16)
        At = T("m_A", [128, 2, 2, 128], BF16)
        Ku = T("m_Ku", [128, 2, 2, 256], BF16)
        sc = T("m_sc", [128, 2, 2, 4], F32)
        ot = T("m_o", [128, 2, 256], F32)
        gB = T("m_gB", [128, 256], F32)
        fs = T("m_fs", [128, 2, 4], F32)
        yy = T("m_y", [128, 2, 256], F32)
        stg = T("m_stg", [128, 2, 2, 512], BF16)
        B = Buf
        qb_, kb_, ktb, vab, gtb, trib, gBb = B(), B(), B(), B(), B(), B(), B()
        haccb = [B() for _ in range(NT)]
        St2b = [[B(), B()], [B(), B()]]
        Stb = [[B(), B()], [B(), B()]]
        Sbb = [[B(), B()], [B(), B()]]
        Atb = [[B(), B()], [B(), B()]]
        Kub = [[B(), B()], [B(), B()]]
        scb = [[B(), B()], [B(), B()]]
        otb, fsb, yyb, stgb = [B(), B()], [B(), B()], [B(), B()], [B(), B()]
        S.dma("sp", lambda e: e.dma_start(out=gt[:], in_=dgt), writes=(gtb,))
        S.dma("sp", lambda e: e.dma_start(out=tri[:], in_=tri_ap), writes=(trib,))
        S.op("pool", lambda e: e.memset(va[:], 1.0), writes=(vab,))
        for hd in range(4):
            S.dma("sp", (lambda e, hd=hd: e.dma_start(out=qTh[:], in_=dqT[2 * hd:2 * hd + 2].rearrange("c p t -> p c t"))), writes=(qb_,))
            S.dma("sp", (lambda e, hd=hd: e.dma_start(out=kTh[:], in_=dkT[2 * hd:2 * hd + 2].rearrange("c p t -> p c t"))), writes=(kb_,))
            S.dma("sp", (lambda e, hd=hd: e.dma_start(out=kt[:], in_=dkb[:, hd * 256:(hd + 1) * 256].rearrange("(c p) d -> p c d", p=128))),
                  writes=(ktb,))
            S.dma("sp", (lambda e, hd=hd: e.dma_start(out=va[:, :, 0:256], in_=dvb[:, hd * 256:(hd + 1) * 256].rearrange("(c p) d -> p c d", p=128))),
                  writes=(vab,))
            S.dma("sp", (lambda e, hd=hd: e.dma_start(out=gB[:], in_=mlg_ap[hd * 256:(hd + 1) * 256].partition_broadcast(128))), writes=(gBb,))
            for dr in range(2):
                for dc in range(2):
                    S.op("pool", (lambda e, dr=dr, dc=dc: e.memset(St[:, dr, dc, :], 0.0)), writes=(Stb[dr][dc],))
                    S.op("pool", (lambda e, dr=dr, dc=dc: e.memset(Sb[:, dr, dc, :], 0.0)), writes=(Sbb[dr][dc],))
            for j in range(NT):
                sl = j % 2
                for dr in range(2):
                    c = j if dr == 0 else NT - 1 - j
                    cs = slice(c * 128, (c + 1) * 128)
                    col = dr * 4 + hd
                    sbank = dr
                    xbank = 2 + dr
                    rb = 4 + 2 * dr

                    def smm(e, cs=cs, sbank=sbank):
                        e.matmul(C.ps[:, sbank, 0:128], lhsT=kTh[:, 0, cs], rhs=qTh[:, 0, cs], start=True, stop=False)
                        return e.matmul(C.ps[:, sbank, 0:128], lhsT=kTh[:, 1, cs], rhs=qTh[:, 1, cs], start=False, stop=True)
                    S.op("pe", smm, reads=(kb_, qb_), writes=(C.psb[sbank],))
                    S.op("dve", (lambda e, sl=sl, dr=dr, c=c, col=col, sbank=sbank: e.scalar_tensor_tensor(
                        out=At[:, sl, dr, :], in0=C.ps[:, sbank, 0:128], scalar=gt[:, c, 8 + col:9 + col],
                        in1=tri[:, dr * 128:(dr + 1) * 128], op0=ALU.mult, op1=ALU.mult)),
                        reads=(C.psb[sbank], gtb, trib), writes=(Atb[sl][dr],))
                    S.op("act", (lambda e, sl=sl, dr=dr, c=c, col=col: e.activation(
                        out=Ku[:, sl, dr, :], in_=kt[:, c, :], func=AF.Copy, scale=gt[:, c, 8 + col:9 + col])),
                        reads=(ktb, gtb), writes=(Kub[sl][dr],))

                    def xmm(e, cs=cs, c=c, sl=sl, dr=dr, xbank=xbank):
                        e.matmul(C.ps[:, xbank, 0:257], lhsT=qTh[:, 0, cs], rhs=Sb[:, dr, 0, 0:257], start=True, stop=False)
                        e.matmul(C.ps[:, xbank, 0:257], lhsT=qTh[:, 1, cs], rhs=Sb[:, dr, 1, 0:257], start=False, stop=False)
                        return e.matmul(C.ps[:, xbank, 0:257], lhsT=At[:, sl, dr, :], rhs=va[:, c, 0:257], start=False, stop=True)
                    S.op("pe", xmm, reads=(qb_, Sbb[dr][0], Sbb[dr][1], Atb[sl][dr], vab), writes=(C.psb[xbank],))

                    def rmm(e, c=c, sl=sl, dr=dr, rb=rb):
                        e.matmul(C.ps[:, rb, 0:257], lhsT=Ku[:, sl, dr, 0:128], rhs=va[:, c, 0:257], start=True, stop=True)
                        return e.matmul(C.ps[:, rb + 1, 0:257], lhsT=Ku[:, sl, dr, 128:256], rhs=va[:, c, 0:257], start=True, stop=True)
                    S.op("pe", rmm, reads=(Kub[sl][dr], vab), writes=(C.psb[rb], C.psb[rb + 1]))
                    sv = sc[:, sl, dr, :]
                    S.op("act", (lambda e, sv=sv, xbank=xbank: e.activation(out=sv[:, 0:1], in_=C.ps[:, xbank, 256:257], func=AF.Abs)),
                         reads=(C.psb[xbank],), writes=(scb[sl][dr],))
                    S.op("dve", (lambda e, sv=sv, c=c, col=col: e.tensor_tensor(out=sv[:, 1:2], in0=sv[:, 0:1], in1=gt[:, c, 24 + col:25 + col],
                                                                                op=ALU.max)), reads=(scb[sl][dr], gtb), writes=(scb[sl][dr],))
                    S.op("dve", (lambda e, sv=sv: e.reciprocal(out=sv[:, 2:3], in_=sv[:, 1:2])), reads=(scb[sl][dr],), writes=(scb[sl][dr],))
                    first = (dr == 0 and c < NT // 2) or (dr == 1 and c >= NT // 2)
                    if first:
                        S.op("act", (lambda e, sv=sv, c=c, xbank=xbank: e.activation(out=hacc[:, c, :], in_=C.ps[:, xbank, 0:256],
                                                                                      func=AF.Copy, scale=sv[:, 2:3])),
                             reads=(C.psb[xbank], scb[sl][dr]), writes=(haccb[c],))
                    else:
                        S.op("dve", (lambda e, sv=sv, c=c, xbank=xbank: e.scalar_tensor_tensor(
                            out=hacc[:, c, :], in0=C.ps[:, xbank, 0:256], scalar=sv[:, 2:3], in1=hacc[:, c, :],
                            op0=ALU.mult, op1=ALU.add)), reads=(C.psb[xbank], scb[sl][dr], haccb[c]), writes=(haccb[c],))
                    for dc in range(2):
                        S.op("dve", (lambda e, dr=dr, dc=dc, c=c, col=col: e.tensor_scalar(
                            out=St2[:, dr, dc, :], in0=St[:, dr, dc, :], scalar1=gt[:, c, 16 + col:17 + col], scalar2=None, op0=ALU.mult)),
                            reads=(gtb, Stb[dr][dc]), writes=(St2b[dr][dc],))
                        S.op("dve", (lambda e, dr=dr, dc=dc, c=c, col=col, rb=rb: e.scalar_tensor_tensor(
                            out=St[:, dr, dc, :], in0=C.ps[:, rb + dc, 0:257], scalar=gt[:, c, 16 + col:17 + col], in1=St2[:, dr, dc, :],
                            op0=ALU.mult, op1=ALU.add)), reads=(C.psb[rb + dc], gtb, St2b[dr][dc]), writes=(Stb[dr][dc],))
                        S.op("act", (lambda e, dr=dr, dc=dc: e.activation(out=Sb[:, dr, dc, 0:257], in_=St[:, dr, dc, :], func=AF.Copy)),
                             reads=(Stb[dr][dc],), writes=(Sbb[dr][dc],))
            for c in range(NT):
                i = c % 2
                r0 = c * 128
                S.dma("sp", (lambda e, i=i, r0=r0, hd=hd: e.dma_start(out=ot[:, i, :], in_=proj[r0:r0 + 128, O_DO + hd * 256:O_DO + (hd + 1) * 256])),
                      writes=(otb[i],))
                S.op("act", (lambda e, i=i, c=c: e.activation(out=yy[:, i, :], in_=hacc[:, c, :], func=AF.Square, accum_out=fs[:, i, 0:1])),
                     reads=(haccb[c],), writes=(yyb[i], fsb[i]))
                S.op("dve", (lambda e, i=i: e.tensor_scalar(out=fs[:, i, 1:2], in0=fs[:, i, 0:1], scalar1=1.0 / 256, scalar2=EPS,
                                                             op0=ALU.mult, op1=ALU.add)), reads=(fsb[i],), writes=(fsb[i],))
                S.op("act", (lambda e, i=i: e.activation(out=fs[:, i, 2:3], in_=fs[:, i, 1:2], func=AF.Sqrt)), reads=(fsb[i],), writes=(fsb[i],))
                S.op("dve", (lambda e, i=i: e.reciprocal(out=fs[:, i, 3:4], in_=fs[:, i, 2:3])), reads=(fsb[i],), writes=(fsb[i],))
                S.op("dve", (lambda e, i=i, c=c: e.scalar_tensor_tensor(out=yy[:, i, :], in0=hacc[:, c, :], scalar=fs[:, i, 3:4], in1=gB[:],
                                                                          op0=ALU.mult, op1=ALU.mult)),
                     reads=(haccb[c], fsb[i], gBb), writes=(yyb[i],))
                S.op("act", (lambda e, i=i: e.activation(out=ot[:, i, :], in_=ot[:, i, :], func=AF.Sigmoid)), reads=(otb[i],), writes=(otb[i],))
                S.op("dve", (lambda e, i=i: e.tensor_tensor(out=yy[:, i, :], in0=yy[:, i, :], in1=ot[:, i, :], op=ALU.mult)),
                     reads=(yyb[i], otb[i]), writes=(yyb[i],))
                bank = c % 2

                def tr(e, i=i, bank=bank):
                    e.transpose(out=C.ps[:, bank, 0:128], in_=yy[:, i, 0:128], identity=C.ident[:])
                    return e.transpose(out=C.ps[:, bank, 128:256], in_=yy[:, i, 128:256], identity=C.ident[:])
                S.op("pe", tr, reads=(yyb[i], C.identb), writes=(C.psb[bank],))
                si = (c // 4) % 2
                tt = c % 4
                copy_op(S, C.evac_eng(), stg[:, si, :, tt * 128:(tt + 1) * 128], C.ps[:, bank, 0:256].rearrange("p (a t) -> p a t", t=128),
                        reads=(C.psb[bank],), writes=(stgb[si],))
                if tt == 3:
                    c0 = (c // 4) * 512
                    row0 = 3072 + hd * 256
                    S.dma("pool", (lambda e, si=si, c0=c0, row0=row0: e.dma_start(
                        out=yT[row0:row0 + 256, c0:c0 + 512].rearrange("(a p) t -> p a t", p=128), in_=stg[:, si, :, :])), reads=(stgb[si],))
        S.run_phase()


def d_consts(is_prompt):
    NT = TOK // 128
    a, b = np.meshgrid(np.arange(128), np.arange(128), indexing="ij")
    tri = np.concatenate([(a <= b), (a >= b), np.ones((128, 128), bool)], 1).astype(np.float32)
    carry = np.ones((128, NT, 8), np.float32)
    if is_prompt:
        carry[:, NT // 2 - 1, 0:4] = 0.0
        carry[:, NT // 2, 4:8] = 0.0
    return {"tri": tri, "carry": carry}


NF = 3072
NDFT = 6144


def hy_conv_phase(C, proj, ucT, vtok, cw_ap, cb_ap, mprev_ap, mnext_ap):
    nc, S = C.nc, C.S
    NT = TOK // 128
    with contextlib.ExitStack() as es:
        T = lambda n, s, d: es.enter_context(C.sbuf_tensor(n, s, d))
        u = T("c_u", [128, 2, 3, 1024], F32)
        wB = T("c_w", [128, 3, 1024], F32)
        bB = T("c_b", [128, 1024], F32)
        acc = T("c_acc", [128, 2, 3, 1024], F32)
        vb = T("c_vb", [128, 2, 1024], BF16)
        mp = T("c_mp", [128, 2, NT], F32)
        stg = T("c_stg", [128, 2, 8, 512], F32)
        B = Buf
        ub = [[B(), B(), B()], [B(), B(), B()]]
        wb, bb, mpb = B(), B(), B()
        accb = [[B(), B(), B()], [B(), B(), B()]]
        vbb, stgb = [B(), B()], [B(), B()]
        S.dma("sp", lambda e: e.dma_start(out=mp[:, 0, :], in_=mprev_ap), writes=(mpb,))
        S.dma("sp", lambda e: e.dma_start(out=mp[:, 1, :], in_=mnext_ap), writes=(mpb,))
        for i in range(2):
            for k in range(3):
                S.op("pool", (lambda e, i=i, k=k: e.memset(u[:, i, k, :], 0.0)), writes=(ub[i][k],))
        for grp in range(3):
            g0 = grp * 1024
            for k in range(3):
                S.dma("sp", (lambda e, k=k, g0=g0: e.dma_start(out=wB[:, k, :], in_=cw_ap[k, g0:g0 + 1024].partition_broadcast(128))),
                      writes=(wb,))
            S.dma("sp", (lambda e, g0=g0: e.dma_start(out=bB[:], in_=cb_ap[g0:g0 + 1024].partition_broadcast(128))), writes=(bb,))
            for t in range(NT):
                i = t % 2
                r0 = t * 128
                c0 = O_CU + g0
                if t == 0:
                    S.dma("sp", (lambda e, i=i, c0=c0: e.dma_start(out=u[1:128, i, 0, :], in_=proj[0:127, c0:c0 + 1024])), writes=(ub[i][0],))
                else:
                    S.dma("sp", (lambda e, i=i, c0=c0, r0=r0: e.dma_start(out=u[:, i, 0, :], in_=proj[r0 - 1:r0 + 127, c0:c0 + 1024])),
                          writes=(ub[i][0],))
                S.dma("sp", (lambda e, i=i, c0=c0, r0=r0: e.dma_start(out=u[:, i, 1, :], in_=proj[r0:r0 + 128, c0:c0 + 1024])), writes=(ub[i][1],))
                if t == NT - 1:
                    S.dma("sp", (lambda e, i=i, c0=c0, r0=r0: e.dma_start(out=u[0:127, i, 2, :], in_=proj[r0 + 1:r0 + 128, c0:c0 + 1024])),
                          writes=(ub[i][2],))
                else:
                    S.dma("sp", (lambda e, i=i, c0=c0, r0=r0: e.dma_start(out=u[:, i, 2, :], in_=proj[r0 + 1:r0 + 129, c0:c0 + 1024])),
                          writes=(ub[i][2],))
                S.op("dve", (lambda e, i=i, t=t: e.scalar_tensor_tensor(out=acc[:, i, 0, :], in0=u[:, i, 0, :], scalar=mp[:, 0, t:t + 1],
                                                                          in1=wB[:, 0, :], op0=ALU.mult, op1=ALU.mult)),
                     reads=(ub[i][0], mpb, wb), writes=(accb[i][0],))
                S.op("pool", (lambda e, i=i: e.tensor_tensor(out=acc[:, i, 1, :], in0=u[:, i, 1, :], in1=wB[:, 1, :], op=ALU.mult)),
                     reads=(ub[i][1], wb), writes=(accb[i][1],))
                S.op("dve", (lambda e, i=i, t=t: e.scalar_tensor_tensor(out=acc[:, i, 2, :], in0=u[:, i, 2, :], scalar=mp[:, 1, t:t + 1],
                                                                          in1=wB[:, 2, :], op0=ALU.mult, op1=ALU.mult)),
                     reads=(ub[i][2], mpb, wb), writes=(accb[i][2],))
                S.op("pool", (lambda e, i=i: e.tensor_tensor(out=acc[:, i, 1, :], in0=acc[:, i, 1, :], in1=bB[:], op=ALU.add)),
                     reads=(accb[i][1], bb), writes=(accb[i][1],))
                S.op("dve", (lambda e, i=i: e.tensor_tensor(out=acc[:, i, 0, :], in0=acc[:, i, 0, :], in1=acc[:, i, 2, :], op=ALU.add)),
                     reads=(accb[i][0], accb[i][2]), writes=(accb[i][0],))
                S.op("dve", (lambda e, i=i: e.tensor_tensor(out=acc[:, i, 0, :], in0=acc[:, i, 0, :], in1=acc[:, i, 1, :], op=ALU.add)),
                     reads=(accb[i][0], accb[i][1]), writes=(accb[i][0],))
                if grp == 0:
                    S.op("act", (lambda e, i=i: e.activation(out=vb[:, i, :], in_=acc[:, i, 0, :], func=AF.Copy)),
                         reads=(accb[i][0],), writes=(vbb[i],))
                    S.dma("pool", (lambda e, i=i, r0=r0: e.dma_start(out=vtok[r0:r0 + 128, :], in_=vb[:, i, :])), reads=(vbb[i],))
                si = (t // 4) % 2
                tt = t % 4
                for gq in range(2):
                    bank = 4 + (t % 2) * 2 + gq

                    def tr(e, i=i, gq=gq, bank=bank):
                        ins = None
                        for q in range(4):
                            cc = gq * 4 + q
                            ins = e.transpose(out=C.ps[:, bank, q * 128:(q + 1) * 128],
                                              in_=acc[:, i, 0, cc * 128:(cc + 1) * 128], identity=C.ident[:])
                        return ins
                    S.op("pe", tr, reads=(accb[i][0], C.identb), writes=(C.psb[bank],))
                    copy_op(S, C.evac_eng(), stg[:, si, gq * 4:gq * 4 + 4, tt * 128:(tt + 1) * 128],
                            C.ps[:, bank, :].rearrange("p (a t) -> p a t", t=128), reads=(C.psb[bank],), writes=(stgb[si],))
                if tt == 3:
                    t0 = (t // 4) * 512
                    S.dma("pool", (lambda e, si=si, t0=t0, g0=g0: e.dma_start(
                        out=ucT[g0:g0 + 1024, t0:t0 + 512].rearrange("(a p) t -> p a t", p=128), in_=stg[:, si, :, :])), reads=(stgb[si],))
        S.run_phase()


def hy_filter_phase(C, hb, zT_ap, r_ap, w1, b1, f1, w2, b2, f2, w3, decay):
    nc, S = C.nc, C.S
    with contextlib.ExitStack() as es:
        T = lambda n, s, d: es.enter_context(C.sbuf_tensor(n, s, d))
        zt = T("f_z", [33, 2, 512], F32)
        W1 = T("f_w1", [33, 64], F32)
        W2 = T("f_w2", [64, 64], F32)
        W3 = T("f_w3", [64, 2048], F32)
        pr = T("f_pr", [64, 8], F32)
        dB = T("f_dB", [128, 2048], F32)
        rr = T("f_r", [128, TOK // 128], F32)
        nr = T("f_nr", [128, TOK // 128], F32)
        xa = T("f_xa", [64, 6, 512], F32)
        h2 = T("f_h2", [64, 2, 512], F32)
        E = T("f_E", [128, 2, 512], F32)
        ho = T("f_ho", [128, 2, 512], BF16)
        B = Buf
        ztb, w1b, w2b, w3b, prb, dBb, rrb = [B(), B()], B(), B(), B(), B(), B(), B()
        xab, h2b, Eb, hob = B(), [B(), B()], [B(), B()], [B(), B()]
        col = lambda ap: ap.rearrange("(p o) -> p o", o=1)
        S.dma("sp", lambda e: e.dma_start(out=W1[:], in_=w1), writes=(w1b,))
        S.dma("sp", lambda e: e.dma_start(out=W2[:], in_=w2), writes=(w2b,))
        S.dma("sp", lambda e: e.dma_start(out=W3[:], in_=w3), writes=(w3b,))
        for k, ap in enumerate((b1, f1, b2, f2)):
            S.dma("sp", (lambda e, k=k, ap=ap: e.dma_start(out=pr[:, k:k + 1], in_=col(ap))), writes=(prb,))
        S.dma("sp", lambda e: e.dma_start(out=dB[:], in_=decay.partition_broadcast(128)), writes=(dBb,))
        S.dma("sp", lambda e: e.dma_start(out=rr[:], in_=r_ap), writes=(rrb,))
        S.op("dve", lambda e: e.tensor_tensor(out=pr[:, 4:5], in0=pr[:, 0:1], in1=pr[:, 1:2], op=ALU.mult), reads=(prb,), writes=(prb,))
        S.op("dve", lambda e: e.tensor_tensor(out=pr[:, 5:6], in0=pr[:, 2:3], in1=pr[:, 3:4], op=ALU.mult), reads=(prb,), writes=(prb,))
        S.op("dve", lambda e: e.tensor_scalar(out=nr[:], in0=rr[:], scalar1=-1.0, scalar2=None, op0=ALU.mult), reads=(rrb,), writes=(rrb,))

        def sin_layer(bank, fcol, fbcol, out_ap, out_buf):
            S.op("act", (lambda e: e.activation(out=xa[:, 0, :], in_=C.ps[0:64, bank, :], func=AF.Identity,
                                                scale=pr[:, fcol:fcol + 1], bias=pr[:, fbcol:fbcol + 1])),
                 reads=(C.psb[bank], prb), writes=(xab,))
            S.op("act", lambda e: e.activation(out=xa[:, 1, :], in_=xa[:, 0, :], func=AF.Sin, scale=0.5), reads=(xab,), writes=(xab,))
            S.op("act", lambda e: e.activation(out=xa[:, 2, :], in_=xa[:, 0, :], func=AF.Sin, scale=0.25), reads=(xab,), writes=(xab,))
            S.op("dve", lambda e: e.tensor_tensor(out=xa[:, 3, :], in0=xa[:, 2, :], in1=xa[:, 2, :], op=ALU.mult), reads=(xab,), writes=(xab,))
            S.op("dve", lambda e: e.tensor_scalar(out=xa[:, 4, :], in0=xa[:, 3, :], scalar1=-2.0, scalar2=1.0, op0=ALU.mult, op1=ALU.add),
                 reads=(xab,), writes=(xab,))
            S.op("dve", lambda e: e.scalar_tensor_tensor(out=out_ap, in0=xa[:, 1, :], scalar=2.0, in1=xa[:, 4, :], op0=ALU.mult, op1=ALU.mult),
                 reads=(xab,), writes=(xab, out_buf))
        cnt = 0
        for nb in range(TOK // 512):
            i = nb % 2
            S.dma("sp", (lambda e, i=i, nb=nb: e.dma_start(out=zt[:, i, :], in_=zT_ap[:, nb * 512:(nb + 1) * 512])), writes=(ztb[i],))
            S.op("pe", (lambda e, i=i: e.matmul(C.ps[0:64, 0, :], lhsT=W1[:], rhs=zt[:, i, :], start=True, stop=True)),
                 reads=(w1b, ztb[i]), writes=(C.psb[0],))
            sin_layer(0, 1, 4, xa[:, 5, :], xab)
            S.op("pe", lambda e: e.matmul(C.ps[0:64, 1, :], lhsT=W2[:], rhs=xa[:, 5, :], start=True, stop=True),
                 reads=(w2b, xab), writes=(C.psb[1],))
            sin_layer(1, 3, 5, h2[:, i, :], h2b[i])
            for nt in range(4):
                tile_n = nb * 4 + nt
                for cb in range(4):
                    k = cnt % 2
                    bank = 2 + cnt % 4
                    cnt += 1
                    S.op("pe", (lambda e, i=i, nt=nt, cb=cb, bank=bank: e.matmul(
                        C.ps[:, bank, :], lhsT=h2[:, i, nt * 128:(nt + 1) * 128], rhs=W3[:, cb * 512:(cb + 1) * 512], start=True, stop=True)),
                        reads=(h2b[i], w3b), writes=(C.psb[bank],))
                    S.op("act", (lambda e, k=k, cb=cb, tile_n=tile_n: e.activation(
                        out=E[:, k, :], in_=dB[:, cb * 512:(cb + 1) * 512], func=AF.Exp, scale=nr[:, tile_n:tile_n + 1])),
                        reads=(dBb, rrb), writes=(Eb[k],))
                    S.op("dve", (lambda e, k=k, bank=bank: e.tensor_tensor(out=ho[:, k, :], in0=C.ps[:, bank, :], in1=E[:, k, :], op=ALU.mult)),
                         reads=(C.psb[bank], Eb[k]), writes=(hob[k],))
                    S.dma("pool", (lambda e, k=k, tile_n=tile_n, cb=cb: e.dma_start(
                        out=hb[tile_n * 128:(tile_n + 1) * 128, cb * 512:(cb + 1) * 512], in_=ho[:, k, :])), reads=(hob[k],))
        S.run_phase()


def storeF_epi(C, es, dst_ap, name="sf"):
    nc, S = C.nc, C.S
    ot = es.enter_context(C.sbuf_tensor(name + "_o", [128, 4, 512], F32))
    ob = [Buf() for _ in range(4)]
    cnt = [0]

    def mk(row0):
        def epi(C, tb, j, bank, width):
            i = cnt[0] % 4
            cnt[0] += 1
            copy_op(S, C.evac_eng(), ot[:, i, :], C.ps[:, bank, :], reads=(C.psb[bank],), writes=(ob[i],))
            r = row0 + j * 128
            S.dma("pool", (lambda e, i=i, r=r, tb=tb: e.dma_start(out=dst_ap[r:r + 128, tb * 512:(tb + 1) * 512], in_=ot[:, i, :])),
                  reads=(ob[i],))
        return epi
    return mk


def cmul_epi(C, es, Hc, Hs, hc0, YY, name="cm"):
    nc, S = C.nc, C.S
    T = lambda n, s, d: es.enter_context(C.sbuf_tensor(name + n, s, d))
    hh = T("_h", [128, 2, 2, 512], F32)
    xx = T("_x", [128, 2, 2, 512], F32)
    tt = T("_t", [128, 2, 4, 512], F32)
    yy = T("_y", [128, 2, 2, 512], BF16)
    hb_, xb, tb_, yb = [Buf(), Buf()], [Buf(), Buf()], [[Buf() for _ in range(4)] for _ in range(2)], [[Buf(), Buf()], [Buf(), Buf()]]
    cnt = [0]

    def mk(f0):
        def epi(C, tb, j, bank, width):
            if j < 2:
                return
            jj = j - 2
            i = cnt[0] % 2
            cnt[0] += 1
            fr = f0 + jj * 128
            cs = slice(hc0 + tb * 512, hc0 + (tb + 1) * 512)
            S.dma("sp", (lambda e, i=i, fr=fr, cs=cs: e.dma_start(out=hh[:, i, 0, :], in_=Hc[fr:fr + 128, cs])), writes=(hb_[i],))
            S.dma("sp", (lambda e, i=i, fr=fr, cs=cs: e.dma_start(out=hh[:, i, 1, :], in_=Hs[fr:fr + 128, cs])), writes=(hb_[i],))
            S.op("act", (lambda e, i=i, b=bank - 2: e.activation(out=xx[:, i, 0, :], in_=C.ps[:, b, :], func=AF.Copy)),
                 reads=(C.psb[bank - 2],), writes=(xb[i],))
            S.op("act", (lambda e, i=i, b=bank: e.activation(out=xx[:, i, 1, :], in_=C.ps[:, b, :], func=AF.Copy)),
                 reads=(C.psb[bank],), writes=(xb[i],))
            rd = (xb[i], hb_[i])
            S.op("dve", (lambda e, i=i: e.tensor_tensor(out=tt[:, i, 0, :], in0=xx[:, i, 0, :], in1=hh[:, i, 0, :], op=ALU.mult)), reads=rd, writes=(tb_[i][0],))
            S.op("pool", (lambda e, i=i: e.tensor_tensor(out=tt[:, i, 1, :], in0=xx[:, i, 1, :], in1=hh[:, i, 1, :], op=ALU.mult)), reads=rd, writes=(tb_[i][1],))
            S.op("dve", (lambda e, i=i: e.tensor_tensor(out=tt[:, i, 2, :], in0=xx[:, i, 0, :], in1=hh[:, i, 1, :], op=ALU.mult)), reads=rd, writes=(tb_[i][2],))
            S.op("pool", (lambda e, i=i: e.tensor_tensor(out=tt[:, i, 3, :], in0=xx[:, i, 1, :], in1=hh[:, i, 0, :], op=ALU.mult)), reads=rd, writes=(tb_[i][3],))
            S.op("dve", (lambda e, i=i: e.tensor_tensor(out=yy[:, i, 0, :], in0=tt[:, i, 0, :], in1=tt[:, i, 1, :], op=ALU.subtract)),
                 reads=(tb_[i][0], tb_[i][1]), writes=(yb[i][0],))
            S.op("pool", (lambda e, i=i: e.tensor_tensor(out=yy[:, i, 1, :], in0=tt[:, i, 2, :], in1=tt[:, i, 3, :], op=ALU.add)),
                 reads=(tb_[i][2], tb_[i][3]), writes=(yb[i][1],))
            S.dma("pool", (lambda e, i=i, fr=fr, tb=tb: e.dma_start(out=YY[fr:fr + 128, tb * 512:(tb + 1) * 512], in_=yy[:, i, 0, :])),
                  reads=(yb[i][0],))
            S.dma("pool", (lambda e, i=i, fr=fr, tb=tb: e.dma_start(out=YY[NF + fr:NF + fr + 128, tb * 512:(tb + 1) * 512], in_=yy[:, i, 1, :])),
                  reads=(yb[i][1],))
        return epi
    return mk


def gate_epi(C, es, order, ucT, z1T, z1tok, yT, skip_ap, name="ge"):
    nc, S = C.nc, C.S
    T = lambda n, s, d: es.enter_context(C.sbuf_tensor(name + n, s, d))
    ys = T("_ys", [128, 2, 512], F32)
    zp = T("_zp", [128, 2, 512], F32)
    gt = T("_gt", [128, 2, 512], F32)
    zo = T("_zo", [128, 2, 512], BF16)
    zt = T("_zt", [128, 2, 512], BF16)
    sk = T("_sk", [128, 8], F32)
    ysb, zpb, gtb, zob, ztb = [Buf(), Buf()], [Buf(), Buf()], [Buf(), Buf()], [Buf(), Buf()], [Buf(), Buf()]
    skb = Buf()
    S.dma("sp", lambda e: e.dma_start(out=sk[:], in_=skip_ap[order * 1024:(order + 1) * 1024].rearrange("(a p) -> p a", p=128),
                                      allow_slow_non_contiguous=True), writes=(skb,))
    zprev = ucT if order == 0 else z1T
    cnt = [0]

    def mk(t0):
        def epi(C, tb, j, bank, width):
            i = cnt[0] % 2
            cnt[0] += 1
            cc = tb * 4 + j
            c0 = cc * 128
            g0 = 1024 * (order + 1) + c0
            S.dma("sp", (lambda e, i=i, c0=c0: e.dma_start(out=zp[:, i, :], in_=zprev[c0:c0 + 128, t0:t0 + 512])), writes=(zpb[i],))
            S.dma("sp", (lambda e, i=i, g0=g0: e.dma_start(out=gt[:, i, :], in_=ucT[g0:g0 + 128, t0:t0 + 512])), writes=(gtb[i],))
            S.op("act", (lambda e, i=i, bank=bank: e.activation(out=ys[:, i, :], in_=C.ps[:, bank, :], func=AF.Copy, scale=2.0 / NDFT)),
                 reads=(C.psb[bank],), writes=(ysb[i],))
            S.op("dve", (lambda e, i=i, cc=cc: e.scalar_tensor_tensor(out=ys[:, i, :], in0=zp[:, i, :], scalar=sk[:, cc:cc + 1], in1=ys[:, i, :],
                                                                       op0=ALU.mult, op1=ALU.add)),
                 reads=(zpb[i], skb, ysb[i]), writes=(ysb[i],))
            if order == 0:
                S.op("dve", (lambda e, i=i: e.tensor_tensor(out=zp[:, i, :], in0=ys[:, i, :], in1=gt[:, i, :], op=ALU.mult)),
                     reads=(ysb[i], gtb[i]), writes=(zpb[i],))
                S.dma("pool", (lambda e, i=i, c0=c0: e.dma_start(out=z1T[c0:c0 + 128, t0:t0 + 512], in_=zp[:, i, :])), reads=(zpb[i],))

                def tr(e, i=i, bank=bank):
                    ins = None
                    for q in range(4):
                        ins = e.transpose(out=C.ps[:, bank, q * 128:(q + 1) * 128], in_=zp[:, i, q * 128:(q + 1) * 128], identity=C.ident[:])
                    return ins
                S.op("pe", tr, reads=(zpb[i], C.identb), writes=(C.psb[bank],))
                S.op("act", (lambda e, i=i, bank=bank: e.activation(out=zt[:, i, :], in_=C.ps[:, bank, :], func=AF.Copy)),
                     reads=(C.psb[bank],), writes=(ztb[i],))
                S.dma("pool", (lambda e, i=i, c0=c0: e.dma_start(
                    out=z1tok[t0:t0 + 512, c0:c0 + 128].rearrange("(q p) c -> p q c", p=128),
                    in_=zt[:, i, :].rearrange("p (q c) -> p q c", c=128))), reads=(ztb[i],))
            else:
                S.op("dve", (lambda e, i=i: e.tensor_tensor(out=zo[:, i, :], in0=ys[:, i, :], in1=gt[:, i, :], op=ALU.mult)),
                     reads=(ysb[i], gtb[i]), writes=(zob[i],))
                S.dma("pool", (lambda e, i=i, c0=c0: e.dma_start(out=yT[2048 + c0:2048 + c0 + 128, t0:t0 + 512], in_=zo[:, i, :])),
                      reads=(zob[i],))
        return epi
    return mk


def mixer_c(C, proj, yT, sc, hp, tabs):
    hy_conv_phase(C, proj, sc["ucT"], sc["vtok"], hp["conv_w"], hp["conv_b"], tabs["mprev"], tabs["mnext"])
    hy_filter_phase(C, sc["hb"], tabs["zT"], tabs["hr"], hp["w1"], hp["b1"], hp["f1"], hp["w2"], hp["b2"], hp["f2"], hp["w3"], hp["decay"])
    with contextlib.ExitStack() as es:
        mkc = storeF_epi(C, es, sc["Hc"], "hfc")
        mks = storeF_epi(C, es, sc["Hs"], "hfs")
        blocks = [dict(cols=[(tabs["Eh"], f0, 512)], orient="F", epi=mkc(f0)) for f0 in range(0, NF, 512)]
        blocks += [dict(cols=[(tabs["Eh"], NF + f0, 512)], orient="F", epi=mks(f0)) for f0 in range(0, NF, 512)]
        gemm_phase(C, TOK=2048, K=TOK, blocks=blocks, a_loader=dram_loader(C, sc["hb"], TOK), nbufA=2)
    for order in range(2):
        src = sc["vtok"] if order == 0 else sc["z1tok"]
        with contextlib.ExitStack() as es:
            mk = cmul_epi(C, es, sc["Hc"], sc["Hs"], order * 1024, sc["YY"], "cm%d" % order)
            blocks = [dict(cols=[(tabs["Eu"], f0, 256), (tabs["Eu"], NF + f0, 256)], orient="F", epi=mk(f0)) for f0 in range(0, NF, 256)]
            gemm_phase(C, TOK=1024, K=TOK, blocks=blocks, a_loader=dram_loader(C, src, TOK), nbufA=2)
        with contextlib.ExitStack() as es:
            mk = gate_epi(C, es, order, sc["ucT"], sc["z1T"], sc["z1tok"], yT, hp["skip"], "ge%d" % order)
            blocks = [dict(cols=[(tabs["Ei"], t0, 512)], orient="T", epi=mk(t0)) for t0 in range(0, TOK, 512)]
            gemm_phase(C, TOK=1024, K=2 * NF, blocks=blocks, a_loader=dram_loader(C, sc["YY"], 2 * NF), nbufA=1)


def c_consts(is_prompt):
    L = 2048 if is_prompt else 4096
    NT = TOK // 128
    nseg = TOK // L
    pos_u = np.concatenate([np.arange(L) + s * 3072 for s in range(nseg)]).astype(np.int64)
    k2 = (2 * np.arange(NF, dtype=np.int64) + 1)

    def table(pos):
        m = (pos[:, None] * k2[None, :]) % (2 * NDFT)
        ang = m.astype(np.float64) * (np.pi / NDFT)
        return np.cos(ang), np.sin(ang)
    cu, su = table(pos_u)
    Eu = np.concatenate([cu, su], 1).astype(ml_dtypes.bfloat16)
    Ei = np.ascontiguousarray(np.concatenate([cu, su], 1).T).astype(ml_dtypes.bfloat16)
    pos_h = np.arange(TOK, dtype=np.int64) - L // 2
    ch, sh = table(pos_h)
    Eh = np.concatenate([ch, sh], 1)
    Eh[L:, :] = 0.0
    Eh = Eh.astype(ml_dtypes.bfloat16)
    n = np.arange(L, dtype=np.float32)
    t = n / np.float32(L - 1)
    f = np.linspace(1e-4, 15, 16, dtype=np.float32)
    ang = (np.float32(2.0 * math.pi / L)) * n[:, None] * f[None, :]
    z = np.concatenate([t[:, None], np.cos(ang), -np.sin(ang)], -1).astype(np.float32)
    zT = np.zeros((33, TOK), np.float32)
    zT[:, :L] = z.T
    r = np.zeros(TOK, np.float32)
    r[:L] = np.abs(n - L // 2) / np.float32(L // 2)
    hr = np.ascontiguousarray(r.reshape(NT, 128).T)
    mprev = np.ones((128, NT), np.float32)
    mnext = np.ones((128, NT), np.float32)
    for s in range(nseg):
        mprev[0, s * L // 128] = 0.0
        mnext[127, (s + 1) * L // 128 - 1] = 0.0
    return {"Eu": Eu, "Ei": Ei, "Eh": Eh, "zT": zT, "hr": hr, "mprev": mprev, "mnext": mnext}


WNAMES = ("w_in", "w_out", "w_gate", "w_up", "w_down")
WSHAPES = {"w_in": (D_MODEL, N_IN), "w_out": (D_MODEL, D_MODEL), "w_gate": (D_MODEL, FF),
           "w_up": (D_MODEL, FF), "w_down": (FF, D_MODEL)}
SMALL = {"norm1_g": (2, D_MODEL), "qk_norm_g": (2, 2, 128), "norm2_g": (2, D_MODEL), "final_g": (D_MODEL,),
         "hy_conv_w": (2, 3, 3072), "hy_conv_b": (2, 3072), "hy_w1": (2, 33, 64), "hy_b1": (2, 64), "hy_freq1": (2, 64),
         "hy_w2": (2, 64, 64), "hy_b2": (2, 64), "hy_freq2": (2, 64), "hy_w3": (2, 64, 2048), "hy_decay": (2, 2048),
         "hy_skip": (2, 2048), "ml_gate_b": (2, 4, 4), "ml_norm_g": (2, 1024)}
A_NU = a_units()[1]
CONST_SPECS = {"ident": ((128, 128), "f"), "ropeB": ((TOK, 128), "f"), "segb": ((128, 4), "f"),
               "ropeA": ((TOK, 32), "f"), "abias": ((128, A_NU), "f"), "amask": ((128, 256), "f"),
               "tri": ((128, 384), "f"), "carry": ((128, TOK // 128, 8), "f"),
               "Eu": ((TOK, 2 * NF), "b"), "Ei": ((2 * NF, TOK), "b"), "Eh": ((TOK, 2 * NF), "b"),
               "zT": ((33, TOK), "f"), "hr": ((128, TOK // 128), "f"), "mprev": ((128, TOK // 128), "f"),
               "mnext": ((128, TOK // 128), "f")}
DEPTH = 2


def zero_rows(C, yT, r0, r1):
    nc, S = C.nc, C.S
    with C.sbuf_tensor("zr", [128, TOK], BF16) as z:
        zb = Buf()
        S.op("pool", lambda e: e.memset(z[:], 0.0), writes=(zb,))
        for r in range(r0, r1, 128):
            S.dma("sp", (lambda e, r=r: e.dma_start(out=yT[r:r + 128, :], in_=z[:])), reads=(zb,))
        S.run_phase()


def build_program():
    nc = bass.Bass("TRN2", target_bir_lowering=False)
    dt = lambda n, s, d, k: nc.dram_tensor(n, list(s), d, kind=k).ap()
    x_in = dt("x", (TOK, D_MODEL), F32, "ExternalInput")
    y_out = dt("y", (TOK, D_MODEL), F32, "ExternalOutput")
    W = {n: dt(n, (DEPTH,) + WSHAPES[n], F32, "ExternalInput") for n in WNAMES}
    P = {n: dt(n, s, F32, "ExternalInput") for n, s in SMALL.items()}
    CT = {n: dt(n, sh, F32 if k == "f" else BF16, "ExternalInput") for n, (sh, k) in CONST_SPECS.items()}
    ident, ropeB, segb = CT["ident"], CT["ropeB"], CT["segb"]
    Wb = {n: [dt("%s_bf%d" % (n, l), WSHAPES[n], BF16, "Internal") for l in range(DEPTH)] for n in WNAMES}
    proj = dt("proj", (TOK, N_IN), F32, "Internal")
    yT = dt("yT", (D_MODEL, TOK), BF16, "Internal")
    aT = dt("aT", (FF, TOK), BF16, "Internal")
    XA = dt("XA", (TOK, D_MODEL), F32, "Internal")
    XB = dt("XB", (TOK, D_MODEL), F32, "Internal")
    aqT = dt("aqT", (8, 128, TOK), BF16, "Internal")
    akT = dt("akT", (8, 128, TOK), BF16, "Internal")
    avb = dt("avb", (TOK, 1024), BF16, "Internal")
    dqT = dt("dqT", (8, 128, TOK), BF16, "Internal")
    dkT = dt("dkT", (8, 128, TOK), BF16, "Internal")
    dkb = dt("dkb", (TOK, 1024), BF16, "Internal")
    dvb = dt("dvb", (TOK, 1024), BF16, "Internal")
    dgt = dt("dgt", (128, TOK // 128, 32), F32, "Internal")
    sc = {"ucT": dt("ucT", (3072, TOK), F32, "Internal"), "vtok": dt("vtok", (TOK, 1024), BF16, "Internal"),
          "hb": dt("hb", (TOK, 2048), BF16, "Internal"), "Hc": dt("Hc", (NF, 2048), F32, "Internal"),
          "Hs": dt("Hs", (NF, 2048), F32, "Internal"), "YY": dt("YY", (2 * NF, 1024), BF16, "Internal"),
          "z1T": dt("z1T", (1024, TOK), F32, "Internal"), "z1tok": dt("z1tok", (TOK, 1024), BF16, "Internal")}

    C = Ctx(nc)
    C.load_consts(ident)
    cast_weights(C, [(W["w_in"][0], Wb["w_in"][0])])
    order = [(n, 0) for n in WNAMES[1:]] + [(n, 1) for n in WNAMES]
    C.bg_jobs = cast_jobs(C, [(W[n][l], Wb[n][l]) for n, l in order], q="poolc")
    for l in range(DEPTH):
        xl = x_in if l == 0 else XB
        with contextlib.ExitStack() as es:
            ld = rmsnorm_loader(C, es, xl, P["norm1_g"][l], D_MODEL, "n1_%d" % l)
            mk = store_epi(C, es, proj, None, F32, "pe%d" % l)
            blocks = []
            for c0 in range(0, N_IN, 512):
                wd = min(512, N_IN - c0)
                blocks.append(dict(cols=[(Wb["w_in"][l], c0, wd)], orient="T", epi=mk(c0)))
            gemm_phase(C, TOK=TOK, K=D_MODEL, blocks=blocks, a_loader=ld, nbufA=2)
        mixer_a_prep(C, proj, aqT, akT, avb, CT["ropeA"])
        mixer_a(C, aqT, akT, avb, yT, CT["abias"], CT["amask"])
        mixer_b(C, proj, yT, ropeB, segb, P["qk_norm_g"][l])
        hp = {"conv_w": P["hy_conv_w"][l], "conv_b": P["hy_conv_b"][l], "w1": P["hy_w1"][l], "b1": P["hy_b1"][l],
              "f1": P["hy_freq1"][l], "w2": P["hy_w2"][l], "b2": P["hy_b2"][l], "f2": P["hy_freq2"][l],
              "w3": P["hy_w3"][l], "decay": P["hy_decay"][l], "skip": P["hy_skip"][l]}
        mixer_c(C, proj, yT, sc, hp, CT)
        mixer_d_prep(C, proj, dqT, dkT, dkb, dvb, dgt, P["ml_gate_b"][l], CT["carry"], CT["tri"])
        mixer_d(C, proj, dqT, dkT, dkb, dvb, dgt, yT, P["ml_norm_g"][l], CT["tri"])
        assert len(C.bg_jobs) <= (204 if l == 0 else 0), len(C.bg_jobs)
        with contextlib.ExitStack() as es:
            mk = resid_epi(C, es, xl, XA, "ro%d" % l)
            blocks = [dict(cols=[(Wb["w_out"][l], c0, 512)], orient="T", epi=mk(c0)) for c0 in range(0, D_MODEL, 512)]
            gemm_phase(C, TOK=TOK, K=D_MODEL, blocks=blocks, a_loader=dram_loader(C, yT, D_MODEL), nbufA=2)
        with contextlib.ExitStack() as es:
            ld = rmsnorm_loader(C, es, XA, P["norm2_g"][l], D_MODEL, "n2_%d" % l)
            mk = swiglu_epi(C, es, aT, "sg%d" % l)
            blocks = [dict(cols=[(Wb["w_gate"][l], c0, 256), (Wb["w_up"][l], c0, 256)], orient="F", epi=mk(c0))
                      for c0 in range(0, FF, 256)]
            gemm_phase(C, TOK=TOK, K=D_MODEL, blocks=blocks, a_loader=ld, nbufA=2)
        with contextlib.ExitStack() as es:
            mk = resid_epi(C, es, XA, XB, "rd%d" % l)
            blocks = [dict(cols=[(Wb["w_down"][l], c0, 512)], orient="T", epi=mk(c0)) for c0 in range(0, D_MODEL, 512)]
            gemm_phase(C, TOK=TOK, K=FF, blocks=blocks, a_loader=dram_loader(C, aT, FF), nbufA=1)
    final_norm(C, XB, P["final_g"], y_out, D_MODEL)
    return nc


def _rope_tab_b(L):
    pos = np.arange(L)
    row = (pos // 64).astype(np.float32)
    col = (pos % 64).astype(np.float32)
    inv = (np.float32(10000.0) ** (-np.arange(0, 64, 2, dtype=np.float32) / np.float32(64))).astype(np.float32)
    ar = row[:, None] * inv[None]
    ac = col[:, None] * inv[None]
    return np.concatenate([np.cos(ar), np.sin(ar), np.cos(ac), np.sin(ac)], -1).astype(np.float32)


def _core_consts(is_prompt):
    L = 2048 if is_prompt else 4096
    segb = np.zeros((128, 4), np.float32)
    if is_prompt:
        segb[:, 1] = -30000.0
        segb[:, 2] = -30000.0
    out = {"ident": np.eye(128, dtype=np.float32),
           "ropeB": np.concatenate([_rope_tab_b(L)] * (TOK // L), 0),
           "segb": segb}
    out.update(a_consts(is_prompt))
    out.update(d_consts(is_prompt))
    out.update(c_consts(is_prompt))
    return out


def kernel(**inputs):
    xp = np.asarray(inputs["x_prompt"], np.float32)
    xs = np.asarray(inputs["x_sample"], np.float32)
    nc = build_program()
    shared = {n: np.ascontiguousarray(np.asarray(inputs[n], np.float32)) for n in list(WNAMES) + list(SMALL)}
    in_maps = []
    cp, cs_ = _core_consts(True), _core_consts(False)
    for c in range(8):
        m = dict(shared)
        if c < 4:
            m["x"] = np.ascontiguousarray(xp[2 * c:2 * c + 2].reshape(TOK, D_MODEL))
        else:
            m["x"] = np.ascontiguousarray(xs[c - 4].reshape(TOK, D_MODEL))
        m.update(cp if c < 4 else cs_)
        in_maps.append(m)
    res = run_bass_kernel_spmd(nc, in_maps, core_ids=list(range(8)))
    ys = [np.asarray(r["y"], np.float32) for r in res.results]
    y_prompt = np.stack([ys[c].reshape(2, 2048, D_MODEL) for c in range(4)], 0).reshape(8, 2048, D_MODEL)
    y_sample = np.stack([ys[c] for c in range(4, 8)], 0)
    return (y_prompt, y_sample)
```
